# Optimizing a Trainium2 kernel written in Bass

```python
import jax
import jax.numpy as jnp
from jax import lax
import numpy as np

D_MODEL = 2048
BATCH = 4
SEQ = 2048
DEPTH = 4
DEC_BATCH = 8
DEC_SEQ = 4
PAST_LEN = 16384
PAGE_SIZE = 128

N_MIXERS = 3
LAYER_KIND = tuple(i % N_MIXERS for i in range(DEPTH))
LAYER_SLOT = tuple(i // N_MIXERS for i in range(DEPTH))
N_A = LAYER_KIND.count(0)
N_B = LAYER_KIND.count(1)
N_C = LAYER_KIND.count(2)
NORM_EPS = 1e-6

A_GROUPS = ((128, 1), (512, 4), (2048, 16))
N_A_GROUPS = len(A_GROUPS)
A_HEADS = 16
A_HEAD_DIM = D_MODEL // A_HEADS
A_WIDTH = A_HEADS * A_HEAD_DIM
A_ROT = A_HEAD_DIM // 4
A_ROPE_THETA = 500000.0
A_IN = 3 * N_A_GROUPS * A_WIDTH + A_WIDTH

B_HEADS = 8
B_DK = D_MODEL // B_HEADS
B_DV = 2 * B_DK
B_CHUNK = 128
B_ROPE_THETA = 10000.0
B_IN = 2 * B_HEADS * B_DK + 2 * B_HEADS * B_DV

C_HEADS = 4
C_DK = D_MODEL // (2 * C_HEADS)
C_DV = D_MODEL // C_HEADS
C_RANK = 16
C_TAU = 16.0
C_CHUNK = 64
C_IN = 2 * C_HEADS * C_DK + 2 * C_HEADS * C_DV + C_RANK

kernel_name = 'hybrid_dilated_retention_gla_step'

F32 = jnp.float32


def rms_norm(x, g):
    xf = x.astype(F32)
    y = xf * lax.rsqrt(jnp.mean(xf * xf, axis=-1, keepdims=True) + NORM_EPS)
    return (y * g.astype(F32)).astype(x.dtype)


def rotary(x, pos, n_rot, theta):
    half = n_rot // 2
    inv_freq = theta ** (-jnp.arange(half, dtype=F32) / half)
    ang = pos.astype(F32)[:, None] * inv_freq[None, :]
    cos = jnp.cos(ang)[None, :, None, :]
    sin = jnp.sin(ang)[None, :, None, :]
    xf = x.astype(F32)
    x1 = xf[..., :half]
    x2 = xf[..., half:n_rot]
    out = jnp.concatenate([x1 * cos - x2 * sin, x1 * sin + x2 * cos, xf[..., n_rot:]], axis=-1)
    return out.astype(x.dtype)


def softmax_with_lse(s):
    m = jnp.max(s, axis=-1, keepdims=True)
    p = jnp.exp(s - m)
    den = jnp.sum(p, axis=-1, keepdims=True)
    return p / den, (m + jnp.log(den))[..., 0]


def dilated_window_attn_prompt(q, k, v, dilation, nk):
    B, S, H, D = q.shape
    L = S // dilation
    blk = nk
    nb = -(-L // blk)
    Lp = nb * blk

    def to_sub(t):
        t = t.reshape(B, L, dilation, H, D).transpose(0, 2, 1, 3, 4)
        return jnp.pad(t, ((0, 0), (0, 0), (0, Lp - L), (0, 0), (0, 0)))

    def band(t):
        tp = jnp.pad(t, ((0, 0), (0, 0), (blk, 0), (0, 0), (0, 0))).reshape(B, dilation, nb + 1, blk, H, D)
        return jnp.concatenate([tp[:, :, :-1], tp[:, :, 1:]], axis=3)

    qb = to_sub(q).reshape(B, dilation, nb, blk, H, D)
    kb = band(to_sub(k))
    vb = band(to_sub(v))
    i = jnp.arange(blk)[:, None]
    c = jnp.arange(2 * blk)[None, :]
    local = (c >= i) & (c <= i + blk)
    started = (jnp.arange(nb)[:, None, None] > 0) | (c[None] >= blk)
    mask = local[None] & started
    s = jnp.einsum('brnqhd,brnkhd->brnhqk', qb, kb, preferred_element_type=F32) * (D ** -0.5)
    s = jnp.where(mask[None, None, :, None], s, -jnp.inf)
    p, lse = softmax_with_lse(s)
    o = jnp.einsum('brnhqk,brnkhd->brnqhd', p, vb.astype(F32))
    o = o.reshape(B, dilation, Lp, H, D)[:, :, :L].transpose(0, 2, 1, 3, 4).reshape(B, S, H, D)
    lse = lse.transpose(0, 1, 2, 4, 3).reshape(B, dilation, Lp, H)[:, :, :L]
    lse = lse.transpose(0, 2, 1, 3).reshape(B, S, H)
    return o, lse


def dilated_window_attn_sample(q, k_all, v_all, lbuf, dilation, nk):
    T = q.shape[1]
    D = q.shape[-1]
    idx = lbuf + jnp.arange(T)[:, None] - dilation * jnp.arange(nk + 1)[None, :]
    valid = idx >= 0
    idx = jnp.maximum(idx, 0)
    kg = jnp.take(k_all, idx, axis=1)
    vg = jnp.take(v_all, idx, axis=1)
    s = jnp.einsum('bthd,btkhd->bthk', q, kg, preferred_element_type=F32) * (D ** -0.5)
    s = jnp.where(valid[None, :, None, :], s, -jnp.inf)
    p, lse = softmax_with_lse(s)
    o = jnp.einsum('bthk,btkhd->bthd', p, vg.astype(F32))
    return o, lse


def a_project(h, pos, w_in):
    B, S, _ = h.shape
    z = h @ w_in
    n_qkv = 3 * N_A_GROUPS * A_WIDTH
    qkv = z[..., :n_qkv].reshape(B, S, N_A_GROUPS, 3, A_HEADS, A_HEAD_DIM)
    gate = z[..., n_qkv:]
    qs = [rotary(qkv[:, :, g, 0], pos, A_ROT, A_ROPE_THETA) for g in range(N_A_GROUPS)]
    ks = [rotary(qkv[:, :, g, 1], pos, A_ROT, A_ROPE_THETA) for g in range(N_A_GROUPS)]
    vs = [qkv[:, :, g, 2] for g in range(N_A_GROUPS)]
    return qs, ks, vs, gate


def a_merge(outs, lses, gate, w_out):
    wts = jax.nn.softmax(jnp.stack(lses), axis=0)
    o = jnp.einsum('gbsh,gbshd->bshd', wts, jnp.stack(outs))
    B, S = gate.shape[:2]
    o = o.reshape(B, S, A_WIDTH).astype(gate.dtype) * jax.nn.silu(gate)
    return o @ w_out


def attn_layer_prompt(h, pos, w_in, w_out):
    S = h.shape[1]
    qs, ks, vs, gate = a_project(h, pos, w_in)
    outs, lses, kv_rows = [], [], []
    for g, (win, dil) in enumerate(A_GROUPS):
        o, l = dilated_window_attn_prompt(qs[g], ks[g], vs[g], dil, win // dil)
        outs.append(o)
        lses.append(l)
        keep = min(win, S)
        kv_rows.append(jnp.stack([ks[g][:, S - keep:], vs[g][:, S - keep:]], axis=2))
    return a_merge(outs, lses, gate, w_out), kv_rows


def attn_layer_sample(h, pos, caches, w_in, w_out):
    qs, ks, vs, gate = a_project(h, pos, w_in)
    outs, lses, kv_rows = [], [], []
    for g, (win, dil) in enumerate(A_GROUPS):
        cache = caches[g].astype(ks[g].dtype)
        lbuf = cache.shape[1]
        k_all = jnp.concatenate([cache[:, :, 0], ks[g]], axis=1)
        v_all = jnp.concatenate([cache[:, :, 1], vs[g]], axis=1)
        o, l = dilated_window_attn_sample(qs[g], k_all, v_all, lbuf, dil, win // dil)
        outs.append(o)
        lses.append(l)
        kv_rows.append(jnp.stack([ks[g], vs[g]], axis=2))
    return a_merge(outs, lses, gate, w_out), kv_rows


def chunk_scan(chunk_fn, state0, seqs, chunk):
    B, S = seqs[0].shape[:2]
    nc = S // chunk

    def split(t):
        return t.reshape((B, nc, chunk) + t.shape[2:]).swapaxes(0, 1)

    def step(state, inp):
        out, state = chunk_fn(*inp, state)
        return state, out

    state, out = lax.scan(step, state0, tuple(split(t) for t in seqs))
    out = out.swapaxes(0, 1).reshape((B, S) + out.shape[3:])
    return out, state


def retention_log_decay():
    return jnp.log(1.0 - 2.0 ** (-5.0 - jnp.arange(B_HEADS, dtype=F32)))


def retention_chunk(q, k, v, state):
    q, k, v = q.astype(F32), k.astype(F32), v.astype(F32)
    C = q.shape[1]
    lg = retention_log_decay()
    idx = jnp.arange(C, dtype=F32)
    diff = idx[:, None] - idx[None, :]
    decay = jnp.where(diff >= 0, jnp.exp(lg[:, None, None] * jnp.maximum(diff, 0.0)), 0.0)
    scores = jnp.einsum('bqhd,bkhd->bhqk', q, k) * decay[None]
    inner = jnp.einsum('bhqk,bkhv->bqhv', scores, v)
    cross = jnp.einsum('bqhd,bhdv->bqhv', q, state) * jnp.exp(lg[None, :] * (idx[:, None] + 1.0))[None, :, :, None]
    k_dec = k * jnp.exp(lg[None, :] * (C - 1.0 - idx[:, None]))[None, :, :, None]
    new_state = jnp.exp(lg * C)[None, :, None, None] * state + jnp.einsum('bkhd,bkhv->bhdv', k_dec, v)
    return inner + cross, new_state


def retention_layer(h, pos, state0, chunk, w_in, gn, w_out):
    B, S, _ = h.shape
    z = h @ w_in
    nk = B_HEADS * B_DK
    nv = B_HEADS * B_DV
    q = rotary(z[..., :nk].reshape(B, S, B_HEADS, B_DK), pos, B_DK, B_ROPE_THETA)
    k = rotary(z[..., nk:2 * nk].reshape(B, S, B_HEADS, B_DK), pos, B_DK, B_ROPE_THETA) * (B_DK ** -0.5)
    v = z[..., 2 * nk:2 * nk + nv].reshape(B, S, B_HEADS, B_DV)
    gate = z[..., 2 * nk + nv:]
    o, state = chunk_scan(retention_chunk, state0, (q, k, v), chunk)
    mu = jnp.mean(o, axis=-1, keepdims=True)
    var = jnp.mean(jnp.square(o - mu), axis=-1, keepdims=True)
    y = (o - mu) * lax.rsqrt(var + NORM_EPS) * gn.astype(F32)
    y = y.reshape(B, S, nv).astype(h.dtype) * jax.nn.silu(gate)
    return y @ w_out, state


def gla_chunk(q, k, v, log_a, state):
    q, k, v = q.astype(F32), k.astype(F32), v.astype(F32)
    C = q.shape[1]
    b = jnp.cumsum(log_a, axis=1)
    idx = jnp.arange(C)
    tri = (idx[:, None] >= idx[None, :])[None, :, :, None, None]
    diff = b[:, :, None] - b[:, None, :]
    w = jnp.where(tri, jnp.exp(jnp.where(tri, diff, 0.0)), 0.0)
    att = jnp.einsum('btshd,bshd->bhts', q[:, :, None] * w, k)
    inner = jnp.einsum('bhts,bshv->bthv', att, v)
    cross = jnp.einsum('bthd,bhdv->bthv', q * jnp.exp(b), state)
    b_last = b[:, -1]
    new_state = jnp.exp(b_last)[..., None] * state + jnp.einsum('bshd,bshv->bhdv', k * jnp.exp(b_last[:, None] - b), v)
    return inner + cross, new_state


def gla_layer(h, state0, chunk, w_in, w_gate2, b_gate, gn, w_out):
    B, S, _ = h.shape
    z = h @ w_in
    nk = C_HEADS * C_DK
    nv = C_HEADS * C_DV
    q = z[..., :nk].reshape(B, S, C_HEADS, C_DK) * (C_DK ** -0.5)
    k = z[..., nk:2 * nk].reshape(B, S, C_HEADS, C_DK)
    v = z[..., 2 * nk:2 * nk + nv].reshape(B, S, C_HEADS, C_DV)
    gate = z[..., 2 * nk + nv:2 * nk + 2 * nv]
    low_rank = z[..., 2 * nk + 2 * nv:]
    log_a = jax.nn.log_sigmoid((low_rank @ w_gate2 + b_gate).astype(F32)) / C_TAU
    log_a = log_a.reshape(B, S, C_HEADS, C_DK)
    o, state = chunk_scan(gla_chunk, state0, (q, k, v, log_a), chunk)
    y = o * lax.rsqrt(jnp.mean(o * o, axis=-1, keepdims=True) + NORM_EPS) * gn.astype(F32)
    y = y.reshape(B, S, nv).astype(h.dtype) * jax.nn.silu(gate)
    return y @ w_out, state


def setup_inputs(seed: int = 0) -> dict:
    key = jax.random.key(seed)
    ks = jax.random.split(key, 20)

    def nrm(k, shape, scale):
        return jax.random.normal(k, shape, F32) * scale

    inp = {}
    inp['x_prompt'] = nrm(ks[0], (BATCH, SEQ, D_MODEL), 1.0)
    inp['x_sample'] = nrm(ks[1], (DEC_BATCH, DEC_SEQ, D_MODEL), 1.0)
    inp['cache_a_kv1'] = nrm(ks[2], (N_A, DEC_BATCH, min(A_GROUPS[0][0], PAST_LEN), 2, A_HEADS, A_HEAD_DIM), 1.0)
    inp['cache_a_kv2'] = nrm(ks[3], (N_A, DEC_BATCH, min(A_GROUPS[1][0], PAST_LEN), 2, A_HEADS, A_HEAD_DIM), 1.0)
    inp['cache_a_kv3'] = nrm(ks[4], (N_A, DEC_BATCH, min(A_GROUPS[2][0], PAST_LEN), 2, A_HEADS, A_HEAD_DIM), 1.0)
    inp['state_b'] = nrm(ks[5], (N_B, DEC_BATCH, B_HEADS, B_DK, B_DV), 0.1)
    inp['state_c'] = nrm(ks[6], (N_C, DEC_BATCH, C_HEADS, C_DK, C_DV), 0.1)
    inp['norm_g'] = 1.0 + nrm(ks[7], (DEPTH, D_MODEL), 0.1)
    inp['final_g'] = 1.0 + nrm(ks[8], (D_MODEL,), 0.1)
    inp['w_in_a'] = nrm(ks[9], (N_A, D_MODEL, A_IN), D_MODEL ** -0.5)
    inp['w_out_a'] = nrm(ks[10], (N_A, A_WIDTH, D_MODEL), A_WIDTH ** -0.5)
    inp['w_in_b'] = nrm(ks[11], (N_B, D_MODEL, B_IN), D_MODEL ** -0.5)
    inp['gn_b'] = 1.0 + nrm(ks[12], (N_B, B_HEADS, B_DV), 0.1)
    inp['w_out_b'] = nrm(ks[13], (N_B, B_HEADS * B_DV, D_MODEL), (B_HEADS * B_DV) ** -0.5)
    inp['w_in_c'] = nrm(ks[14], (N_C, D_MODEL, C_IN), D_MODEL ** -0.5)
    inp['w_gate2_c'] = nrm(ks[15], (N_C, C_RANK, C_HEADS * C_DK), C_RANK ** -0.5)
    inp['b_gate_c'] = nrm(ks[16], (N_C, C_HEADS * C_DK), 0.1)
    inp['gn_c'] = 1.0 + nrm(ks[17], (N_C, C_HEADS, C_DV), 0.1)
    inp['w_out_c'] = nrm(ks[18], (N_C, C_HEADS * C_DV, D_MODEL), (C_HEADS * C_DV) ** -0.5)
    return inp


def reference(x_prompt, x_sample, cache_a_kv1, cache_a_kv2, cache_a_kv3, state_b, state_c,
              norm_g, final_g, w_in_a, w_out_a, w_in_b, gn_b, w_out_b,
              w_in_c, w_gate2_c, b_gate_c, gn_c, w_out_c):
    Bp, S, _ = x_prompt.shape
    T = x_sample.shape[1]
    pos_p = jnp.arange(S)
    pos_s = PAST_LEN + jnp.arange(T)
    caches_a = (cache_a_kv1, cache_a_kv2, cache_a_kv3)
    xp, xs = x_prompt, x_sample
    a_rows_p, a_rows_s = [], []
    b_states_p, b_states_s, c_states_p, c_states_s = [], [], [], []
    for i in range(DEPTH):
        kind, slot = LAYER_KIND[i], LAYER_SLOT[i]
        hp = rms_norm(xp, norm_g[i])
        hs = rms_norm(xs, norm_g[i])
        if kind == 0:
            dp, rows_p = attn_layer_prompt(hp, pos_p, w_in_a[slot], w_out_a[slot])
            ds, rows_s = attn_layer_sample(hs, pos_s, [c[slot] for c in caches_a], w_in_a[slot], w_out_a[slot])
            a_rows_p.append(rows_p)
            a_rows_s.append(rows_s)
        elif kind == 1:
            zero_b = jnp.zeros((Bp, B_HEADS, B_DK, B_DV), F32)
            dp, st_p = retention_layer(hp, pos_p, zero_b, min(B_CHUNK, S), w_in_b[slot], gn_b[slot], w_out_b[slot])
            ds, st_s = retention_layer(hs, pos_s, state_b[slot].astype(F32), T, w_in_b[slot], gn_b[slot], w_out_b[slot])
            b_states_p.append(st_p.astype(state_b.dtype))
            b_states_s.append(st_s.astype(state_b.dtype))
        else:
            zero_c = jnp.zeros((Bp, C_HEADS, C_DK, C_DV), F32)
            dp, st_p = gla_layer(hp, zero_c, min(C_CHUNK, S), w_in_c[slot], w_gate2_c[slot], b_gate_c[slot], gn_c[slot], w_out_c[slot])
            ds, st_s = gla_layer(hs, state_c[slot].astype(F32), T, w_in_c[slot], w_gate2_c[slot], b_gate_c[slot], gn_c[slot], w_out_c[slot])
            c_states_p.append(st_p.astype(state_c.dtype))
            c_states_s.append(st_s.astype(state_c.dtype))
        xp = xp + dp
        xs = xs + ds
    y_prompt = rms_norm(xp, final_g)
    y_sample = rms_norm(xs, final_g)
    prompt_a_kv1 = jnp.stack([r[0] for r in a_rows_p])
    prompt_a_kv2 = jnp.stack([r[1] for r in a_rows_p])
    prompt_a_kv3 = jnp.stack([r[2] for r in a_rows_p])
    sample_a_kv1 = jnp.stack([r[0] for r in a_rows_s])
    sample_a_kv2 = jnp.stack([r[1] for r in a_rows_s])
    sample_a_kv3 = jnp.stack([r[2] for r in a_rows_s])
    prompt_state_b = jnp.stack(b_states_p)
    sample_state_b = jnp.stack(b_states_s)
    prompt_state_c = jnp.stack(c_states_p)
    sample_state_c = jnp.stack(c_states_s)
    return (y_prompt, y_sample, prompt_a_kv1, prompt_a_kv2, prompt_a_kv3, prompt_state_b, prompt_state_c,
            sample_a_kv1, sample_a_kv2, sample_a_kv3, sample_state_b, sample_state_c)
```

```python
import math
import numpy as np
import ml_dtypes
import concourse.bass as bass
import concourse.mybir as mybir
from concourse.bass_utils import run_bass_kernel_spmd

F32 = mybir.dt.float32
BF16 = mybir.dt.bfloat16
ALU = mybir.AluOpType
AF = mybir.ActivationFunctionType
AX = mybir.AxisListType

ENGS = ("sync", "scalar", "vector", "gpsimd", "tensor")
EPOCH = 12000
S = 2048
DM = 2048
SS = 2052
PAST = 16384
EPS = 1e-6
DILS = (1, 4, 16)
LBUF = (128, 512, 2048)
NL_DEFAULT = 4


PSUM_KEYS = {"pj0", "pj1", "ptb0", "psc0", "psc1", "pod0", "pod1", "pg"}


class Prog:
    def __init__(self, nc):
        self.nc = nc
        self.ops = []

    def op(self, eng, fn, reads=(), writes=(), dma=None):
        self.ops.append(dict(eng=eng, fn=fn, reads=tuple(reads), writes=tuple(writes), dma=dma, bar=False))

    def dma(self, eng, out, in_, reads=(), writes=(), slot="d", **kw):
        self.op(eng, lambda e: e.dma_start(out=out, in_=in_, **kw), reads, writes, dma=slot)

    def barrier(self):
        self.ops.append(dict(bar=True))

    def emit(self):
        nc = self.nc
        import os
        allops = self.ops[:int(os.environ.get('MAXOPS', '100000000'))]
        ops = []
        last_w = {}
        readers = {}
        dma_count = {}
        last_eng = {}
        last_dma = {}
        pending_bar = {}
        for o in allops:
            if o["bar"]:
                deps = set(last_eng.values()) | set(last_dma.values())
                for e in ENGS:
                    pending_bar[e] = set(deps)
                last_w.clear()
                readers.clear()
                continue
            j = len(ops)
            ops.append(o)
            deps = set()
            for r in o["reads"]:
                if r in last_w:
                    deps.add(last_w[r])
                if r in PSUM_KEYS:
                    for rd in readers.get(r, ()):
                        if ops[rd]["eng"] != o["eng"]:
                            deps.add(rd)
            for w in o["writes"]:
                if w in last_w:
                    deps.add(last_w[w])
                for rd in readers.get(w, ()):
                    if ops[rd]["eng"] == o["eng"] and ops[rd]["dma"] is None and o["dma"] is None:
                        continue
                    deps.add(rd)
            if o["eng"] in pending_bar:
                deps |= pending_bar.pop(o["eng"])
            fin = set()
            for i in deps:
                if i == j:
                    continue
                if ops[i]["dma"] is None and o["dma"] is None and ops[i]["eng"] == "tensor" and o["eng"] == "tensor":
                    continue
                fin.add(i)
            o["deps"] = fin
            o["dma_snap"] = dict(dma_count)
            if o["dma"] is not None:
                dma_count[o["dma"]] = dma_count.get(o["dma"], 0) + 1
                o["dma_idx"] = dma_count[o["dma"]]
                last_dma[o["dma"]] = j
            else:
                last_eng[o["eng"]] = j
            for r in o["reads"]:
                readers.setdefault(r, []).append(j)
            for w in o["writes"]:
                last_w[w] = j
                readers[w] = []
        for o in ops:
            o["ms"] = False
        for o in ops:
            for i in o["deps"]:
                if ops[i]["dma"] is None:
                    ops[i]["ms"] = True
        cnt = {e: 0 for e in ENGS}
        for o in ops:
            if o["ms"]:
                cnt[o["eng"]] += 1
                o["ms_idx"] = cnt[o["eng"]]
        sems = {}
        stack = []

        def get_sem(key):
            if key not in sems:
                g = nc.semaphore("s%d" % len(sems))
                sems[key] = g.__enter__()
                stack.append(g)
            return sems[key]

        seen = {e: {} for e in ENGS}
        for o in ops:
            waits = {}
            for i in o["deps"]:
                d = ops[i]
                if d["dma"] is not None:
                    n = max(o["dma_snap"].get(d["dma"], 0), d["dma_idx"])
                    ep, v = divmod(n - 1, EPOCH // 16)
                    key = ("dma", d["dma"], ep)
                    waits[key] = max(waits.get(key, 0), (v + 1) * 16)
                else:
                    ep, v = divmod(d["ms_idx"] - 1, EPOCH)
                    key = ("eng", d["eng"], ep)
                    waits[key] = max(waits.get(key, 0), v + 1)
            wl = []
            sn = seen[o["eng"]]
            for key, v in waits.items():
                if sn.get(key, 0) >= v:
                    continue
                if key[0] == "eng" and any(k2[0] == "eng" and k2[1] == key[1] and k2[2] > key[2] for k2 in sn):
                    continue
                sn[key] = v
                wl.append((key, v))
            o["waits"] = wl
        for o in ops:
            for key, v in o["waits"]:
                get_sem(key)
            if o["ms"]:
                ep, v = divmod(o["ms_idx"] - 1, EPOCH)
                o["inc"] = (get_sem(("eng", o["eng"], ep)), 1)
            if o["dma"] is not None:
                ep, v = divmod(o["dma_idx"] - 1, EPOCH // 16)
                o["inc"] = (get_sem(("dma", o["dma"], ep)), 16)
        if os.environ.get('DUMPW'):
            a_, b_ = [int(v_) for v_ in os.environ['DUMPW'].split(':')]
            for i_ in range(a_, min(b_, len(ops))):
                o_ = ops[i_]
                print('OP', i_, o_['eng'], o_['dma'], 'deps', sorted(o_['deps'])[-6:], 'waits', o_['waits'], 'ms', o_.get('ms_idx') if o_['ms'] else None, 'dmaidx', o_.get('dma_idx'), flush=True)
        final = []
        for slot, n in dma_count.items():
            ep, v = divmod(n - 1, EPOCH // 16)
            final.append((get_sem(("dma", slot, ep)), (v + 1) * 16))

        with nc.Block() as block:
            def mk(engname):
                def body(e):
                    for o in ops:
                        if o["eng"] != engname:
                            continue
                        for key, v in o["waits"]:
                            e.wait_ge(sems[key], v)
                        ins = o["fn"](e)
                        if "inc" in o:
                            ins.then_inc(o["inc"][0], o["inc"][1])
                    if engname == "sync":
                        for s_, v in final:
                            e.wait_ge(s_, v)
                return body
            block.sync(mk("sync"))
            block.scalar(mk("scalar"))
            block.vector(mk("vector"))
            block.gpsimd(mk("gpsimd"))
            block.tensor(mk("tensor"))
        for g in reversed(stack):
            g.__exit__(None, None, None)
        self.stats = dict(n_ops=len(ops), ms=cnt, nsem=len(sems))


def host_consts():
    c = {}
    bf = ml_dtypes.bfloat16
    c["ident"] = np.eye(128, dtype=np.float32).astype(bf)
    k = np.arange(128)[:, None]
    q = np.arange(128)[None, :]
    A = (k >= q).astype(np.float32)
    B = (k <= q).astype(np.float32)
    c["maskAB"] = np.concatenate([A, B, A, B], axis=1).astype(bf)
    c["maskBf"] = B.astype(np.float32)
    c["ones_bf"] = np.ones((128, 128), np.float32).astype(bf)
    c["ones_f"] = np.ones((128, 8), np.float32)
    half = 16
    inv = (500000.0 ** (-np.arange(half, dtype=np.float32) / half)).astype(np.float32)
    cosA = np.zeros((128, 3, 17, 16), np.float32)
    sinA = np.zeros((128, 3, 17, 16), np.float32)
    for g, dil in enumerate(DILS):
        nb = 16 // dil
        for tau in range(16):
            r, n = divmod(tau, nb)
            pos = ((128 * n + np.arange(128)) * dil + r).astype(np.float32)
            ang = pos[:, None] * inv[None, :]
            cosA[:, g, tau] = np.cos(ang)
            sinA[:, g, tau] = np.sin(ang)
        pos = (PAST + np.arange(4)).astype(np.float32)
        ang = pos[:, None] * inv[None, :]
        cosA[0:4, g, 16] = np.cos(ang)
        sinA[0:4, g, 16] = np.sin(ang)
    c["cosA"] = cosA
    c["sinA"] = sinA
    invb = (10000.0 ** (-np.arange(128, dtype=np.float32) / 128)).astype(np.float32)
    cosB = np.zeros((17, 128, 128), np.float32)
    sinB = np.zeros((17, 128, 128), np.float32)
    for t in range(16):
        pos = (128 * t + np.arange(128)).astype(np.float32)
        ang = pos[:, None] * invb[None, :]
        cosB[t] = np.cos(ang)
        sinB[t] = np.sin(ang)
    pos = (PAST + np.arange(4)).astype(np.float32)
    ang = pos[:, None] * invb[None, :]
    cosB[16, 0:4] = np.cos(ang)
    sinB[16, 0:4] = np.sin(ang)
    c["cosB"] = cosB
    c["sinB"] = sinB
    lg = np.log(1.0 - 2.0 ** (-5.0 - np.arange(8, dtype=np.float64)))
    dec = np.zeros((128, 2, 8, 3), np.float64)
    i = np.arange(128, dtype=np.float64)[:, None]
    dec[:, 0, :, 0] = np.exp(lg[None, :] * (i + 1.0))
    dec[:, 0, :, 1] = np.exp(-lg[None, :] * (i + 1.0)) / 16.0
    dec[:, 0, :, 2] = np.exp(lg[None, :] * (127.0 - i)) / 16.0
    dec[:, 1, :, 0] = np.exp(lg[None, :] * (i + 1.0))
    dec[:, 1, :, 1] = np.exp(-lg[None, :] * (i + 1.0)) / 16.0
    dec[:, 1, :, 2] = np.exp(lg[None, :] * (3.0 - i)) / 16.0
    c["decB"] = dec.astype(np.float32)
    j = np.arange(128)[:, None]
    t = np.arange(4)[None, :]
    sm = np.zeros((128, 9, 4), np.float32)
    sm[:, 0, :] = (j >= t)
    for s_ in range(4):
        sm[:, 1 + s_, s_] = 1.0
    c["smask"] = sm.astype(bf)
    mn = np.zeros((4, 3, 4), np.float32)
    tp = np.arange(4)[:, None]
    mn[:, 0, :] = (tp <= t)
    mn[:, 1, :] = (tp == t)
    mn[:, 2, :] = (tp == t)
    c["mnew"] = mn.astype(bf)
    c["triU"] = (k <= q).astype(np.float32)
    return c


CONST_SHAPES = None


def build(NL=NL_DEFAULT, debug=False, NHA=16):
    nc = bass.Bass("TRN2", target_bir_lowering=False)
    hc = host_consts()

    def din(name, shape, dt=F32):
        return nc.dram_tensor(name, list(shape), dt, kind="ExternalInput").ap()

    def dout(name, shape, dt=F32):
        return nc.dram_tensor(name, list(shape), dt, kind="ExternalOutput").ap()

    xp = din("xp", [S, DM])
    xs = din("xs", [4, DM])
    ckv = [din("ckv%d" % g, [2, LBUF[g], 4096]) for g in range(3)]
    stb_in = din("stb", [8, 256, 512])
    stc_in = din("stc", [4, 256, 512])
    norm_g = din("norm_g", [4, DM])
    final_g = din("final_g", [DM])
    w_in_a = din("w_in_a", [2, DM, 20480])
    w_out_a = din("w_out_a", [2, 2048, DM])
    w_in_b = din("w_in_b", [DM, 12288])
    gn_b = din("gn_b", [8, 512])
    w_out_b = din("w_out_b", [4096, DM])
    w_in_c = din("w_in_c", [DM, 6160])
    w_gate2 = din("w_gate2", [16, 1024])
    b_gate = din("b_gate", [1, 1024])
    gn_c = din("gn_c", [4, 512])
    w_out_c = din("w_out_c", [2048, DM])
    cd = {}
    for kk, v in hc.items():
        cd[kk] = din("c_" + kk, v.shape, BF16 if v.dtype == ml_dtypes.bfloat16 else F32)

    y_p = dout("y_p", [S, DM])
    y_s = dout("y_s", [4, DM])
    p_kv = [dout("p_kv%d" % g, [2, LBUF[g] if LBUF[g] < S else S, 4096]) for g in range(3)]
    p_stb = dout("p_stb", [8, 256, 512])
    p_stc = dout("p_stc", [4, 256, 512])
    s_kv = dout("s_kv", [2, 3, 4, 4096])
    s_stb = dout("s_stb", [8, 256, 512])
    s_stc = dout("s_stc", [4, 256, 512])
    xres = nc.dram_tensor("xres", [SS, DM], F32, kind="Internal").ap()
    dbg_x = dout("dbg_x", [SS, DM]) if debug else None
    dbg_acc = dout("dbg_acc", [128, 2, SS]) if debug else None
    dbg_vg = dout("dbg_vg", [128, 17, 128], BF16) if debug else None
    dbg_qkT = dout("dbg_qkT", [128, 2, 17, 128], BF16) if debug else None
    dbg_yT = dout("dbg_yT", [128, 2, SS], BF16) if debug else None
    hTd = nc.dram_tensor("hTd", [17, 128, 16, 128], BF16, kind="Internal").ap()

    P = Prog(nc)
    ctxs = []

    def sb(name, shape, dt):
        g = nc.sbuf_tensor(name, list(shape), dt)
        t = g.__enter__()
        ctxs.append(g)
        return t

    def ps(name, shape, dt):
        g = nc.psum_tensor(name, list(shape), dt)
        t = g.__enter__()
        ctxs.append(g)
        return t

    pj = [ps("pj0", [128, 512], F32), ps("pj1", [128, 512], F32)]
    ptb = ps("ptb", [128, 512], BF16)
    psc = [ps("psc0", [128, 512], F32), ps("psc1", [128, 512], F32)]
    pod = [ps("pod0", [128, 512], F32), ps("pod1", [128, 512], F32)]
    pg = ps("pg", [128, 512], F32)

    ident = sb("ident", [128, 128], BF16)
    maskAB = sb("maskAB", [128, 512], BF16)
    maskBf = sb("maskBf", [128, 128], F32)
    ones_bf = sb("ones_bf", [128, 128], BF16)
    ones_f = sb("ones_f", [128, 8], F32)
    cosA = sb("cosA", [128, 3, 17, 16], F32)
    sinA = sb("sinA", [128, 3, 17, 16], F32)
    decB = sb("decB", [128, 2, 8, 3], F32)
    smask = sb("smask", [128, 9, 4], BF16)
    mnew = sb("mnew", [4, 3, 4], BF16)
    triU = sb("triU", [128, 128], F32)
    gvec = sb("gvec", [128, DM], F32)
    for nm, t in (("ident", ident), ("maskAB", maskAB), ("maskBf", maskBf), ("ones_bf", ones_bf), ("ones_f", ones_f),
                  ("cosA", cosA), ("sinA", sinA), ("decB", decB), ("smask", smask), ("mnew", mnew), ("triU", triU)):
        P.dma("sync", t[:], cd[nm], writes=[nm], slot="const")

    AR_BYTES = 156 * 1024
    arena = sb("arena", [128, AR_BYTES // 4], F32)
    ar_off = [0]

    def ar_reset():
        ar_off[0] = 0

    def av(shape, dt):
        n = int(np.prod(shape[1:])) * (2 if dt == BF16 else 4)
        n4 = (n + 3) // 4
        o = ar_off[0]
        assert o + n4 <= AR_BYTES // 4, ("arena overflow", o, n4)
        ar_off[0] = o + ((n4 + 7) // 8) * 8
        v = arena[0:shape[0], o:o + n4]
        if dt == BF16:
            v = v.bitcast(BF16)
        if len(shape) == 3:
            v = v.rearrange("p (a b) -> p a b", b=shape[2])
        elif len(shape) == 4:
            v = v.rearrange("p (a b c) -> p a b c", b=shape[2], c=shape[3])
        return v

    xt = sb("xt", [128, DM], F32)
    hn = sb("hn", [128, DM], BF16)
    st4 = sb("st4", [128, 8], F32)
    qkf = [sb("qkf0", [128, 512], F32), sb("qkf1", [128, 512], F32)]
    vf = [sb("vf0", [128, 128], F32), sb("vf1", [128, 128], F32)]
    qkb = [sb("qkb0", [128, 1024], BF16), sb("qkb1", [128, 1024], BF16)]
    rt = sb("rt", [128, 6, 256], F32)
    PT = [sb("PT0", [128, 512], BF16), sb("PT1", [128, 512], BF16)]
    sg = sb("sg", [128, 512], F32)
    rc = sb("rc", [128, 512], F32)

    HT_ALL = ["hT%d" % t for t in range(17)]
    scale_a = 1.0 / math.sqrt(128.0)
    cnt = {"pj": 0, "att": 0, "ptb": 0, "t": 0}

    def rms_to(nt, gsrc_key, out_ap, out_key, extra_reads=()):
        P.op("scalar", lambda e: e.activation(out=hn[0:nt, :], in_=xt[0:nt, :], func=AF.Square, accum_out=st4[0:nt, 0:1]),
             reads=["xt"], writes=["hn", "st4"])
        P.op("scalar", lambda e: e.activation(out=st4[0:nt, 1:2], in_=st4[0:nt, 0:1], func=AF.Sqrt, scale=1.0 / DM, bias=EPS),
             reads=["st4"], writes=["st4"])
        P.op("vector", lambda e: e.reciprocal(out=st4[0:nt, 2:3], in_=st4[0:nt, 1:2]), reads=["st4"], writes=["st4"])
        P.op("vector", lambda e: e.scalar_tensor_tensor(out=out_ap, in0=xt[0:nt, :], scalar=st4[0:nt, 2:3], in1=gvec[0:nt, :],
                                                       op0=ALU.mult, op1=ALU.mult),
             reads=["xt", "st4", gsrc_key] + list(extra_reads), writes=[out_key])

    def load_gvec(src_row_ap):
        P.dma("sync", gvec[:], src_row_ap.partition_broadcast(128), writes=["gvec"], slot="gvec")

    def produce_h(t, nt, hT, to_dram):
        for q4 in range(4):
            sl = 0
            cnt["ptb"] += 1
            for j in range(4):
                kc = q4 * 4 + j
                P.op("tensor", lambda e, kc=kc, j=j, sl=sl: e.transpose(out=ptb[:, sl * 512 + j * 128: sl * 512 + j * 128 + max(nt, 32)],
                                                                         in_=hn[0:max(nt, 32), kc * 128:(kc + 1) * 128], identity=ident[0:max(nt, 32), 0:max(nt, 32)]),
                     reads=["hn", "ident"], writes=["ptb%d" % sl])
            src = ptb[:, sl * 512:(sl + 1) * 512].rearrange("p (a t) -> p a t", t=128)[:, :, 0:nt]
            if not to_dram:
                dst = hT[:, q4 * 4:(q4 + 1) * 4, t * 128:t * 128 + nt]
                P.op("scalar", lambda e, dst=dst, src=src: e.copy(out=dst, in_=src), reads=["ptb%d" % sl], writes=["hT%d" % t])
            else:
                dst = hT[:, q4 * 4:(q4 + 1) * 4, 0:nt]
                P.op("scalar", lambda e, dst=dst, src=src: e.copy(out=dst, in_=src), reads=["ptb%d" % sl], writes=["hTst"])
        if to_dram:
            P.dma("sync", hTd[t, :, :, 0:nt], hT[:, :, 0:nt], reads=["hTst"], writes=["hTd%d" % t], slot="hTd")

    def wout_pass(li, yT, wo, nk, first, last, next_kind, hT_next):
        for t in range(17):
            nt = 128 if t < 16 else 4
            r0 = t * 128
            if li == 0 and first:
                src = xp[r0:r0 + nt, :] if t < 16 else xs[:, :]
                P.dma("sync", xt[0:nt, :], src, writes=["xt"], slot="xt")
            else:
                P.dma("sync", xt[0:nt, :], xres[r0:r0 + nt, :], reads=["xres%d" % t], writes=["xt"], slot="xt")
            banks = [pj[0], pj[1], psc[0], psc[1]]
            bkeys = ["pj0", "pj1", "psc0", "psc1"]
            for cb in range(4):
                for k in range(nk):
                    P.op("tensor", lambda e, cb=cb, k=k, nt=nt, r0=r0, banks=banks: e.matmul(banks[cb][0:nt, :], lhsT=yT[:, k, r0:r0 + nt],
                                                                   rhs=wo[:, k, cb * 512:(cb + 1) * 512], start=(k == 0), stop=(k == nk - 1)),
                         reads=["yT", "wo"], writes=[bkeys[cb]])
            for cb in range(4):
                P.op("vector", lambda e, cb=cb, nt=nt, banks=banks: e.tensor_tensor(out=xt[0:nt, cb * 512:(cb + 1) * 512], in0=xt[0:nt, cb * 512:(cb + 1) * 512],
                                                                 in1=banks[cb][0:nt, :], op=ALU.add),
                     reads=["xt", bkeys[cb]], writes=["xt"])
            is_final = last and (li == NL - 1)
            if not is_final:
                P.dma("sync", xres[r0:r0 + nt, :], xt[0:nt, :], reads=["xt"], writes=["xres%d" % t], slot="xres")
            if last:
                if is_final:
                    if debug:
                        P.dma("sync", dbg_x[r0:r0 + nt, :], xt[0:nt, :], reads=["xt"], slot="dbg")
                    rms_to(nt, "gvec", xt[0:nt, :], "xt")
                    dst = y_p[r0:r0 + nt, :] if t < 16 else y_s[:, :]
                    P.dma("sync", dst, xt[0:nt, :], reads=["xt"], slot="yout")
                else:
                    rms_to(nt, "gvec", hn[0:nt, :], "hn", extra_reads=["hn"])
                    produce_h(t, nt, hT_next, to_dram=(next_kind != 0))

    def layer_A(li, slot):
        ar_reset()
        hT = av([128, 16, SS], BF16)
        wb = [av([128, 16, 384], BF16) for _ in range(3)]
        wo = av([128, 2, DM], BF16)
        yT = av([128, 2, SS], BF16)
        qkT = av([128, 2, 17, 128], BF16)
        vg = av([128, 17, 128], BF16)
        acc = av([128, 2, SS], F32)
        ck = av([128, 4, 2, 128], BF16)
        ckT = av([128, 128], BF16)
        PTs = av([128, 32], BF16)
        hst = av([128, 16, 128], BF16)
        return dict(hT=hT, wb=wb, wo=wo, yT=yT, qkT=qkT, vg=vg, acc=acc, ck=ck, ckT=ckT, PTs=PTs, hst=hst)

    def run_layer_A(li, slot, V, next_kind, hT_next):
        hT, wbs, wo, yT, qkT, vg, acc, ck, ckT, PTs = (V[k] for k in ("hT", "wb", "wo", "yT", "qkT", "vg", "acc", "ck", "ckT", "PTs"))
        wsrc = w_in_a[slot].rearrange("(kc p) (s c) -> p kc s c", p=128, c=2048)
        wi = 0
        for h in range(NHA):
            hs = h % 2
            for g in range(3):
                dil = DILS[g]
                nb = 16 // dil
                bi = wi % 3
                wi += 1
                wb = wbs[bi]
                wkeys = ["wb%d_%d" % (bi, kq) for kq in range(3)]
                for kq in range(3):
                    P.dma("gpsimd", wb[:, :, kq * 128:(kq + 1) * 128],
                          wsrc[:, :, 3 * g + kq, h * 128:(h + 1) * 128], writes=[wkeys[kq]], slot=wkeys[kq])
                for tau in range(17):
                    nt = 128 if tau < 16 else 4
                    if tau < 16:
                        r, n = divmod(tau, nb)
                        st0 = n * 128 * dil + r
                        tsl = slice(st0, st0 + 127 * dil + 1, dil)
                    else:
                        tsl = slice(S, S + 4)
                    pi = cnt["pj"] % 2
                    cnt["pj"] += 1
                    pjt = pj[pi]
                    for kc in range(16):
                        P.op("tensor", lambda e, kc=kc, tsl=tsl, pjt=pjt, wb=wb, nt=nt: e.matmul(pjt[0:nt, 0:384], lhsT=hT[:, kc, tsl], rhs=wb[:, kc, :],
                                                                                          start=(kc == 0), stop=(kc == 15)),
                             reads=HT_ALL + wkeys, writes=["pj%d" % pi])
                    ti = cnt["t"] % 2
                    cnt["t"] += 1
                    qf = qkf[ti]
                    P.op("scalar", lambda e, qf=qf, pjt=pjt, nt=nt: e.copy(out=qf[0:nt, 0:256], in_=pjt[0:nt, 0:256]),
                         reads=["pj%d" % pi], writes=["qkf%d" % ti])
                    P.op("vector", lambda e, pjt=pjt, nt=nt, tau=tau: e.tensor_copy(out=vg[0:nt, tau, :], in_=pjt[0:nt, 256:384]),
                         reads=["pj%d" % pi], writes=["vg%d" % tau])
                    if tau == 16:
                        need = True
                    elif g == 0:
                        need = (tau == 15)
                    elif g == 1:
                        need = (tau % nb == nb - 1)
                    else:
                        need = True
                    if need:
                        P.op("scalar", lambda e, pjt=pjt, nt=nt, ti=ti: e.copy(out=vf[ti][0:nt, :], in_=pjt[0:nt, 256:384]),
                             reads=["pj%d" % pi], writes=["vf%d" % ti])
                    q3 = qf[0:nt, 0:256].rearrange("p (a d) -> p a d", d=128)
                    x1 = q3[:, :, 0:16]
                    x2 = q3[:, :, 16:32]
                    cs = cosA[0:nt, g, tau:tau + 1, :].broadcast_to([nt, 2, 16])
                    sn = sinA[0:nt, g, tau:tau + 1, :].broadcast_to([nt, 2, 16])
                    tv = [rt[0:nt, i, 0:32].rearrange("p (a d) -> p a d", d=16) for i in range(4)]
                    kq_ = "qkf%d" % ti
                    P.op("gpsimd", lambda e, tv=tv, x1=x1, cs=cs: e.tensor_tensor(out=tv[0], in0=x1, in1=cs, op=ALU.mult), reads=[kq_, "cosA"], writes=["rt0"])
                    P.op("gpsimd", lambda e, tv=tv, x2=x2, sn=sn: e.tensor_tensor(out=tv[1], in0=x2, in1=sn, op=ALU.mult), reads=[kq_, "sinA"], writes=["rt1"])
                    P.op("gpsimd", lambda e, tv=tv, x1=x1, sn=sn: e.tensor_tensor(out=tv[2], in0=x1, in1=sn, op=ALU.mult), reads=[kq_, "sinA"], writes=["rt2"])
                    P.op("gpsimd", lambda e, tv=tv, x2=x2, cs=cs: e.tensor_tensor(out=tv[3], in0=x2, in1=cs, op=ALU.mult), reads=[kq_, "cosA"], writes=["rt3"])
                    P.op("gpsimd", lambda e, tv=tv, x1=x1: e.tensor_tensor(out=x1, in0=tv[0], in1=tv[1], op=ALU.subtract), reads=["rt0", "rt1", kq_], writes=[kq_])
                    P.op("gpsimd", lambda e, tv=tv, x2=x2: e.tensor_tensor(out=x2, in0=tv[2], in1=tv[3], op=ALU.add), reads=["rt2", "rt3", kq_], writes=[kq_])
                    qb = qkb[ti]
                    P.op("vector", lambda e, qb=qb, qf=qf, nt=nt: e.tensor_copy(out=qb[0:nt, 0:256], in_=qf[0:nt, 0:256]),
                         reads=[kq_], writes=["qkb%d" % ti])
                    sl = 0
                    cnt["ptb"] += 1
                    for a in range(2):
                        P.op("tensor", lambda e, a=a, sl=sl, qb=qb, nt=nt: e.transpose(out=ptb[:, sl * 512 + a * 128: sl * 512 + a * 128 + max(nt, 32)],
                                                                                    in_=qb[0:max(nt, 32), a * 128:(a + 1) * 128], identity=ident[0:max(nt, 32), 0:max(nt, 32)]),
                             reads=["qkb%d" % ti, "ident"], writes=["ptb%d" % sl])
                    srcT = ptb[:, sl * 512:sl * 512 + 256].rearrange("p (a t) -> p a t", t=128)[:, :, 0:nt]
                    P.op("scalar", lambda e, srcT=srcT, tau=tau, nt=nt: e.copy(out=qkT[:, :, tau, 0:nt], in_=srcT),
                         reads=["ptb%d" % sl], writes=["qkT%d" % tau])
                    if need:
                        if tau == 16:
                            dk = s_kv[slot, g, :, h * 128:(h + 1) * 128]
                            dv = s_kv[slot, g, :, 2048 + h * 128:2048 + (h + 1) * 128]
                        else:
                            r, n = divmod(tau, nb)
                            keep = min(LBUF[g], S)
                            base = (n * 128 * dil + r) - (S - keep)
                            rows = slice(base, base + 127 * dil + 1, dil)
                            dk = p_kv[g][slot, rows, h * 128:(h + 1) * 128]
                            dv = p_kv[g][slot, rows, 2048 + h * 128:2048 + (h + 1) * 128]
                        P.dma("sync", dk, qf[0:nt, 128:256], reads=[kq_], slot="okv%d" % ti)
                        P.dma("sync", dv, vf[ti][0:nt, :], reads=["vf%d" % ti], slot="okv%d" % ti)
                for pair in range(8):
                    ai = cnt["att"] % 2
                    cnt["att"] += 1
                    pst, pot, ptt = psc[ai], pod[ai], PT[ai]
                    ks_, ko_, kp_ = "psc%d" % ai, "pod%d" % ai, "PT%d" % ai
                    for b in range(2):
                        tau = 2 * pair + b
                        r, n = divmod(tau, nb)
                        if n > 0:
                            P.op("tensor", lambda e, tau=tau, b=b, pst=pst: e.matmul(pst[:, (2 * b) * 128:(2 * b + 1) * 128], lhsT=qkT[:, 1, tau - 1, :],
                                                                                   rhs=qkT[:, 0, tau, :], start=True, stop=True),
                                 reads=["qkT%d" % (tau - 1), "qkT%d" % tau], writes=[ks_])
                        P.op("tensor", lambda e, tau=tau, b=b, pst=pst: e.matmul(pst[:, (2 * b + 1) * 128:(2 * b + 2) * 128], lhsT=qkT[:, 1, tau, :],
                                                                               rhs=qkT[:, 0, tau, :], start=True, stop=True),
                             reads=["qkT%d" % tau], writes=[ks_])
                    P.op("scalar", lambda e, pst=pst, ptt=ptt: e.activation(out=ptt[:, :], in_=pst[:, :], func=AF.Exp, scale=scale_a),
                         reads=[ks_], writes=[kp_])
                    P.op("vector", lambda e, ptt=ptt: e.tensor_tensor(out=ptt[:, :], in0=ptt[:, :], in1=maskAB[:, :], op=ALU.mult),
                         reads=[kp_, "maskAB"], writes=[kp_])
                    P.op("vector", lambda e, pot=pot: e.memset(pot[:, :], 0.0), writes=[ko_])
                    for b in range(2):
                        tau = 2 * pair + b
                        r, n = divmod(tau, nb)
                        oreg = pot[:, b * 256:b * 256 + 128]
                        dreg = pot[:, b * 256 + 128:b * 256 + 256]
                        parts = ([(tau - 1, 2 * b)] if n > 0 else []) + [(tau, 2 * b + 1)]
                        for (kt, pslot) in parts:
                            P.op("tensor", lambda e, kt=kt, pslot=pslot, oreg=oreg, ptt=ptt: e.matmul(oreg, lhsT=vg[:, kt, :], rhs=ptt[:, pslot * 128:(pslot + 1) * 128],
                                                                                                    start=False, stop=False, skip_group_check=True),
                                 reads=["vg%d" % kt, kp_, ko_], writes=[ko_])
                            P.op("tensor", lambda e, pslot=pslot, dreg=dreg, ptt=ptt: e.matmul(dreg, lhsT=ones_bf[:, :], rhs=ptt[:, pslot * 128:(pslot + 1) * 128],
                                                                                             start=False, stop=False, skip_group_check=True),
                                 reads=["ones_bf", kp_, ko_], writes=[ko_])
                        st0 = n * 128 * dil + r
                        dst = acc[:, :, st0:st0 + 127 * dil + 1:dil]
                        src = pot[:, b * 256:(b + 1) * 256].rearrange("p (a t) -> p a t", t=128)
                        if g == 0:
                            P.op("scalar", lambda e, dst=dst, src=src: e.copy(out=dst, in_=src), reads=[ko_], writes=["acc"])
                        else:
                            P.op("vector", lambda e, dst=dst, src=src: e.tensor_tensor(out=dst, in0=dst, in1=src, op=ALU.add), reads=[ko_, "acc"], writes=["acc"])
                nset = 1 if g == 0 else 4
                for kv in range(2):
                    if g == 0:
                        srcc = ckv[0][slot, :, kv * 2048 + h * 128: kv * 2048 + (h + 1) * 128]
                        P.dma("gpsimd", ck[:, 0, kv, :], srcc, writes=["ck%d" % kv], slot="ck%d" % kv)
                    else:
                        srcc = ckv[g][slot, :, :].rearrange("(i t) c -> i t c", t=dil)[:, 0:4, kv * 2048 + h * 128: kv * 2048 + (h + 1) * 128]
                        P.dma("gpsimd", ck[:, :, kv, :], srcc, writes=["ck%d" % kv], slot="ck%d" % kv)
                ai = cnt["att"] % 2
                cnt["att"] += 1
                pst, pot = psc[ai], pod[ai]
                ks_, ko_ = "psc%d" % ai, "pod%d" % ai
                P.op("vector", lambda e, pot=pot: e.memset(pot[:, 0:8], 0.0), writes=[ko_])
                for s_ in range(nset):
                    sl = 0
                    cnt["ptb"] += 1
                    P.op("tensor", lambda e, s_=s_, sl=sl: e.transpose(out=ptb[:, sl * 512:sl * 512 + 128], in_=ck[:, s_, 0, :], identity=ident[:, :]),
                         reads=["ck0", "ident"], writes=["ptb%d" % sl])
                    P.op("scalar", lambda e, sl=sl: e.copy(out=ckT[:, :], in_=ptb[:, sl * 512:sl * 512 + 128]), reads=["ptb%d" % sl], writes=["ckT"])
                    P.op("tensor", lambda e, s_=s_, pst=pst: e.matmul(pst[:, s_ * 4:s_ * 4 + 4], lhsT=ckT[:, :], rhs=qkT[:, 0, 16, 0:4], start=True, stop=True),
                         reads=["ckT", "qkT16"], writes=[ks_])
                    P.op("scalar", lambda e, s_=s_, pst=pst: e.activation(out=PTs[:, s_ * 4:s_ * 4 + 4], in_=pst[:, s_ * 4:s_ * 4 + 4], func=AF.Exp, scale=scale_a),
                         reads=[ks_], writes=["PTs"])
                    mi = 0 if g == 0 else 1 + s_
                    P.op("vector", lambda e, s_=s_, mi=mi: e.tensor_tensor(out=PTs[:, s_ * 4:s_ * 4 + 4], in0=PTs[:, s_ * 4:s_ * 4 + 4], in1=smask[:, mi, :], op=ALU.mult),
                         reads=["PTs", "smask"], writes=["PTs"])
                    P.op("tensor", lambda e, s_=s_, pot=pot: e.matmul(pot[:, 0:4], lhsT=ck[:, s_, 1, :], rhs=PTs[:, s_ * 4:s_ * 4 + 4], start=False, stop=False, skip_group_check=True),
                         reads=["ck1", "PTs", ko_], writes=[ko_])
                    P.op("tensor", lambda e, s_=s_, pot=pot: e.matmul(pot[:, 4:8], lhsT=ones_bf[:, :], rhs=PTs[:, s_ * 4:s_ * 4 + 4], start=False, stop=False, skip_group_check=True),
                         reads=["ones_bf", "PTs", ko_], writes=[ko_])
                P.op("tensor", lambda e, pst=pst: e.matmul(pst[0:4, 16:20], lhsT=qkT[:, 1, 16, 0:4], rhs=qkT[:, 0, 16, 0:4], start=True, stop=True),
                     reads=["qkT16"], writes=[ks_])
                P.op("scalar", lambda e, pst=pst: e.activation(out=PTs[0:4, 16:20], in_=pst[0:4, 16:20], func=AF.Exp, scale=scale_a), reads=[ks_], writes=["PTs2"])
                P.op("vector", lambda e, g=g: e.tensor_tensor(out=PTs[0:4, 16:20], in0=PTs[0:4, 16:20], in1=mnew[0:4, g, :], op=ALU.mult),
                     reads=["PTs2", "mnew"], writes=["PTs2"])
                P.op("tensor", lambda e, pot=pot: e.matmul(pot[:, 0:4], lhsT=vg[0:4, 16, :], rhs=PTs[0:4, 16:20], start=False, stop=False, skip_group_check=True),
                     reads=["vg16", "PTs2", ko_], writes=[ko_])
                P.op("tensor", lambda e, pot=pot: e.matmul(pot[:, 4:8], lhsT=ones_bf[0:4, :], rhs=PTs[0:4, 16:20], start=False, stop=False, skip_group_check=True),
                     reads=["ones_bf", "PTs2", ko_], writes=[ko_])
                dst = acc[:, :, S:S + 4]
                src = pot[:, 0:8].rearrange("p (a t) -> p a t", t=4)
                if g == 0:
                    P.op("scalar", lambda e, dst=dst, src=src: e.copy(out=dst, in_=src), reads=[ko_], writes=["acc"])
                else:
                    P.op("vector", lambda e, dst=dst, src=src: e.tensor_tensor(out=dst, in0=dst, in1=src, op=ALU.add), reads=[ko_, "acc"], writes=["acc"])
            if debug and h == 0:
                P.dma("sync", dbg_acc, acc[:, :, :], reads=["acc"], slot="dbg")
                P.dma("sync", dbg_vg, vg[:, :, :], reads=["vg%d" % i for i in range(17)], slot="dbg")
                P.dma("sync", dbg_qkT.rearrange("p a t d -> p (a t) d"), qkT[:, :, :, :].rearrange("p a t d -> p (a t) d"), reads=["qkT%d" % i for i in range(17)], slot="dbg")
            bi = wi % 3
            wi += 1
            wgt = wbs[bi]
            gk = "wb%d_0" % bi
            P.dma("gpsimd", wgt[:, :, 0:128], wsrc[:, :, 9, h * 128:(h + 1) * 128], writes=[gk], slot=gk)
            for c in range(5):
                c0 = c * 512
                n_ = 512 if c < 4 else 4
                for kc in range(16):
                    P.op("tensor", lambda e, kc=kc, c0=c0, n_=n_, wgt=wgt: e.matmul(pg[:, 0:n_], lhsT=wgt[:, kc, 0:128], rhs=hT[:, kc, c0:c0 + n_],
                                                                                  start=(kc == 0), stop=(kc == 15)),
                         reads=HT_ALL + [gk], writes=["pg"])
                P.op("scalar", lambda e, n_=n_: e.activation(out=sg[:, 0:n_], in_=pg[:, 0:n_], func=AF.Silu), reads=["pg"], writes=["sg"])
                P.op("vector", lambda e, c0=c0, n_=n_: e.reciprocal(out=rc[:, 0:n_], in_=acc[:, 1, c0:c0 + n_]), reads=["acc"], writes=["rc"])
                P.op("vector", lambda e, c0=c0, n_=n_: e.tensor_tensor(out=rc[:, 0:n_], in0=rc[:, 0:n_], in1=acc[:, 0, c0:c0 + n_], op=ALU.mult),
                     reads=["rc", "acc"], writes=["rc"])
                P.op("vector", lambda e, c0=c0, n_=n_, hs=hs: e.tensor_tensor(out=yT[:, hs, c0:c0 + n_], in0=rc[:, 0:n_], in1=sg[:, 0:n_], op=ALU.mult),
                     reads=["rc", "sg"], writes=["yT"])
            if debug and h == 1:
                P.dma("sync", dbg_yT, yT[:, :, :], reads=["yT"], slot="dbg")
            if hs == 1:
                P.dma("gpsimd", wo[:, :, :], w_out_a[slot, (h - 1) * 128:(h + 1) * 128, :].rearrange("(k p) n -> p k n", p=128), writes=["wo"], slot="wo")
                if h == NHA - 1:
                    load_gvec(norm_g[li + 1, :] if li < NL - 1 else final_g)
                wout_pass(li, yT, wo, 2, first=(h == 1), last=(h == NHA - 1), next_kind=next_kind, hT_next=hT_next)

    def views_BC():
        ar_reset()
        V = {}
        V["hTt"] = [av([128, 16, 128], BF16) for _ in range(2)]
        V["hTst"] = av([128, 16, 128], BF16)
        V["wb"] = [av([128, 16, 512], BF16) for _ in range(3)]
        V["wo"] = av([128, 4, DM], BF16)
        V["yT"] = av([128, 4, SS], BF16)
        V["stf"] = [av([128, 2, 512], F32) for _ in range(2)]
        V["stb"] = [av([128, 2, 512], BF16) for _ in range(2)]
        V["cs"] = [av([128, 2, 128], F32) for _ in range(2)]
        V["qkT"] = av([128, 4, 128], BF16)
        V["vb"] = av([128, 512], BF16)
        V["scT"] = av([128, 128], BF16)
        V["of"] = av([128, 512], F32)
        V["gf"] = av([128, 512], F32)
        V["gnv"] = av([128, 512], F32)
        V["yb"] = av([128, 512], BF16)
        V["bst"] = av([128, 4, 6], F32)
        V["mv"] = av([128, 8], F32)
        V["wlr"] = av([128, 16, 16], BF16)
        V["lrT"] = av([32, 128], F32)
        V["wg2"] = av([17, 1024], F32)
        V["sp"] = av([128, 256], F32)
        V["eb"] = av([128, 2, 256], F32)
        V["ebl"] = av([128, 2], F32)
        return V

    def run_layer_BC(li, kind, V, next_kind, hT_next):
        NH = 8 if kind == 1 else 4
        w_in = w_in_b if kind == 1 else w_in_c
        w_out = w_out_b if kind == 1 else w_out_c
        gn = gn_b if kind == 1 else gn_c
        wsrc = w_in.rearrange("(kc p) n -> p kc n", p=128)
        hTt, wbs, wo, yT = V["hTt"], V["wb"], V["wo"], V["yT"]
        stf, stbf, cs, qkT, vb, scT = V["stf"], V["stb"], V["cs"], V["qkT"], V["vb"], V["scT"]
        of, gf, gnv, yb, bst, mv = V["of"], V["gf"], V["gnv"], V["yb"], V["bst"], V["mv"]
        wlr, lrT, wg2, sp, eb, ebl = V["wlr"], V["lrT"], V["wg2"], V["sp"], V["eb"], V["ebl"]
        lgam = [math.log(1.0 - 2.0 ** (-5.0 - hh)) for hh in range(8)]
        if kind == 2:
            P.dma("gpsimd", wlr[:, :, :], wsrc[:, :, 6144:6160], writes=["wlr"], slot="wlr")
            P.dma("sync", wg2[0:16, :], w_gate2, writes=["wg2a"], slot="wg2")
            P.dma("sync", wg2[16:17, :], b_gate, writes=["wg2b"], slot="wg2")
        hcnt = 0
        for h in range(NH):
            if kind == 1:
                cols = [(h * 256, 256, 0), (2048 + h * 256, 256, 256)]
                vcol = 4096 + h * 512
                gcol = 8192 + h * 512
            else:
                cols = [(h * 256, 256, 0), (1024 + h * 256, 256, 256)]
                vcol = 2048 + h * 512
                gcol = 4096 + h * 512
            for (c0, n_, d0) in cols:
                for kq in range(2):
                    P.dma("gpsimd", wbs[0][:, kq * 8:(kq + 1) * 8, d0:d0 + n_], wsrc[:, kq * 8:(kq + 1) * 8, c0:c0 + n_], writes=["wq%d_%d" % (d0, kq)], slot="wq%d_%d" % (d0, kq))
            for kq in range(2):
                P.dma("gpsimd", wbs[1][:, kq * 8:(kq + 1) * 8, :], wsrc[:, kq * 8:(kq + 1) * 8, vcol:vcol + 512], writes=["wv_%d" % kq], slot="wv_%d" % kq)
                P.dma("gpsimd", wbs[2][:, kq * 8:(kq + 1) * 8, :], wsrc[:, kq * 8:(kq + 1) * 8, gcol:gcol + 512], writes=["wg_%d" % kq], slot="wg_%d" % kq)
            WQK = ["wq0_0", "wq0_1", "wq256_0", "wq256_1"]
            P.dma("sync", gnv[:, :], gn[h, :].partition_broadcast(128), writes=["gnv"], slot="gnv")
            P.op("gpsimd", lambda e: e.memset(stf[0][:, :, :], 0.0), writes=["stf0"])
            P.op("gpsimd", lambda e: e.memset(stbf[0][:, :, :], 0.0), writes=["stb0"])
            st_in = stb_in if kind == 1 else stc_in
            P.dma("sync", stf[1][:, :, :], st_in[h].rearrange("(c p) v -> p c v", p=128), writes=["stf1"], slot="stin")
            P.dma("gpsimd", stbf[1][:, :, :], st_in[h].rearrange("(c p) v -> p c v", p=128), writes=["stb1"], slot="stin2")
            for t in range(17):
                nt = 128 if t < 16 else 4
                si = 0 if t < 16 else 1
                hi = hcnt % 2
                hcnt += 1
                ht = hTt[hi]
                hk = "hTt%d" % hi
                P.dma("sync", ht[:, :, 0:nt], hTd[t, :, :, 0:nt], reads=["hTd%d" % t], writes=[hk], slot=hk)
                if kind == 1:
                    P.dma("sync", cs[hi][0:nt, 0, :], cd["cosB"][t, 0:nt, :], writes=["cs%da" % hi], slot="cs%da" % hi)
                    P.dma("sync", cs[hi][0:nt, 1, :], cd["sinB"][t, 0:nt, :], writes=["cs%db" % hi], slot="cs%db" % hi)
                for kc in range(16):
                    P.op("tensor", lambda e, kc=kc, ht=ht, nt=nt: e.matmul(pj[0][0:nt, :], lhsT=ht[:, kc, 0:nt], rhs=wbs[0][:, kc, :], start=(kc == 0), stop=(kc == 15)),
                         reads=[hk] + WQK, writes=["pj0"])
                for kc in range(16):
                    P.op("tensor", lambda e, kc=kc, ht=ht, nt=nt: e.matmul(pj[1][0:nt, :], lhsT=ht[:, kc, 0:nt], rhs=wbs[1][:, kc, :], start=(kc == 0), stop=(kc == 15)),
                         reads=[hk, "wv_0", "wv_1"], writes=["pj1"])
                for kc in range(16):
                    P.op("tensor", lambda e, kc=kc, ht=ht, nt=nt: e.matmul(pg[0:nt, :], lhsT=ht[:, kc, 0:nt], rhs=wbs[2][:, kc, :], start=(kc == 0), stop=(kc == 15)),
                         reads=[hk, "wg_0", "wg_1"], writes=["pg"])
                P.op("vector", lambda e, nt=nt: e.tensor_copy(out=vb[0:nt, :], in_=pj[1][0:nt, :]), reads=["pj1"], writes=["vb"])
                P.op("scalar", lambda e, nt=nt: e.activation(out=gf[0:nt, :], in_=pg[0:nt, :], func=AF.Silu), reads=["pg"], writes=["gf"])
                qb = qkb[0]
                if kind == 1:
                    qf = qkf[0]
                    P.op("scalar", lambda e, nt=nt, qf=qf: e.copy(out=qf[0:nt, :], in_=pj[0][0:nt, :]), reads=["pj0"], writes=["qkf0"])
                    q4 = qf[0:nt, :].rearrange("p (a b d) -> p a b d", a=2, b=2)
                    x1 = q4[:, :, 0, :]
                    x2 = q4[:, :, 1, :]
                    cs_ = cs[hi][0:nt, 0:1, :].broadcast_to([nt, 2, 128])
                    sn_ = cs[hi][0:nt, 1:2, :].broadcast_to([nt, 2, 128])
                    tv = [rt[0:nt, i, :].rearrange("p (a d) -> p a d", d=128) for i in range(6)]
                    ck_ = "cs%d" % hi
                    P.op("gpsimd", lambda e, tv=tv, x1=x1, cs_=cs_: e.tensor_tensor(out=tv[0], in0=x1, in1=cs_, op=ALU.mult), reads=["qkf0", ck_ + "a", ck_ + "b"], writes=["rt0"])
                    P.op("gpsimd", lambda e, tv=tv, x2=x2, sn_=sn_: e.tensor_tensor(out=tv[1], in0=x2, in1=sn_, op=ALU.mult), reads=["qkf0", ck_ + "a", ck_ + "b"], writes=["rt1"])
                    P.op("gpsimd", lambda e, tv=tv, x1=x1, sn_=sn_: e.tensor_tensor(out=tv[2], in0=x1, in1=sn_, op=ALU.mult), reads=["qkf0", ck_ + "a", ck_ + "b"], writes=["rt2"])
                    P.op("gpsimd", lambda e, tv=tv, x2=x2, cs_=cs_: e.tensor_tensor(out=tv[3], in0=x2, in1=cs_, op=ALU.mult), reads=["qkf0", ck_ + "a", ck_ + "b"], writes=["rt3"])
                    P.op("gpsimd", lambda e, tv=tv: e.tensor_tensor(out=tv[4], in0=tv[0], in1=tv[1], op=ALU.subtract), reads=["rt0", "rt1"], writes=["rt4"])
                    P.op("gpsimd", lambda e, tv=tv: e.tensor_tensor(out=tv[5], in0=tv[2], in1=tv[3], op=ALU.add), reads=["rt2", "rt3"], writes=["rt5"])
                    for (a, dsti, di) in ((0, 0, 0), (1, 1, 1), (1, 2, 2)):
                        for hf in range(2):
                            P.op("vector", lambda e, a=a, dsti=dsti, di=di, hf=hf, tv=tv, nt=nt, si=si, h=h: e.tensor_scalar(
                                out=qb[0:nt, dsti * 256 + hf * 128: dsti * 256 + (hf + 1) * 128], in0=tv[4 + hf][:, a, :],
                                scalar1=decB[0:nt, si, h, di:di + 1], scalar2=None, op0=ALU.mult),
                                reads=["rt4", "rt5", "decB"], writes=["qkb0"])
                    gC = math.exp(lgam[h] * (128.0 if t < 16 else 4.0))
                else:
                    for kc in range(16):
                        P.op("tensor", lambda e, kc=kc, ht=ht, nt=nt: e.matmul(psc[1][0:16, 0:nt], lhsT=wlr[:, kc, :], rhs=ht[:, kc, 0:nt], start=(kc == 0), stop=(kc == 15)),
                             reads=[hk, "wlr"], writes=["psc1"])
                    P.op("vector", lambda e: e.memset(lrT[:, :], 1.0), writes=["lrT"])
                    P.op("vector", lambda e, nt=nt: e.tensor_copy(out=lrT[0:16, 0:nt], in_=psc[1][0:16, 0:nt]), reads=["psc1", "lrT"], writes=["lrT"])
                    P.op("tensor", lambda e, nt=nt, h=h: e.matmul(pod[1][0:nt, 0:256], lhsT=lrT[0:17, 0:nt], rhs=wg2[0:17, h * 256:(h + 1) * 256], start=True, stop=True),
                         reads=["lrT", "wg2a", "wg2b"], writes=["pod1"])
                    P.op("scalar", lambda e, nt=nt: e.activation(out=sp[0:nt, :], in_=pod[1][0:nt, 0:256], func=AF.Exp, scale=-1.0), reads=["pod1"], writes=["sp"])
                    P.op("scalar", lambda e, nt=nt: e.activation(out=sp[0:nt, :], in_=sp[0:nt, :], func=AF.Ln, bias=1.0), reads=["sp"], writes=["sp"])
                    P.op("tensor", lambda e, nt=nt: e.matmul(pod[1][0:nt, 256:512], lhsT=triU[0:nt, 0:nt], rhs=sp[0:nt, :], start=True, stop=True),
                         reads=["sp", "triU", "pod1"], writes=["pod1"])
                    P.op("scalar", lambda e, nt=nt: e.activation(out=eb[0:nt, 0, :], in_=pod[1][0:nt, 256:512], func=AF.Exp, scale=-1.0 / 16.0), reads=["pod1"], writes=["eb"])
                    P.op("scalar", lambda e, nt=nt: e.activation(out=eb[0:nt, 1, :], in_=pod[1][0:nt, 256:512], func=AF.Exp, scale=1.0 / 16.0), reads=["pod1"], writes=["eb"])
                    for ch in range(2):
                        P.op("tensor", lambda e, nt=nt, ch=ch: e.matmul(psc[1][:, 128 + ch:129 + ch], lhsT=sp[0:nt, ch * 128:(ch + 1) * 128], rhs=ones_f[0:nt, 0:1], start=True, stop=True),
                             reads=["sp", "ones_f", "psc1"], writes=["psc1"])
                    P.op("scalar", lambda e: e.activation(out=ebl[:, 0:2], in_=psc[1][:, 128:130], func=AF.Exp, scale=-1.0 / 16.0), reads=["psc1"], writes=["ebl"])
                    P.op("vector", lambda e, nt=nt: e.scalar_tensor_tensor(out=qb[0:nt, 0:256], in0=pj[0][0:nt, 0:256], scalar=1.0 / 16.0, in1=eb[0:nt, 0, :], op0=ALU.mult, op1=ALU.mult),
                         reads=["pj0", "eb"], writes=["qkb0"])
                    P.op("vector", lambda e, nt=nt: e.tensor_tensor(out=qb[0:nt, 256:512], in0=pj[0][0:nt, 256:512], in1=eb[0:nt, 1, :], op=ALU.mult),
                         reads=["pj0", "eb"], writes=["qkb0"])
                kd0 = 512 if kind == 1 else 256
                sl = 0
                cnt["ptb"] += 1
                for j in range(4):
                    P.op("tensor", lambda e, j=j, sl=sl, nt=nt: e.transpose(out=ptb[:, sl * 512 + j * 128: sl * 512 + j * 128 + max(nt, 32)], in_=qb[0:max(nt, 32), j * 128:(j + 1) * 128], identity=ident[0:max(nt, 32), 0:max(nt, 32)]),
                         reads=["qkb0", "ident"], writes=["ptb%d" % sl])
                srcT = ptb[:, sl * 512:(sl + 1) * 512].rearrange("p (a t) -> p a t", t=128)[:, :, 0:nt]
                P.op("scalar", lambda e, srcT=srcT, nt=nt: e.copy(out=qkT[:, :, 0:nt], in_=srcT), reads=["ptb%d" % sl], writes=["qkT"])
                for ch in range(2):
                    P.op("tensor", lambda e, ch=ch, nt=nt: e.matmul(psc[0][0:nt, 0:nt], lhsT=qkT[:, 2 + ch, 0:nt], rhs=qkT[:, ch, 0:nt], start=(ch == 0), stop=(ch == 1)),
                         reads=["qkT"], writes=["psc0"])
                P.op("vector", lambda e, nt=nt: e.tensor_tensor(out=scT[0:nt, 0:nt], in0=psc[0][0:nt, 0:nt], in1=maskBf[0:nt, 0:nt], op=ALU.mult),
                     reads=["psc0", "maskBf"], writes=["scT"])
                P.op("tensor", lambda e, nt=nt: e.matmul(pod[0][0:nt, :], lhsT=scT[0:nt, 0:nt], rhs=vb[0:nt, :], start=True, stop=False),
                     reads=["scT", "vb"], writes=["pod0"])
                for ch in range(2):
                    P.op("tensor", lambda e, ch=ch, nt=nt, si=si: e.matmul(pod[0][0:nt, :], lhsT=qkT[:, ch, 0:nt], rhs=stbf[si][:, ch, :], start=False, stop=(ch == 1)),
                         reads=["qkT", "stb%d" % si], writes=["pod0"])
                for ch in range(2):
                    pstt = psc[1] if ch == 0 else pod[1]
                    pk = "psc1" if ch == 0 else "pod1"
                    P.op("tensor", lambda e, ch=ch, nt=nt, pstt=pstt: e.matmul(pstt[:, :], lhsT=qb[0:nt, kd0 + ch * 128: kd0 + (ch + 1) * 128], rhs=vb[0:nt, :], start=True, stop=True),
                         reads=["qkb0", "vb", pk], writes=[pk])
                    if kind == 1:
                        P.op("vector", lambda e, ch=ch, si=si, pstt=pstt, gC=gC: e.scalar_tensor_tensor(out=stf[si][:, ch, :], in0=stf[si][:, ch, :], scalar=gC, in1=pstt[:, :], op0=ALU.mult, op1=ALU.add),
                             reads=["stf%d" % si, pk], writes=["stf%d" % si])
                    else:
                        P.op("vector", lambda e, ch=ch, si=si, pstt=pstt: e.tensor_tensor(out=stf[si][:, ch, :], in0=stf[si][:, ch, :], in1=pstt[:, :], op=ALU.add),
                             reads=["stf%d" % si, pk], writes=["stf%d" % si])
                        P.op("vector", lambda e, ch=ch, si=si: e.tensor_scalar(out=stf[si][:, ch, :], in0=stf[si][:, ch, :], scalar1=ebl[:, ch:ch + 1], scalar2=None, op0=ALU.mult),
                             reads=["stf%d" % si, "ebl"], writes=["stf%d" % si])
                    P.op("scalar", lambda e, ch=ch, si=si: e.copy(out=stbf[si][:, ch, :], in_=stf[si][:, ch, :]), reads=["stf%d" % si], writes=["stb%d" % si])
                if kind == 1:
                    for c4 in range(4):
                        P.op("vector", lambda e, c4=c4, nt=nt: e.bn_stats(out=bst[0:nt, c4, :], in_=pod[0][0:nt, c4 * 128:(c4 + 1) * 128]), reads=["pod0"], writes=["bst"])
                    P.op("vector", lambda e, nt=nt: e.bn_aggr(out=mv[0:nt, 0:2], in_=bst[0:nt, :, :]), reads=["bst"], writes=["mv"])
                    P.op("scalar", lambda e, nt=nt: e.activation(out=mv[0:nt, 2:3], in_=mv[0:nt, 1:2], func=AF.Sqrt, bias=EPS), reads=["mv"], writes=["mv"])
                    P.op("vector", lambda e, nt=nt: e.reciprocal(out=mv[0:nt, 3:4], in_=mv[0:nt, 2:3]), reads=["mv"], writes=["mv"])
                    P.op("vector", lambda e, nt=nt: e.tensor_scalar(out=mv[0:nt, 4:5], in0=mv[0:nt, 0:1], scalar1=mv[0:nt, 3:4], scalar2=-1.0, op0=ALU.mult, op1=ALU.mult),
                         reads=["mv"], writes=["mv"])
                    P.op("scalar", lambda e, nt=nt: e.activation(out=of[0:nt, :], in_=pod[0][0:nt, :], func=AF.Identity, scale=mv[0:nt, 3:4], bias=mv[0:nt, 4:5]),
                         reads=["pod0", "mv"], writes=["of"])
                else:
                    P.op("scalar", lambda e, nt=nt: e.activation(out=of[0:nt, :], in_=pod[0][0:nt, :], func=AF.Square, accum_out=mv[0:nt, 0:1]), reads=["pod0"], writes=["of", "mv"])
                    P.op("scalar", lambda e, nt=nt: e.activation(out=mv[0:nt, 2:3], in_=mv[0:nt, 0:1], func=AF.Sqrt, scale=1.0 / 512.0, bias=EPS), reads=["mv"], writes=["mv"])
                    P.op("vector", lambda e, nt=nt: e.reciprocal(out=mv[0:nt, 3:4], in_=mv[0:nt, 2:3]), reads=["mv"], writes=["mv"])
                    P.op("scalar", lambda e, nt=nt: e.activation(out=of[0:nt, :], in_=pod[0][0:nt, :], func=AF.Identity, scale=mv[0:nt, 3:4]), reads=["pod0", "mv", "of"], writes=["of"])
                P.op("gpsimd", lambda e, nt=nt: e.tensor_tensor(out=of[0:nt, :], in0=of[0:nt, :], in1=gnv[0:nt, :], op=ALU.mult), reads=["of", "gnv"], writes=["of"])
                P.op("vector", lambda e, nt=nt: e.tensor_tensor(out=yb[0:nt, :], in0=of[0:nt, :], in1=gf[0:nt, :], op=ALU.mult), reads=["of", "gf"], writes=["yb"])
                sl = 0
                cnt["ptb"] += 1
                for j in range(4):
                    P.op("tensor", lambda e, j=j, sl=sl, nt=nt: e.transpose(out=ptb[:, sl * 512 + j * 128: sl * 512 + j * 128 + max(nt, 32)], in_=yb[0:max(nt, 32), j * 128:(j + 1) * 128], identity=ident[0:max(nt, 32), 0:max(nt, 32)]),
                         reads=["yb", "ident"], writes=["ptb%d" % sl])
                srcT = ptb[:, sl * 512:(sl + 1) * 512].rearrange("p (a t) -> p a t", t=128)[:, :, 0:nt]
                P.op("scalar", lambda e, srcT=srcT, nt=nt, t=t: e.copy(out=yT[:, :, t * 128:t * 128 + nt], in_=srcT), reads=["ptb%d" % sl], writes=["yT"])
            po = p_stb if kind == 1 else p_stc
            so = s_stb if kind == 1 else s_stc
            P.dma("sync", po[h].rearrange("(c p) v -> p c v", p=128), stf[0][:, :, :], reads=["stf0"], slot="sto")
            P.dma("sync", so[h].rearrange("(c p) v -> p c v", p=128), stf[1][:, :, :], reads=["stf1"], slot="sto")
            P.dma("gpsimd", wo[:, :, :], w_out[h * 512:(h + 1) * 512, :].rearrange("(k p) n -> p k n", p=128), writes=["wo"], slot="wo")
            if h == NH - 1:
                load_gvec(norm_g[li + 1, :] if li < NL - 1 else final_g)
            wout_pass(li, yT, wo, 4, first=(h == 0), last=(h == NH - 1), next_kind=next_kind, hT_next=hT_next)

    kinds = [i % 3 for i in range(NL)]
    VA = layer_A(0, 0)
    load_gvec(norm_g[0, :])
    for t in range(17):
        nt = 128 if t < 16 else 4
        src = xp[t * 128:t * 128 + nt, :] if t < 16 else xs[:, :]
        P.dma("sync", xt[0:nt, :], src, writes=["xt"], slot="xt")
        rms_to(nt, "gvec", hn[0:nt, :], "hn", extra_reads=["hn"])
        produce_h(t, nt, VA["hT"], to_dram=False)
    for li in range(NL):
        kind = kinds[li]
        print('MARK layer', li, 'starts at op', len(P.ops), flush=True)
        nk_ = kinds[li + 1] if li + 1 < NL else -1
        if kind == 0:
            run_layer_A(li, li // 3, VA, nk_, VA["hst"] if nk_ in (1, 2) else VA["hT"])
            P.barrier()
        else:
            VB = views_BC()
            if nk_ == 0:
                hT_next = VB["hTst"]
                run_layer_BC(li, kind, VB, 1, hT_next)
                P.barrier()
                VA = layer_A(li + 1, (li + 1) // 3)
                for t in range(17):
                    nt = 128 if t < 16 else 4
                    P.dma("sync", VA["hT"][:, :, t * 128:t * 128 + nt], hTd[t, :, :, 0:nt], reads=["hTd%d" % t], writes=["hT%d" % t], slot="hTl")
            else:
                run_layer_BC(li, kind, VB, nk_, VB["hTst"])
                P.barrier()
    P.emit()
    for g in reversed(ctxs):
        g.__exit__(None, None, None)
    return nc, hc, P.stats


_CACHE = {}


def kernel(**inp):
    if "nc" not in _CACHE:
        _CACHE["nc"] = build()
    nc, hc, stats = _CACHE["nc"]
    f = lambda a: np.ascontiguousarray(np.asarray(a, dtype=np.float32))
    in_maps = []
    shared = dict(
        norm_g=f(inp["norm_g"]), final_g=f(inp["final_g"]), w_in_a=f(inp["w_in_a"]), w_out_a=f(inp["w_out_a"]),
        w_in_b=f(inp["w_in_b"][0]), gn_b=f(inp["gn_b"][0]), w_out_b=f(inp["w_out_b"][0]), w_in_c=f(inp["w_in_c"][0]),
        w_gate2=f(inp["w_gate2_c"][0]), b_gate=f(inp["b_gate_c"][0]).reshape(1, 1024), gn_c=f(inp["gn_c"][0]), w_out_c=f(inp["w_out_c"][0]))
    for kk, v in hc.items():
        shared["c_" + kk] = v
    for c in range(8):
        m = dict(shared)
        m["xp"] = f(inp["x_prompt"][c % 4])
        m["xs"] = f(inp["x_sample"][c])
        for g, nm in enumerate(("cache_a_kv1", "cache_a_kv2", "cache_a_kv3")):
            m["ckv%d" % g] = f(inp[nm][:, c]).reshape(2, LBUF[g], 4096)
        m["stb"] = f(inp["state_b"][0, c])
        m["stc"] = f(inp["state_c"][0, c])
        in_maps.append(m)
    res = run_bass_kernel_spmd(nc, in_maps, core_ids=list(range(8)))
    R = res.results
    y_prompt = np.stack([R[b]["y_p"] for b in range(4)])
    y_sample = np.stack([R[c]["y_s"] for c in range(8)])
    pkv = []
    for g in range(3):
        keep = min(LBUF[g], S)
        a = np.stack([R[b]["p_kv%d" % g] for b in range(4)], axis=1)
        pkv.append(a.reshape(2, 4, keep, 2, 16, 128))
    p_stb = np.stack([R[b]["p_stb"] for b in range(4)])[None]
    p_stc = np.stack([R[b]["p_stc"] for b in range(4)])[None]
    skv = np.stack([R[c]["s_kv"] for c in range(8)], axis=0)
    s_list = [np.ascontiguousarray(skv[:, :, g].transpose(1, 0, 2, 3)).reshape(2, 8, 4, 2, 16, 128) for g in range(3)]
    s_stb = np.stack([R[c]["s_stb"] for c in range(8)])[None]
    s_stc = np.stack([R[c]["s_stc"] for c in range(8)])[None]
    outs = (y_prompt, y_sample, pkv[0], pkv[1], pkv[2], p_stb, p_stc, s_list[0], s_list[1], s_list[2], s_stb, s_stc)
    return tuple(np.ascontiguousarray(o, dtype=np.float32) for o in outs)
```

```python
import math
import numpy as np
import ml_dtypes
import concourse.bass as bass
import concourse.mybir as mybir
from concourse.bass_utils import run_bass_kernel_spmd

F32 = mybir.dt.float32
BF16 = mybir.dt.bfloat16
ALU = mybir.AluOpType
AF = mybir.ActivationFunctionType
AX = mybir.AxisListType

ENGS = ("sync", "scalar", "vector", "gpsimd", "tensor")
EPOCH = 12000
S = 2048
DM = 2048
SS = 2052
PAST = 16384
EPS = 1e-6
DILS = (1, 4, 16)
LBUF = (128, 512, 2048)
NL_DEFAULT = 4


PSUM_KEYS = {"pj0", "pj1", "ptb0", "psc0", "psc1", "pod0", "pod1", "pg"}


class Prog:
    def __init__(self, nc):
        self.nc = nc
        self.ops = []

    def op(self, eng, fn, reads=(), writes=(), dma=None):
        self.ops.append(dict(eng=eng, fn=fn, reads=tuple(reads), writes=tuple(writes), dma=dma, bar=False))

    def dma(self, eng, out, in_, reads=(), writes=(), slot="d", **kw):
        self.op(eng, lambda e: e.dma_start(out=out, in_=in_, **kw), reads, writes, dma=slot)

    def barrier(self):
        self.ops.append(dict(bar=True))

    def emit(self):
        nc = self.nc
        import os
        allops = self.ops[:int(os.environ.get('MAXOPS', '100000000'))]
        ops = []
        last_w = {}
        readers = {}
        dma_count = {}
        last_eng = {}
        last_dma = {}
        pending_bar = {}
        for o in allops:
            if o["bar"]:
                deps = set(last_eng.values()) | set(last_dma.values())
                for e in ENGS:
                    pending_bar[e] = set(deps)
                last_w.clear()
                readers.clear()
                continue
            j = len(ops)
            ops.append(o)
            deps = set()
            for r in o["reads"]:
                if r in last_w:
                    deps.add(last_w[r])
                if r in PSUM_KEYS:
                    for rd in readers.get(r, ()):
                        if ops[rd]["eng"] != o["eng"]:
                            deps.add(rd)
            for w in o["writes"]:
                if w in last_w:
                    deps.add(last_w[w])
                for rd in readers.get(w, ()):
                    if ops[rd]["eng"] == o["eng"] and ops[rd]["dma"] is None and o["dma"] is None:
                        continue
                    deps.add(rd)
            if o["eng"] in pending_bar:
                deps |= pending_bar.pop(o["eng"])
            fin = set()
            for i in deps:
                if i == j:
                    continue
                if ops[i]["dma"] is None and o["dma"] is None and ops[i]["eng"] == "tensor" and o["eng"] == "tensor":
                    continue
                fin.add(i)
            o["deps"] = fin
            o["dma_snap"] = dict(dma_count)
            if o["dma"] is not None:
                dma_count[o["dma"]] = dma_count.get(o["dma"], 0) + 1
                o["dma_idx"] = dma_count[o["dma"]]
                last_dma[o["dma"]] = j
            else:
                last_eng[o["eng"]] = j
            for r in o["reads"]:
                readers.setdefault(r, []).append(j)
            for w in o["writes"]:
                last_w[w] = j
                readers[w] = []
        for o in ops:
            o["ms"] = False
        for o in ops:
            for i in o["deps"]:
                if ops[i]["dma"] is None:
                    ops[i]["ms"] = True
        cnt = {e: 0 for e in ENGS}
        for o in ops:
            if o["ms"]:
                cnt[o["eng"]] += 1
                o["ms_idx"] = cnt[o["eng"]]
        sems = {}
        stack = []

        def get_sem(key):
            if key not in sems:
                g = nc.semaphore("s%d" % len(sems))
                sems[key] = g.__enter__()
                stack.append(g)
            return sems[key]

        seen = {e: {} for e in ENGS}
        for o in ops:
            waits = {}
            for i in o["deps"]:
                d = ops[i]
                if d["dma"] is not None:
                    n = max(o["dma_snap"].get(d["dma"], 0), d["dma_idx"])
                    ep, v = divmod(n - 1, EPOCH // 16)
                    key = ("dma", d["dma"], ep)
                    waits[key] = max(waits.get(key, 0), (v + 1) * 16)
                else:
                    ep, v = divmod(d["ms_idx"] - 1, EPOCH)
                    key = ("eng", d["eng"], ep)
                    waits[key] = max(waits.get(key, 0), v + 1)
            wl = []
            sn = seen[o["eng"]]
            for key, v in waits.items():
                if sn.get(key, 0) >= v:
                    continue
                if key[0] == "eng" and any(k2[0] == "eng" and k2[1] == key[1] and k2[2] > key[2] for k2 in sn):
                    continue
                sn[key] = v
                wl.append((key, v))
            o["waits"] = wl
        for o in ops:
            for key, v in o["waits"]:
                get_sem(key)
            if o["ms"]:
                ep, v = divmod(o["ms_idx"] - 1, EPOCH)
                o["inc"] = (get_sem(("eng", o["eng"], ep)), 1)
            if o["dma"] is not None:
                ep, v = divmod(o["dma_idx"] - 1, EPOCH // 16)
                o["inc"] = (get_sem(("dma", o["dma"], ep)), 16)
        if os.environ.get('DUMPW'):
            a_, b_ = [int(v_) for v_ in os.environ['DUMPW'].split(':')]
            for i_ in range(a_, min(b_, len(ops))):
                o_ = ops[i_]
                print('OP', i_, o_['eng'], o_['dma'], 'deps', sorted(o_['deps'])[-6:], 'waits', o_['waits'], 'ms', o_.get('ms_idx') if o_['ms'] else None, 'dmaidx', o_.get('dma_idx'), flush=True)
        final = []
        for slot, n in dma_count.items():
            ep, v = divmod(n - 1, EPOCH // 16)
            final.append((get_sem(("dma", slot, ep)), (v + 1) * 16))

        with nc.Block() as block:
            def mk(engname):
                def body(e):
                    for o in ops:
                        if o["eng"] != engname:
                            continue
                        for key, v in o["waits"]:
                            e.wait_ge(sems[key], v)
                        ins = o["fn"](e)
                        if "inc" in o:
                            ins.then_inc(o["inc"][0], o["inc"][1])
                    if engname == "sync":
                        for s_, v in final:
                            e.wait_ge(s_, v)
                return body
            block.sync(mk("sync"))
            block.scalar(mk("scalar"))
            block.vector(mk("vector"))
            block.gpsimd(mk("gpsimd"))
            block.tensor(mk("tensor"))
        for g in reversed(stack):
            g.__exit__(None, None, None)
        self.stats = dict(n_ops=len(ops), ms=cnt, nsem=len(sems))


def host_consts():
    c = {}
    bf = ml_dtypes.bfloat16
    c["ident"] = np.eye(128, dtype=np.float32).astype(bf)
    k = np.arange(128)[:, None]
    q = np.arange(128)[None, :]
    A = (k >= q).astype(np.float32)
    B = (k <= q).astype(np.float32)
    c["maskAB"] = np.concatenate([A, B, A, B], axis=1).astype(bf)
    c["maskBf"] = B.astype(np.float32)
    c["ones_bf"] = np.ones((128, 128), np.float32).astype(bf)
    c["ones_f"] = np.ones((128, 8), np.float32)
    half = 16
    inv = (500000.0 ** (-np.arange(half, dtype=np.float32) / half)).astype(np.float32)
    cosA = np.zeros((128, 3, 17, 16), np.float32)
    sinA = np.zeros((128, 3, 17, 16), np.float32)
    for g, dil in enumerate(DILS):
        nb = 16 // dil
        for tau in range(16):
            r, n = divmod(tau, nb)
            pos = ((128 * n + np.arange(128)) * dil + r).astype(np.float32)
            ang = pos[:, None] * inv[None, :]
            cosA[:, g, tau] = np.cos(ang)
            sinA[:, g, tau] = np.sin(ang)
        pos = (PAST + np.arange(4)).astype(np.float32)
        ang = pos[:, None] * inv[None, :]
        cosA[0:4, g, 16] = np.cos(ang)
        sinA[0:4, g, 16] = np.sin(ang)
    c["cosA"] = cosA
    c["sinA"] = sinA
    invb = (10000.0 ** (-np.arange(128, dtype=np.float32) / 128)).astype(np.float32)
    cosB = np.zeros((17, 128, 128), np.float32)
    sinB = np.zeros((17, 128, 128), np.float32)
    for t in range(16):
        pos = (128 * t + np.arange(128)).astype(np.float32)
        ang = pos[:, None] * invb[None, :]
        cosB[t] = np.cos(ang)
        sinB[t] = np.sin(ang)
    pos = (PAST + np.arange(4)).astype(np.float32)
    ang = pos[:, None] * invb[None, :]
    cosB[16, 0:4] = np.cos(ang)
    sinB[16, 0:4] = np.sin(ang)
    c["cosB"] = cosB
    c["sinB"] = sinB
    lg = np.log(1.0 - 2.0 ** (-5.0 - np.arange(8, dtype=np.float64)))
    dec = np.zeros((128, 2, 8, 3), np.float64)
    i = np.arange(128, dtype=np.float64)[:, None]
    dec[:, 0, :, 0] = np.exp(lg[None, :] * (i + 1.0))
    dec[:, 0, :, 1] = np.exp(-lg[None, :] * (i + 1.0)) / 16.0
    dec[:, 0, :, 2] = np.exp(lg[None, :] * (127.0 - i)) / 16.0
    dec[:, 1, :, 0] = np.exp(lg[None, :] * (i + 1.0))
    dec[:, 1, :, 1] = np.exp(-lg[None, :] * (i + 1.0)) / 16.0
    dec[:, 1, :, 2] = np.exp(lg[None, :] * (3.0 - i)) / 16.0
    c["decB"] = dec.astype(np.float32)
    j = np.arange(128)[:, None]
    t = np.arange(4)[None, :]
    sm = np.zeros((128, 9, 4), np.float32)
    sm[:, 0, :] = (j >= t)
    for s_ in range(4):
        sm[:, 1 + s_, s_] = 1.0
    c["smask"] = sm.astype(bf)
    mn = np.zeros((4, 3, 4), np.float32)
    tp = np.arange(4)[:, None]
    mn[:, 0, :] = (tp <= t)
    mn[:, 1, :] = (tp == t)
    mn[:, 2, :] = (tp == t)
    c["mnew"] = mn.astype(bf)
    c["triU"] = (k <= q).astype(np.float32)
    return c


CONST_SHAPES = None


def build(NL=NL_DEFAULT, debug=False, NHA=16):
    nc = bass.Bass("TRN2", target_bir_lowering=False)
    hc = host_consts()

    def din(name, shape, dt=F32):
        return nc.dram_tensor(name, list(shape), dt, kind="ExternalInput").ap()

    def dout(name, shape, dt=F32):
        return nc.dram_tensor(name, list(shape), dt, kind="ExternalOutput").ap()

    xp = din("xp", [S, DM])
    xs = din("xs", [4, DM])
    ckv = [din("ckv%d" % g, [2, LBUF[g], 4096]) for g in range(3)]
    stb_in = din("stb", [8, 256, 512])
    stc_in = din("stc", [4, 256, 512])
    norm_g = din("norm_g", [4, DM])
    final_g = din("final_g", [DM])
    w_in_a = din("w_in_a", [2, DM, 20480])
    w_out_a = din("w_out_a", [2, 2048, DM])
    w_in_b = din("w_in_b", [DM, 12288])
    gn_b = din("gn_b", [8, 512])
    w_out_b = din("w_out_b", [4096, DM])
    w_in_c = din("w_in_c", [DM, 6160])
    w_gate2 = din("w_gate2", [16, 1024])
    b_gate = din("b_gate", [1, 1024])
    gn_c = din("gn_c", [4, 512])
    w_out_c = din("w_out_c", [2048, DM])
    cd = {}
    for kk, v in hc.items():
        cd[kk] = din("c_" + kk, v.shape, BF16 if v.dtype == ml_dtypes.bfloat16 else F32)

    y_p = dout("y_p", [S, DM])
    y_s = dout("y_s", [4, DM])
    p_kv = [dout("p_kv%d" % g, [2, LBUF[g] if LBUF[g] < S else S, 4096]) for g in range(3)]
    p_stb = dout("p_stb", [8, 256, 512])
    p_stc = dout("p_stc", [4, 256, 512])
    s_kv = dout("s_kv", [2, 3, 4, 4096])
    s_stb = dout("s_stb", [8, 256, 512])
    s_stc = dout("s_stc", [4, 256, 512])
    xres = nc.dram_tensor("xres", [SS, DM], F32, kind="Internal").ap()
    dbg_x = dout("dbg_x", [SS, DM]) if debug else None
    dbg_acc = dout("dbg_acc", [128, 2, SS]) if debug else None
    dbg_vg = dout("dbg_vg", [128, 17, 128], BF16) if debug else None
    dbg_qkT = dout("dbg_qkT", [128, 2, 17, 128], BF16) if debug else None
    dbg_yT = dout("dbg_yT", [128, 2, SS], BF16) if debug else None
    hTd = nc.dram_tensor("hTd", [17, 128, 16, 128], BF16, kind="Internal").ap()

    P = Prog(nc)
    ctxs = []

    def sb(name, shape, dt):
        g = nc.sbuf_tensor(name, list(shape), dt)
        t = g.__enter__()
        ctxs.append(g)
        return t

    def ps(name, shape, dt):
        g = nc.psum_tensor(name, list(shape), dt)
        t = g.__enter__()
        ctxs.append(g)
        return t

    pj = [ps("pj0", [128, 512], F32), ps("pj1", [128, 512], F32)]
    ptb = ps("ptb", [128, 512], BF16)
    psc = [ps("psc0", [128, 512], F32), ps("psc1", [128, 512], F32)]
    pod = [ps("pod0", [128, 512], F32), ps("pod1", [128, 512], F32)]
    pg = ps("pg", [128, 512], F32)

    ident = sb("ident", [128, 128], BF16)
    maskAB = sb("maskAB", [128, 512], BF16)
    maskBf = sb("maskBf", [128, 128], F32)
    ones_bf = sb("ones_bf", [128, 128], BF16)
    ones_f = sb("ones_f", [128, 8], F32)
    cosA = sb("cosA", [128, 3, 17, 16], F32)
    sinA = sb("sinA", [128, 3, 17, 16], F32)
    decB = sb("decB", [128, 2, 8, 3], F32)
    smask = sb("smask", [128, 9, 4], BF16)
    mnew = sb("mnew", [4, 3, 4], BF16)
    triU = sb("triU", [128, 128], F32)
    gvec = sb("gvec", [128, DM], F32)
    for nm, t in (("ident", ident), ("maskAB", maskAB), ("maskBf", maskBf), ("ones_bf", ones_bf), ("ones_f", ones_f),
                  ("cosA", cosA), ("sinA", sinA), ("decB", decB), ("smask", smask), ("mnew", mnew), ("triU", triU)):
        P.dma("sync", t[:], cd[nm], writes=[nm], slot="const")

    AR_BYTES = 154880
    arena = sb("arena", [128, AR_BYTES // 4], F32)
    ar_off = [0]

    def ar_reset():
        ar_off[0] = 0

    def av(shape, dt):
        n = int(np.prod(shape[1:])) * (2 if dt == BF16 else 4)
        n4 = (n + 3) // 4
        o = ar_off[0]
        assert o + n4 <= AR_BYTES // 4, ("arena overflow", o, n4)
        ar_off[0] = o + ((n4 + 7) // 8) * 8
        v = arena[0:shape[0], o:o + n4]
        if dt == BF16:
            v = v.bitcast(BF16)
        if len(shape) == 3:
            v = v.rearrange("p (a b) -> p a b", b=shape[2])
        elif len(shape) == 4:
            v = v.rearrange("p (a b c) -> p a b c", b=shape[2], c=shape[3])
        return v

    xts = [sb("xt0", [128, DM], F32), sb("xt1", [128, DM], F32)]
    hn = sb("hn", [128, DM], BF16)
    st4 = sb("st4", [128, 8], F32)
    qkf = [sb("qkf0", [128, 512], F32), sb("qkf1", [128, 256], F32)]
    vf = [sb("vf0", [128, 128], F32), sb("vf1", [128, 128], F32)]
    qkb = [sb("qkb0", [128, 1024], BF16), sb("qkb1", [128, 256], BF16)]
    rt = sb("rt", [128, 6, 256], F32)
    PT = [sb("PT0", [128, 512], BF16), sb("PT1", [128, 512], BF16)]
    sg = sb("sg", [128, 512], F32)
    rc = sb("rc", [128, 512], F32)

    HT_ALL = ["hT%d" % t for t in range(17)]
    scale_a = 1.0 / math.sqrt(128.0)
    cnt = {"pj": 0, "att": 0, "ptb": 0, "t": 0}

    def rms_to(nt, gsrc_key, out_ap, out_key, xt, xk, extra_reads=()):
        P.op("scalar", lambda e: e.activation(out=hn[0:nt, :], in_=xt[0:nt, :], func=AF.Square, accum_out=st4[0:nt, 0:1]),
             reads=[xk], writes=["hn", "st4"])
        P.op("scalar", lambda e: e.activation(out=st4[0:nt, 1:2], in_=st4[0:nt, 0:1], func=AF.Sqrt, scale=1.0 / DM, bias=EPS),
             reads=["st4"], writes=["st4"])
        P.op("vector", lambda e: e.reciprocal(out=st4[0:nt, 2:3], in_=st4[0:nt, 1:2]), reads=["st4"], writes=["st4"])
        P.op("vector", lambda e: e.scalar_tensor_tensor(out=out_ap, in0=xt[0:nt, :], scalar=st4[0:nt, 2:3], in1=gvec[0:nt, :],
                                                       op0=ALU.mult, op1=ALU.mult),
             reads=[xk, "st4", gsrc_key] + list(extra_reads), writes=[out_key])

    def load_gvec(src_row_ap):
        P.dma("sync", gvec[:], src_row_ap.partition_broadcast(128), writes=["gvec"], slot="gvec")

    def produce_h(t, nt, hT, to_dram):
        for q4 in range(4):
            sl = 0
            cnt["ptb"] += 1
            for j in range(4):
                kc = q4 * 4 + j
                P.op("tensor", lambda e, kc=kc, j=j, sl=sl: e.transpose(out=ptb[:, sl * 512 + j * 128: sl * 512 + j * 128 + max(nt, 32)],
                                                                         in_=hn[0:max(nt, 32), kc * 128:(kc + 1) * 128], identity=ident[0:max(nt, 32), 0:max(nt, 32)]),
                     reads=["hn", "ident"], writes=["ptb%d" % sl])
            src = ptb[:, sl * 512:(sl + 1) * 512].rearrange("p (a t) -> p a t", t=128)[:, :, 0:nt]
            if not to_dram:
                dst = hT[:, q4 * 4:(q4 + 1) * 4, t * 128:t * 128 + nt]
                P.op("scalar", lambda e, dst=dst, src=src: e.copy(out=dst, in_=src), reads=["ptb%d" % sl], writes=["hT%d" % t])
            else:
                dst = hT[:, q4 * 4:(q4 + 1) * 4, 0:nt]
                P.op("scalar", lambda e, dst=dst, src=src: e.copy(out=dst, in_=src), reads=["ptb%d" % sl], writes=["hTst"])
        if to_dram:
            P.dma("sync", hTd[t, :, :, 0:nt], hT[:, :, 0:nt], reads=["hTst"], writes=["hTd%d" % t], slot="hTd")

    def wout_pass(li, yT, wo, nk, first, last, next_kind, hT_next):
        for t in range(17):
            nt = 128 if t < 16 else 4
            r0 = t * 128
            xt = xts[t % 2]
            xk = "xt%d" % (t % 2)
            if li == 0 and first:
                src = xp[r0:r0 + nt, :] if t < 16 else xs[:, :]
                P.dma("sync", xt[0:nt, :], src, writes=[xk], slot=xk)
            else:
                P.dma("sync", xt[0:nt, :], xres[r0:r0 + nt, :], reads=["xres%d" % t], writes=[xk], slot=xk)
            banks = [pj[0], pj[1], psc[0], psc[1]]
            bkeys = ["pj0", "pj1", "psc0", "psc1"]
            for cb in range(4):
                for k in range(nk):
                    P.op("tensor", lambda e, cb=cb, k=k, nt=nt, r0=r0, banks=banks: e.matmul(banks[cb][0:nt, :], lhsT=yT[:, k, r0:r0 + nt],
                                                                   rhs=wo[:, k, cb * 512:(cb + 1) * 512], start=(k == 0), stop=(k == nk - 1)),
                         reads=["yT", "wo"], writes=[bkeys[cb]])
            for cb in range(4):
                P.op("vector", lambda e, cb=cb, nt=nt, banks=banks, xt=xt: e.tensor_tensor(out=xt[0:nt, cb * 512:(cb + 1) * 512], in0=xt[0:nt, cb * 512:(cb + 1) * 512],
                                                                 in1=banks[cb][0:nt, :], op=ALU.add),
                     reads=[xk, bkeys[cb]], writes=[xk])
            is_final = last and (li == NL - 1)
            if not is_final:
                P.dma("sync", xres[r0:r0 + nt, :], xt[0:nt, :], reads=[xk], writes=["xres%d" % t], slot="xres")
            if last:
                if is_final:
                    if debug:
                        P.dma("sync", dbg_x[r0:r0 + nt, :], xt[0:nt, :], reads=[xk], slot="dbg")
                    rms_to(nt, "gvec", xt[0:nt, :], xk, xt, xk)
                    dst = y_p[r0:r0 + nt, :] if t < 16 else y_s[:, :]
                    P.dma("sync", dst, xt[0:nt, :], reads=[xk], slot="yout")
                else:
                    rms_to(nt, "gvec", hn[0:nt, :], "hn", xt, xk, extra_reads=["hn"])
                    produce_h(t, nt, hT_next, to_dram=(next_kind != 0))

    def layer_A(li, slot):
        ar_reset()
        hT = av([128, 16, SS], BF16)
        wb = [av([128, 16, 384], BF16) for _ in range(3)]
        wo = av([128, 2, DM], BF16)
        yT = av([128, 2, SS], BF16)
        qkT = av([128, 2, 17, 128], BF16)
        vg = av([128, 17, 128], BF16)
        acc = av([128, 2, SS], F32)
        ck = av([128, 4, 2, 128], BF16)
        ckT = av([128, 128], BF16)
        PTs = av([128, 32], BF16)
        hst = av([128, 16, 128], BF16)
        return dict(hT=hT, wb=wb, wo=wo, yT=yT, qkT=qkT, vg=vg, acc=acc, ck=ck, ckT=ckT, PTs=PTs, hst=hst)

    def run_layer_A(li, slot, V, next_kind, hT_next):
        hT, wbs, wo, yT, qkT, vg, acc, ck, ckT, PTs = (V[k] for k in ("hT", "wb", "wo", "yT", "qkT", "vg", "acc", "ck", "ckT", "PTs"))
        wsrc = w_in_a[slot].rearrange("(kc p) (s c) -> p kc s c", p=128, c=2048)
        wi = 0
        for h in range(NHA):
            hs = h % 2
            for g in range(3):
                dil = DILS[g]
                nb = 16 // dil
                bi = wi % 3
                wi += 1
                wb = wbs[bi]
                wkeys = ["wb%d_%d" % (bi, kq) for kq in range(3)]
                for kq in range(3):
                    P.dma("gpsimd", wb[:, :, kq * 128:(kq + 1) * 128],
                          wsrc[:, :, 3 * g + kq, h * 128:(h + 1) * 128], writes=[wkeys[kq]], slot=wkeys[kq])
                pendT = None
                for tau in range(17):
                    nt = 128 if tau < 16 else 4
                    if tau < 16:
                        r, n = divmod(tau, nb)
                        st0 = n * 128 * dil + r
                        tsl = slice(st0, st0 + 127 * dil + 1, dil)
                    else:
                        tsl = slice(S, S + 4)
                    pi = cnt["pj"] % 2
                    cnt["pj"] += 1
                    pjt = pj[pi]
                    for kc in range(16):
                        P.op("tensor", lambda e, kc=kc, tsl=tsl, pjt=pjt, wb=wb, nt=nt: e.matmul(pjt[0:nt, 0:384], lhsT=hT[:, kc, tsl], rhs=wb[:, kc, :],
                                                                                          start=(kc == 0), stop=(kc == 15)),
                             reads=HT_ALL + wkeys, writes=["pj%d" % pi])
                    ti = cnt["t"] % 2
                    cnt["t"] += 1
                    qf = qkf[ti]
                    P.op("scalar", lambda e, qf=qf, pjt=pjt, nt=nt: e.copy(out=qf[0:nt, 0:256], in_=pjt[0:nt, 0:256]),
                         reads=["pj%d" % pi], writes=["qkf%d" % ti])
                    P.op("vector", lambda e, pjt=pjt, nt=nt, tau=tau: e.tensor_copy(out=vg[0:nt, tau, :], in_=pjt[0:nt, 256:384]),
                         reads=["pj%d" % pi], writes=["vg%d" % tau])
                    if tau == 16:
                        need = True
                    elif g == 0:
                        need = (tau == 15)
                    elif g == 1:
                        need = (tau % nb == nb - 1)
                    else:
                        need = True
                    if need:
                        P.op("scalar", lambda e, pjt=pjt, nt=nt, ti=ti: e.copy(out=vf[ti][0:nt, :], in_=pjt[0:nt, 256:384]),
                             reads=["pj%d" % pi], writes=["vf%d" % ti])
                    q3 = qf[0:nt, 0:256].rearrange("p (a d) -> p a d", d=128)
                    x1 = q3[:, :, 0:16]
                    x2 = q3[:, :, 16:32]
                    cs = cosA[0:nt, g, tau:tau + 1, :].broadcast_to([nt, 2, 16])
                    sn = sinA[0:nt, g, tau:tau + 1, :].broadcast_to([nt, 2, 16])
                    tv = [rt[0:nt, i, 0:32].rearrange("p (a d) -> p a d", d=16) for i in range(4)]
                    kq_ = "qkf%d" % ti
                    P.op("gpsimd", lambda e, tv=tv, x1=x1, cs=cs: e.tensor_tensor(out=tv[0], in0=x1, in1=cs, op=ALU.mult), reads=[kq_, "cosA"], writes=["rt0"])
                    P.op("gpsimd", lambda e, tv=tv, x2=x2, sn=sn: e.tensor_tensor(out=tv[1], in0=x2, in1=sn, op=ALU.mult), reads=[kq_, "sinA"], writes=["rt1"])
                    P.op("gpsimd", lambda e, tv=tv, x1=x1, sn=sn: e.tensor_tensor(out=tv[2], in0=x1, in1=sn, op=ALU.mult), reads=[kq_, "sinA"], writes=["rt2"])
                    P.op("gpsimd", lambda e, tv=tv, x2=x2, cs=cs: e.tensor_tensor(out=tv[3], in0=x2, in1=cs, op=ALU.mult), reads=[kq_, "cosA"], writes=["rt3"])
                    P.op("gpsimd", lambda e, tv=tv, x1=x1: e.tensor_tensor(out=x1, in0=tv[0], in1=tv[1], op=ALU.subtract), reads=["rt0", "rt1", kq_], writes=[kq_])
                    P.op("gpsimd", lambda e, tv=tv, x2=x2: e.tensor_tensor(out=x2, in0=tv[2], in1=tv[3], op=ALU.add), reads=["rt2", "rt3", kq_], writes=[kq_])
                    qb = qkb[ti]
                    P.op("vector", lambda e, qb=qb, qf=qf, nt=nt: e.tensor_copy(out=qb[0:nt, 0:256], in_=qf[0:nt, 0:256]),
                         reads=[kq_], writes=["qkb%d" % ti])
                    def emit_T(tau, nt, ti, qb):
                        sl = 0
                        for a in range(2):
                            P.op("tensor", lambda e, a=a, sl=sl, qb=qb, nt=nt: e.transpose(out=ptb[:, sl * 512 + a * 128: sl * 512 + a * 128 + max(nt, 32)],
                                                                                        in_=qb[0:max(nt, 32), a * 128:(a + 1) * 128], identity=ident[0:max(nt, 32), 0:max(nt, 32)]),
                                 reads=["qkb%d" % ti, "ident"], writes=["ptb%d" % sl])
                        srcT = ptb[:, sl * 512:sl * 512 + 256].rearrange("p (a t) -> p a t", t=128)[:, :, 0:nt]
                        P.op("scalar", lambda e, srcT=srcT, tau=tau, nt=nt: e.copy(out=qkT[:, :, tau, 0:nt], in_=srcT),
                             reads=["ptb%d" % sl], writes=["qkT%d" % tau])
                    if pendT is not None:
                        emit_T(*pendT)
                    pendT = (tau, nt, ti, qb)
                    if need:
                        if tau == 16:
                            dk = s_kv[slot, g, :, h * 128:(h + 1) * 128]
                            dv = s_kv[slot, g, :, 2048 + h * 128:2048 + (h + 1) * 128]
                        else:
                            r, n = divmod(tau, nb)
                            keep = min(LBUF[g], S)
                            base = (n * 128 * dil + r) - (S - keep)
                            rows = slice(base, base + 127 * dil + 1, dil)
                            dk = p_kv[g][slot, rows, h * 128:(h + 1) * 128]
                            dv = p_kv[g][slot, rows, 2048 + h * 128:2048 + (h + 1) * 128]
                        P.dma("sync", dk, qf[0:nt, 128:256], reads=[kq_], slot="okv%d" % ti)
                        P.dma("sync", dv, vf[ti][0:nt, :], reads=["vf%d" % ti], slot="okv%d" % ti)
                emit_T(*pendT)
                def emit_S(pair):
                    ai = cnt["att"] % 2
                    cnt["att"] += 1
                    pst, pot, ptt = psc[ai], pod[ai], PT[ai]
                    ks_, ko_, kp_ = "psc%d" % ai, "pod%d" % ai, "PT%d" % ai
                    for b in range(2):
                        tau = 2 * pair + b
                        r, n = divmod(tau, nb)
                        if n > 0:
                            P.op("tensor", lambda e, tau=tau, b=b, pst=pst: e.matmul(pst[:, (2 * b) * 128:(2 * b + 1) * 128], lhsT=qkT[:, 1, tau - 1, :],
                                                                                   rhs=qkT[:, 0, tau, :], start=True, stop=True),
                                 reads=["qkT%d" % (tau - 1), "qkT%d" % tau], writes=[ks_])
                        P.op("tensor", lambda e, tau=tau, b=b, pst=pst: e.matmul(pst[:, (2 * b + 1) * 128:(2 * b + 2) * 128], lhsT=qkT[:, 1, tau, :],
                                                                               rhs=qkT[:, 0, tau, :], start=True, stop=True),
                             reads=["qkT%d" % tau], writes=[ks_])
                    P.op("scalar", lambda e, pst=pst, ptt=ptt: e.activation(out=ptt[:, :], in_=pst[:, :], func=AF.Exp, scale=scale_a),
                         reads=[ks_], writes=[kp_])
                    P.op("vector", lambda e, ptt=ptt: e.tensor_tensor(out=ptt[:, :], in0=ptt[:, :], in1=maskAB[:, :], op=ALU.mult),
                         reads=[kp_, "maskAB"], writes=[kp_])
                    P.op("vector", lambda e, pot=pot: e.memset(pot[:, :], 0.0), writes=[ko_])
                    return (pot, ptt, ko_, kp_)

                def emit_PV(pair, pot, ptt, ko_, kp_):
                    for b in range(2):
                        tau = 2 * pair + b
                        r, n = divmod(tau, nb)
                        oreg = pot[:, b * 256:b * 256 + 128]
                        dreg = pot[:, b * 256 + 128:b * 256 + 256]
                        parts = ([(tau - 1, 2 * b)] if n > 0 else []) + [(tau, 2 * b + 1)]
                        for (kt, pslot) in parts:
                            P.op("tensor", lambda e, kt=kt, pslot=pslot, oreg=oreg, ptt=ptt: e.matmul(oreg, lhsT=vg[:, kt, :], rhs=ptt[:, pslot * 128:(pslot + 1) * 128],
                                                                                                    start=False, stop=False, skip_group_check=True),
                                 reads=["vg%d" % kt, kp_, ko_], writes=[ko_])
                            P.op("tensor", lambda e, pslot=pslot, dreg=dreg, ptt=ptt: e.matmul(dreg, lhsT=ones_bf[:, :], rhs=ptt[:, pslot * 128:(pslot + 1) * 128],
                                                                                             start=False, stop=False, skip_group_check=True),
                                 reads=["ones_bf", kp_, ko_], writes=[ko_])
                    for b in range(2):
                        tau = 2 * pair + b
                        r, n = divmod(tau, nb)
                        st0 = n * 128 * dil + r
                        dst = acc[:, :, st0:st0 + 127 * dil + 1:dil]
                        src = pot[:, b * 256:(b + 1) * 256].rearrange("p (a t) -> p a t", t=128)
                        if g == 0:
                            P.op("scalar", lambda e, dst=dst, src=src: e.copy(out=dst, in_=src), reads=[ko_], writes=["acc"])
                        else:
                            P.op("vector", lambda e, dst=dst, src=src: e.tensor_tensor(out=dst, in0=dst, in1=src, op=ALU.add), reads=[ko_, "acc"], writes=["acc"])

                pendA = None
                for pair in range(8):
                    ctx = emit_S(pair)
                    if pendA is not None:
                        emit_PV(*pendA)
                    pendA = (pair,) + ctx
                emit_PV(*pendA)
                nset = 1 if g == 0 else 4
                for kv in range(2):
                    if g == 0:
                        srcc = ckv[0][slot, :, kv * 2048 + h * 128: kv * 2048 + (h + 1) * 128]
                        P.dma("gpsimd", ck[:, 0, kv, :], srcc, writes=["ck%d" % kv], slot="ck%d" % kv)
                    else:
                        srcc = ckv[g][slot, :, :].rearrange("(i t) c -> i t c", t=dil)[:, 0:4, kv * 2048 + h * 128: kv * 2048 + (h + 1) * 128]
                        P.dma("gpsimd", ck[:, :, kv, :], srcc, writes=["ck%d" % kv], slot="ck%d" % kv)
                ai = cnt["att"] % 2
                cnt["att"] += 1
                pst, pot = psc[ai], pod[ai]
                ks_, ko_ = "psc%d" % ai, "pod%d" % ai
                P.op("vector", lambda e, pot=pot: e.memset(pot[:, 0:8], 0.0), writes=[ko_])
                for s_ in range(nset):
                    sl = 0
                    cnt["ptb"] += 1
                    P.op("tensor", lambda e, s_=s_, sl=sl: e.transpose(out=ptb[:, sl * 512:sl * 512 + 128], in_=ck[:, s_, 0, :], identity=ident[:, :]),
                         reads=["ck0", "ident"], writes=["ptb%d" % sl])
                    P.op("scalar", lambda e, sl=sl: e.copy(out=ckT[:, :], in_=ptb[:, sl * 512:sl * 512 + 128]), reads=["ptb%d" % sl], writes=["ckT"])
                    P.op("tensor", lambda e, s_=s_, pst=pst: e.matmul(pst[:, s_ * 4:s_ * 4 + 4], lhsT=ckT[:, :], rhs=qkT[:, 0, 16, 0:4], start=True, stop=True),
                         reads=["ckT", "qkT16"], writes=[ks_])
                    P.op("scalar", lambda e, s_=s_, pst=pst: e.activation(out=PTs[:, s_ * 4:s_ * 4 + 4], in_=pst[:, s_ * 4:s_ * 4 + 4], func=AF.Exp, scale=scale_a),
                         reads=[ks_], writes=["PTs"])
                    mi = 0 if g == 0 else 1 + s_
                    P.op("vector", lambda e, s_=s_, mi=mi: e.tensor_tensor(out=PTs[:, s_ * 4:s_ * 4 + 4], in0=PTs[:, s_ * 4:s_ * 4 + 4], in1=smask[:, mi, :], op=ALU.mult),
                         reads=["PTs", "smask"], writes=["PTs"])
                    P.op("tensor", lambda e, s_=s_, pot=pot: e.matmul(pot[:, 0:4], lhsT=ck[:, s_, 1, :], rhs=PTs[:, s_ * 4:s_ * 4 + 4], start=False, stop=False, skip_group_check=True),
                         reads=["ck1", "PTs", ko_], writes=[ko_])
                    P.op("tensor", lambda e, s_=s_, pot=pot: e.matmul(pot[:, 4:8], lhsT=ones_bf[:, :], rhs=PTs[:, s_ * 4:s_ * 4 + 4], start=False, stop=False, skip_group_check=True),
                         reads=["ones_bf", "PTs", ko_], writes=[ko_])
                P.op("tensor", lambda e, pst=pst: e.matmul(pst[0:4, 16:20], lhsT=qkT[:, 1, 16, 0:4], rhs=qkT[:, 0, 16, 0:4], start=True, stop=True),
                     reads=["qkT16"], writes=[ks_])
                P.op("scalar", lambda e, pst=pst: e.activation(out=PTs[0:4, 16:20], in_=pst[0:4, 16:20], func=AF.Exp, scale=scale_a), reads=[ks_], writes=["PTs2"])
                P.op("vector", lambda e, g=g: e.tensor_tensor(out=PTs[0:4, 16:20], in0=PTs[0:4, 16:20], in1=mnew[0:4, g, :], op=ALU.mult),
                     reads=["PTs2", "mnew"], writes=["PTs2"])
                P.op("tensor", lambda e, pot=pot: e.matmul(pot[:, 0:4], lhsT=vg[0:4, 16, :], rhs=PTs[0:4, 16:20], start=False, stop=False, skip_group_check=True),
                     reads=["vg16", "PTs2", ko_], writes=[ko_])
                P.op("tensor", lambda e, pot=pot: e.matmul(pot[:, 4:8], lhsT=ones_bf[0:4, :], rhs=PTs[0:4, 16:20], start=False, stop=False, skip_group_check=True),
                     reads=["ones_bf", "PTs2", ko_], writes=[ko_])
                dst = acc[:, :, S:S + 4]
                src = pot[:, 0:8].rearrange("p (a t) -> p a t", t=4)
                if g == 0:
                    P.op("scalar", lambda e, dst=dst, src=src: e.copy(out=dst, in_=src), reads=[ko_], writes=["acc"])
                else:
                    P.op("vector", lambda e, dst=dst, src=src: e.tensor_tensor(out=dst, in0=dst, in1=src, op=ALU.add), reads=[ko_, "acc"], writes=["acc"])
            if debug and h == 0:
                P.dma("sync", dbg_acc, acc[:, :, :], reads=["acc"], slot="dbg")
                P.dma("sync", dbg_vg, vg[:, :, :], reads=["vg%d" % i for i in range(17)], slot="dbg")
                P.dma("sync", dbg_qkT.rearrange("p a t d -> p (a t) d"), qkT[:, :, :, :].rearrange("p a t d -> p (a t) d"), reads=["qkT%d" % i for i in range(17)], slot="dbg")
            bi = wi % 3
            wi += 1
            wgt = wbs[bi]
            gk = "wb%d_0" % bi
            P.dma("gpsimd", wgt[:, :, 0:128], wsrc[:, :, 9, h * 128:(h + 1) * 128], writes=[gk], slot=gk)
            for c in range(5):
                c0 = c * 512
                n_ = 512 if c < 4 else 4
                for kc in range(16):
                    P.op("tensor", lambda e, kc=kc, c0=c0, n_=n_, wgt=wgt: e.matmul(pg[:, 0:n_], lhsT=wgt[:, kc, 0:128], rhs=hT[:, kc, c0:c0 + n_],
                                                                                  start=(kc == 0), stop=(kc == 15)),
                         reads=HT_ALL + [gk], writes=["pg"])
                P.op("scalar", lambda e, n_=n_: e.activation(out=sg[:, 0:n_], in_=pg[:, 0:n_], func=AF.Silu), reads=["pg"], writes=["sg"])
                P.op("vector", lambda e, c0=c0, n_=n_: e.reciprocal(out=rc[:, 0:n_], in_=acc[:, 1, c0:c0 + n_]), reads=["acc"], writes=["rc"])
                P.op("vector", lambda e, c0=c0, n_=n_: e.tensor_tensor(out=rc[:, 0:n_], in0=rc[:, 0:n_], in1=acc[:, 0, c0:c0 + n_], op=ALU.mult),
                     reads=["rc", "acc"], writes=["rc"])
                P.op("vector", lambda e, c0=c0, n_=n_, hs=hs: e.tensor_tensor(out=yT[:, hs, c0:c0 + n_], in0=rc[:, 0:n_], in1=sg[:, 0:n_], op=ALU.mult),
                     reads=["rc", "sg"], writes=["yT"])
            if debug and h == 1:
                P.dma("sync", dbg_yT, yT[:, :, :], reads=["yT"], slot="dbg")
            if hs == 1:
                P.dma("gpsimd", wo[:, :, :], w_out_a[slot, (h - 1) * 128:(h + 1) * 128, :].rearrange("(k p) n -> p k n", p=128), writes=["wo"], slot="wo")
                if h == NHA - 1:
                    load_gvec(norm_g[li + 1, :] if li < NL - 1 else final_g)
                wout_pass(li, yT, wo, 2, first=(h == 1), last=(h == NHA - 1), next_kind=next_kind, hT_next=hT_next)

    def views_BC():
        ar_reset()
        V = {}
        V["hTt"] = [av([128, 16, 128], BF16) for _ in range(2)]
        V["hTst"] = av([128, 16, 128], BF16)
        V["wb"] = [av([128, 16, 512], BF16) for _ in range(3)]
        V["wo"] = av([128, 4, DM], BF16)
        V["yT"] = av([128, 4, SS], BF16)
        V["stf"] = [av([128, 2, 512], F32) for _ in range(2)]
        V["stb"] = [av([128, 2, 512], BF16) for _ in range(2)]
        V["cs"] = [av([128, 2, 128], F32) for _ in range(2)]
        V["qkT"] = av([128, 4, 128], BF16)
        V["vb"] = [av([128, 512], BF16) for _ in range(2)]
        V["scT"] = av([128, 128], BF16)
        V["of"] = av([128, 512], F32)
        V["gf"] = [av([128, 512], F32) for _ in range(2)]
        V["gnv"] = av([128, 512], F32)
        V["yb"] = [av([128, 512], BF16) for _ in range(2)]
        V["qb"] = [av([128, 768], BF16) for _ in range(2)]
        V["bst"] = av([128, 4, 6], F32)
        V["mv"] = av([128, 8], F32)
        V["wlr"] = av([128, 16, 16], BF16)
        V["lrT"] = av([32, 128], F32)
        V["wg2"] = av([17, 1024], F32)
        V["sp"] = av([128, 256], F32)
        V["eb"] = av([128, 2, 256], F32)
        V["ebl"] = av([128, 2], F32)
        return V

    def run_layer_BC(li, kind, V, next_kind, hT_next):
        NH = 8 if kind == 1 else 4
        w_in = w_in_b if kind == 1 else w_in_c
        w_out = w_out_b if kind == 1 else w_out_c
        gn = gn_b if kind == 1 else gn_c
        wsrc = w_in.rearrange("(kc p) n -> p kc n", p=128)
        hTt, wbs, wo, yT = V["hTt"], V["wb"], V["wo"], V["yT"]
        stf, stbf, cs, qkT, vbs, scT = V["stf"], V["stb"], V["cs"], V["qkT"], V["vb"], V["scT"]
        qbs = V["qb"]
        of, gfs, gnv, ybs, bst, mv = V["of"], V["gf"], V["gnv"], V["yb"], V["bst"], V["mv"]
        wlr, lrT, wg2, sp, eb, ebl = V["wlr"], V["lrT"], V["wg2"], V["sp"], V["eb"], V["ebl"]
        lgam = [math.log(1.0 - 2.0 ** (-5.0 - hh)) for hh in range(8)]
        if kind == 2:
            P.dma("gpsimd", wlr[:, :, :], wsrc[:, :, 6144:6160], writes=["wlr"], slot="wlr")
            P.dma("sync", wg2[0:16, :], w_gate2, writes=["wg2a"], slot="wg2")
            P.dma("sync", wg2[16:17, :], b_gate, writes=["wg2b"], slot="wg2")
        hcnt = 0
        for h in range(NH):
            if kind == 1:
                cols = [(h * 256, 256, 0), (2048 + h * 256, 256, 256)]
                vcol = 4096 + h * 512
                gcol = 8192 + h * 512
            else:
                cols = [(h * 256, 256, 0), (1024 + h * 256, 256, 256)]
                vcol = 2048 + h * 512
                gcol = 4096 + h * 512
            for (c0, n_, d0) in cols:
                for kq in range(2):
                    P.dma("gpsimd", wbs[0][:, kq * 8:(kq + 1) * 8, d0:d0 + n_], wsrc[:, kq * 8:(kq + 1) * 8, c0:c0 + n_], writes=["wq%d_%d" % (d0, kq)], slot="wq%d_%d" % (d0, kq))
            for kq in range(2):
                P.dma("gpsimd", wbs[1][:, kq * 8:(kq + 1) * 8, :], wsrc[:, kq * 8:(kq + 1) * 8, vcol:vcol + 512], writes=["wv_%d" % kq], slot="wv_%d" % kq)
                P.dma("gpsimd", wbs[2][:, kq * 8:(kq + 1) * 8, :], wsrc[:, kq * 8:(kq + 1) * 8, gcol:gcol + 512], writes=["wg_%d" % kq], slot="wg_%d" % kq)
            WQK = ["wq0_0", "wq0_1", "wq256_0", "wq256_1"]
            P.dma("sync", gnv[:, :], gn[h, :].partition_broadcast(128), writes=["gnv"], slot="gnv")
            P.op("gpsimd", lambda e: e.memset(stf[0][:, :, :], 0.0), writes=["stf0"])
            P.op("gpsimd", lambda e: e.memset(stbf[0][:, :, :], 0.0), writes=["stb0"])
            st_in = stb_in if kind == 1 else stc_in
            P.dma("sync", stf[1][:, :, :], st_in[h].rearrange("(c p) v -> p c v", p=128), writes=["stf1"], slot="stin")
            P.dma("gpsimd", stbf[1][:, :, :], st_in[h].rearrange("(c p) v -> p c v", p=128), writes=["stb1"], slot="stin2")
            def f_proj(t):
                    nt = 128 if t < 16 else 4
                    si = 0 if t < 16 else 1
                    hi = t % 2
                    par = t % 2
                    ht = hTt[hi]
                    hk = "hTt%d" % hi
                    vb = vbs[par]; gf = gfs[par]; yb = ybs[par]; qb = qbs[par]
                    kvb = "vb%d" % par; kgf = "gf%d" % par; kyb = "yb%d" % par; kqb = "qbB%d" % par
                    gC = math.exp(lgam[h] * (128.0 if t < 16 else 4.0)) if kind == 1 else 1.0

                    P.dma("sync", ht[:, :, 0:nt], hTd[t, :, :, 0:nt], reads=["hTd%d" % t], writes=[hk], slot=hk)
                    if kind == 1:
                        P.dma("sync", cs[hi][0:nt, 0, :], cd["cosB"][t, 0:nt, :], writes=["cs%da" % hi], slot="cs%da" % hi)
                        P.dma("sync", cs[hi][0:nt, 1, :], cd["sinB"][t, 0:nt, :], writes=["cs%db" % hi], slot="cs%db" % hi)
                    for kc in range(16):
                        P.op("tensor", lambda e, kc=kc, ht=ht, nt=nt: e.matmul(pj[0][0:nt, :], lhsT=ht[:, kc, 0:nt], rhs=wbs[0][:, kc, :], start=(kc == 0), stop=(kc == 15)),
                             reads=[hk] + WQK, writes=["pj0"])
                    for kc in range(16):
                        P.op("tensor", lambda e, kc=kc, ht=ht, nt=nt: e.matmul(pj[1][0:nt, :], lhsT=ht[:, kc, 0:nt], rhs=wbs[1][:, kc, :], start=(kc == 0), stop=(kc == 15)),
                             reads=[hk, "wv_0", "wv_1"], writes=["pj1"])
                    for kc in range(16):
                        P.op("tensor", lambda e, kc=kc, ht=ht, nt=nt: e.matmul(pg[0:nt, :], lhsT=ht[:, kc, 0:nt], rhs=wbs[2][:, kc, :], start=(kc == 0), stop=(kc == 15)),
                             reads=[hk, "wg_0", "wg_1"], writes=["pg"])

            def f_evac(t):
                    nt = 128 if t < 16 else 4
                    si = 0 if t < 16 else 1
                    hi = t % 2
                    par = t % 2
                    ht = hTt[hi]
                    hk = "hTt%d" % hi
                    vb = vbs[par]; gf = gfs[par]; yb = ybs[par]; qb = qbs[par]
                    kvb = "vb%d" % par; kgf = "gf%d" % par; kyb = "yb%d" % par; kqb = "qbB%d" % par
                    gC = math.exp(lgam[h] * (128.0 if t < 16 else 4.0)) if kind == 1 else 1.0

                    P.op("vector", lambda e, nt=nt: e.tensor_copy(out=vb[0:nt, :], in_=pj[1][0:nt, :]), reads=["pj1"], writes=[kvb])
                    P.op("scalar", lambda e, nt=nt: e.activation(out=gf[0:nt, :], in_=pg[0:nt, :], func=AF.Silu), reads=["pg"], writes=[kgf])
                    if kind == 1:
                        qf = qkf[0]
                        P.op("scalar", lambda e, nt=nt, qf=qf: e.copy(out=qf[0:nt, :], in_=pj[0][0:nt, :]), reads=["pj0"], writes=["qkf0"])
                        q4 = qf[0:nt, :].rearrange("p (a b d) -> p a b d", a=2, b=2)
                        x1 = q4[:, :, 0, :]
                        x2 = q4[:, :, 1, :]
                        cs_ = cs[hi][0:nt, 0:1, :].broadcast_to([nt, 2, 128])
                        sn_ = cs[hi][0:nt, 1:2, :].broadcast_to([nt, 2, 128])
                        tv = [rt[0:nt, i, :].rearrange("p (a d) -> p a d", d=128) for i in range(6)]
                        ck_ = "cs%d" % hi
                        P.op("gpsimd", lambda e, tv=tv, x1=x1, cs_=cs_: e.tensor_tensor(out=tv[0], in0=x1, in1=cs_, op=ALU.mult), reads=["qkf0", ck_ + "a", ck_ + "b"], writes=["rt0"])
                        P.op("gpsimd", lambda e, tv=tv, x2=x2, sn_=sn_: e.tensor_tensor(out=tv[1], in0=x2, in1=sn_, op=ALU.mult), reads=["qkf0", ck_ + "a", ck_ + "b"], writes=["rt1"])
                        P.op("gpsimd", lambda e, tv=tv, x1=x1, sn_=sn_: e.tensor_tensor(out=tv[2], in0=x1, in1=sn_, op=ALU.mult), reads=["qkf0", ck_ + "a", ck_ + "b"], writes=["rt2"])
                        P.op("gpsimd", lambda e, tv=tv, x2=x2, cs_=cs_: e.tensor_tensor(out=tv[3], in0=x2, in1=cs_, op=ALU.mult), reads=["qkf0", ck_ + "a", ck_ + "b"], writes=["rt3"])
                        P.op("gpsimd", lambda e, tv=tv: e.tensor_tensor(out=tv[4], in0=tv[0], in1=tv[1], op=ALU.subtract), reads=["rt0", "rt1"], writes=["rt4"])
                        P.op("gpsimd", lambda e, tv=tv: e.tensor_tensor(out=tv[5], in0=tv[2], in1=tv[3], op=ALU.add), reads=["rt2", "rt3"], writes=["rt5"])
                        for (a, dsti, di) in ((0, 0, 0), (1, 1, 1), (1, 2, 2)):
                            for hf in range(2):
                                P.op("vector", lambda e, a=a, dsti=dsti, di=di, hf=hf, tv=tv, nt=nt, si=si, h=h: e.tensor_scalar(
                                    out=qb[0:nt, dsti * 256 + hf * 128: dsti * 256 + (hf + 1) * 128], in0=tv[4 + hf][:, a, :],
                                    scalar1=decB[0:nt, si, h, di:di + 1], scalar2=None, op0=ALU.mult),
                                    reads=["rt4", "rt5", "decB"], writes=[kqb])
                    else:
                        for kc in range(16):
                            P.op("tensor", lambda e, kc=kc, ht=ht, nt=nt: e.matmul(psc[1][0:16, 0:nt], lhsT=wlr[:, kc, :], rhs=ht[:, kc, 0:nt], start=(kc == 0), stop=(kc == 15)),
                                 reads=[hk, "wlr"], writes=["psc1"])
                        P.op("vector", lambda e: e.memset(lrT[:, :], 1.0), writes=["lrT"])
                        P.op("vector", lambda e, nt=nt: e.tensor_copy(out=lrT[0:16, 0:nt], in_=psc[1][0:16, 0:nt]), reads=["psc1", "lrT"], writes=["lrT"])
                        P.op("tensor", lambda e, nt=nt, h=h: e.matmul(pod[1][0:nt, 0:256], lhsT=lrT[0:17, 0:nt], rhs=wg2[0:17, h * 256:(h + 1) * 256], start=True, stop=True),
                             reads=["lrT", "wg2a", "wg2b"], writes=["pod1"])
                        P.op("scalar", lambda e, nt=nt: e.activation(out=sp[0:nt, :], in_=pod[1][0:nt, 0:256], func=AF.Exp, scale=-1.0), reads=["pod1"], writes=["sp"])
                        P.op("scalar", lambda e, nt=nt: e.activation(out=sp[0:nt, :], in_=sp[0:nt, :], func=AF.Ln, bias=1.0), reads=["sp"], writes=["sp"])
                        P.op("tensor", lambda e, nt=nt: e.matmul(pod[1][0:nt, 256:512], lhsT=triU[0:nt, 0:nt], rhs=sp[0:nt, :], start=True, stop=True),
                             reads=["sp", "triU", "pod1"], writes=["pod1"])
                        P.op("scalar", lambda e, nt=nt: e.activation(out=eb[0:nt, 0, :], in_=pod[1][0:nt, 256:512], func=AF.Exp, scale=-1.0 / 16.0), reads=["pod1"], writes=["eb"])
                        P.op("scalar", lambda e, nt=nt: e.activation(out=eb[0:nt, 1, :], in_=pod[1][0:nt, 256:512], func=AF.Exp, scale=1.0 / 16.0), reads=["pod1"], writes=["eb"])
                        for ch in range(2):
                            P.op("tensor", lambda e, nt=nt, ch=ch: e.matmul(psc[1][:, 128 + ch:129 + ch], lhsT=sp[0:nt, ch * 128:(ch + 1) * 128], rhs=ones_f[0:nt, 0:1], start=True, stop=True),
                                 reads=["sp", "ones_f", "psc1"], writes=["psc1"])
                        P.op("scalar", lambda e: e.activation(out=ebl[:, 0:2], in_=psc[1][:, 128:130], func=AF.Exp, scale=-1.0 / 16.0), reads=["psc1"], writes=["ebl"])
                        P.op("vector", lambda e, nt=nt: e.scalar_tensor_tensor(out=qb[0:nt, 0:256], in0=pj[0][0:nt, 0:256], scalar=1.0 / 16.0, in1=eb[0:nt, 0, :], op0=ALU.mult, op1=ALU.mult),
                             reads=["pj0", "eb"], writes=[kqb])
                        P.op("vector", lambda e, nt=nt: e.tensor_tensor(out=qb[0:nt, 256:512], in0=pj[0][0:nt, 256:512], in1=eb[0:nt, 1, :], op=ALU.mult),
                             reads=["pj0", "eb"], writes=[kqb])

            def f_rest(t):
                    nt = 128 if t < 16 else 4
                    si = 0 if t < 16 else 1
                    hi = t % 2
                    par = t % 2
                    ht = hTt[hi]
                    hk = "hTt%d" % hi
                    vb = vbs[par]; gf = gfs[par]; yb = ybs[par]; qb = qbs[par]
                    kvb = "vb%d" % par; kgf = "gf%d" % par; kyb = "yb%d" % par; kqb = "qbB%d" % par
                    gC = math.exp(lgam[h] * (128.0 if t < 16 else 4.0)) if kind == 1 else 1.0

                    kd0 = 512 if kind == 1 else 256
                    sl = 0
                    cnt["ptb"] += 1
                    for j in range(4):
                        P.op("tensor", lambda e, j=j, sl=sl, nt=nt: e.transpose(out=ptb[:, sl * 512 + j * 128: sl * 512 + j * 128 + max(nt, 32)], in_=qb[0:max(nt, 32), j * 128:(j + 1) * 128], identity=ident[0:max(nt, 32), 0:max(nt, 32)]),
                             reads=[kqb, "ident"], writes=["ptb%d" % sl])
                    srcT = ptb[:, sl * 512:(sl + 1) * 512].rearrange("p (a t) -> p a t", t=128)[:, :, 0:nt]
                    P.op("scalar", lambda e, srcT=srcT, nt=nt: e.copy(out=qkT[:, :, 0:nt], in_=srcT), reads=["ptb%d" % sl], writes=["qkT"])
                    for ch in range(2):
                        P.op("tensor", lambda e, ch=ch, nt=nt: e.matmul(psc[0][0:nt, 0:nt], lhsT=qkT[:, 2 + ch, 0:nt], rhs=qkT[:, ch, 0:nt], start=(ch == 0), stop=(ch == 1)),
                             reads=["qkT"], writes=["psc0"])
                    P.op("vector", lambda e, nt=nt: e.tensor_tensor(out=scT[0:nt, 0:nt], in0=psc[0][0:nt, 0:nt], in1=maskBf[0:nt, 0:nt], op=ALU.mult),
                         reads=["psc0", "maskBf"], writes=["scT"])
                    P.op("tensor", lambda e, nt=nt: e.matmul(pod[0][0:nt, :], lhsT=scT[0:nt, 0:nt], rhs=vb[0:nt, :], start=True, stop=False),
                         reads=["scT", kvb], writes=["pod0"])
                    for ch in range(2):
                        P.op("tensor", lambda e, ch=ch, nt=nt, si=si: e.matmul(pod[0][0:nt, :], lhsT=qkT[:, ch, 0:nt], rhs=stbf[si][:, ch, :], start=False, stop=(ch == 1)),
                             reads=["qkT", "stb%d" % si], writes=["pod0"])
                    for ch in range(2):
                        pstt = psc[1] if ch == 0 else pod[1]
                        pk = "psc1" if ch == 0 else "pod1"
                        P.op("tensor", lambda e, ch=ch, nt=nt, pstt=pstt: e.matmul(pstt[:, :], lhsT=qb[0:nt, kd0 + ch * 128: kd0 + (ch + 1) * 128], rhs=vb[0:nt, :], start=True, stop=True),
                             reads=[kqb, kvb, pk], writes=[pk])
                        if kind == 1:
                            P.op("vector", lambda e, ch=ch, si=si, pstt=pstt, gC=gC: e.scalar_tensor_tensor(out=stf[si][:, ch, :], in0=stf[si][:, ch, :], scalar=gC, in1=pstt[:, :], op0=ALU.mult, op1=ALU.add),
                                 reads=["stf%d" % si, pk], writes=["stf%d" % si])
                        else:
                            P.op("vector", lambda e, ch=ch, si=si, pstt=pstt: e.tensor_tensor(out=stf[si][:, ch, :], in0=stf[si][:, ch, :], in1=pstt[:, :], op=ALU.add),
                                 reads=["stf%d" % si, pk], writes=["stf%d" % si])
                            P.op("vector", lambda e, ch=ch, si=si: e.tensor_scalar(out=stf[si][:, ch, :], in0=stf[si][:, ch, :], scalar1=ebl[:, ch:ch + 1], scalar2=None, op0=ALU.mult),
                                 reads=["stf%d" % si, "ebl"], writes=["stf%d" % si])
                        P.op("scalar", lambda e, ch=ch, si=si: e.copy(out=stbf[si][:, ch, :], in_=stf[si][:, ch, :]), reads=["stf%d" % si], writes=["stb%d" % si])
                    if kind == 1:
                        for c4 in range(4):
                            P.op("vector", lambda e, c4=c4, nt=nt: e.bn_stats(out=bst[0:nt, c4, :], in_=pod[0][0:nt, c4 * 128:(c4 + 1) * 128]), reads=["pod0"], writes=["bst"])
                        P.op("vector", lambda e, nt=nt: e.bn_aggr(out=mv[0:nt, 0:2], in_=bst[0:nt, :, :]), reads=["bst"], writes=["mv"])
                        P.op("scalar", lambda e, nt=nt: e.activation(out=mv[0:nt, 2:3], in_=mv[0:nt, 1:2], func=AF.Sqrt, bias=EPS), reads=["mv"], writes=["mv"])
                        P.op("vector", lambda e, nt=nt: e.reciprocal(out=mv[0:nt, 3:4], in_=mv[0:nt, 2:3]), reads=["mv"], writes=["mv"])
                        P.op("vector", lambda e, nt=nt: e.tensor_scalar(out=mv[0:nt, 4:5], in0=mv[0:nt, 0:1], scalar1=mv[0:nt, 3:4], scalar2=-1.0, op0=ALU.mult, op1=ALU.mult),
                             reads=["mv"], writes=["mv"])
                        P.op("scalar", lambda e, nt=nt: e.activation(out=of[0:nt, :], in_=pod[0][0:nt, :], func=AF.Identity, scale=mv[0:nt, 3:4], bias=mv[0:nt, 4:5]),
                             reads=["pod0", "mv"], writes=["of"])
                    else:
                        P.op("scalar", lambda e, nt=nt: e.activation(out=of[0:nt, :], in_=pod[0][0:nt, :], func=AF.Square, accum_out=mv[0:nt, 0:1]), reads=["pod0"], writes=["of", "mv"])
                        P.op("scalar", lambda e, nt=nt: e.activation(out=mv[0:nt, 2:3], in_=mv[0:nt, 0:1], func=AF.Sqrt, scale=1.0 / 512.0, bias=EPS), reads=["mv"], writes=["mv"])
                        P.op("vector", lambda e, nt=nt: e.reciprocal(out=mv[0:nt, 3:4], in_=mv[0:nt, 2:3]), reads=["mv"], writes=["mv"])
                        P.op("scalar", lambda e, nt=nt: e.activation(out=of[0:nt, :], in_=pod[0][0:nt, :], func=AF.Identity, scale=mv[0:nt, 3:4]), reads=["pod0", "mv", "of"], writes=["of"])
                    P.op("gpsimd", lambda e, nt=nt: e.tensor_tensor(out=of[0:nt, :], in0=of[0:nt, :], in1=gnv[0:nt, :], op=ALU.mult), reads=["of", "gnv"], writes=["of"])
                    P.op("vector", lambda e, nt=nt: e.tensor_tensor(out=yb[0:nt, :], in0=of[0:nt, :], in1=gf[0:nt, :], op=ALU.mult), reads=["of", kgf], writes=[kyb])

            def f_yt(t):
                    nt = 128 if t < 16 else 4
                    si = 0 if t < 16 else 1
                    hi = t % 2
                    par = t % 2
                    ht = hTt[hi]
                    hk = "hTt%d" % hi
                    vb = vbs[par]; gf = gfs[par]; yb = ybs[par]; qb = qbs[par]
                    kvb = "vb%d" % par; kgf = "gf%d" % par; kyb = "yb%d" % par; kqb = "qbB%d" % par
                    gC = math.exp(lgam[h] * (128.0 if t < 16 else 4.0)) if kind == 1 else 1.0

                    sl = 0
                    cnt["ptb"] += 1
                    for j in range(4):
                        P.op("tensor", lambda e, j=j, sl=sl, nt=nt: e.transpose(out=ptb[:, sl * 512 + j * 128: sl * 512 + j * 128 + max(nt, 32)], in_=yb[0:max(nt, 32), j * 128:(j + 1) * 128], identity=ident[0:max(nt, 32), 0:max(nt, 32)]),
                             reads=[kyb, "ident"], writes=["ptb%d" % sl])
                    srcT = ptb[:, sl * 512:(sl + 1) * 512].rearrange("p (a t) -> p a t", t=128)[:, :, 0:nt]
                    P.op("scalar", lambda e, srcT=srcT, nt=nt, t=t: e.copy(out=yT[:, :, t * 128:t * 128 + nt], in_=srcT), reads=["ptb%d" % sl], writes=["yT"])
            f_proj(0)
            f_evac(0)
            for t in range(17):
                if t + 1 < 17:
                    f_proj(t + 1)
                if t >= 1:
                    f_yt(t - 1)
                f_rest(t)
                if t + 1 < 17:
                    f_evac(t + 1)
            f_yt(16)

            po = p_stb if kind == 1 else p_stc
            so = s_stb if kind == 1 else s_stc
            P.dma("sync", po[h].rearrange("(c p) v -> p c v", p=128), stf[0][:, :, :], reads=["stf0"], slot="sto")
            P.dma("sync", so[h].rearrange("(c p) v -> p c v", p=128), stf[1][:, :, :], reads=["stf1"], slot="sto")
            P.dma("gpsimd", wo[:, :, :], w_out[h * 512:(h + 1) * 512, :].rearrange("(k p) n -> p k n", p=128), writes=["wo"], slot="wo")
            if h == NH - 1:
                load_gvec(norm_g[li + 1, :] if li < NL - 1 else final_g)
            wout_pass(li, yT, wo, 4, first=(h == 0), last=(h == NH - 1), next_kind=next_kind, hT_next=hT_next)

    kinds = [i % 3 for i in range(NL)]
    VA = layer_A(0, 0)
    load_gvec(norm_g[0, :])
    for t in range(17):
        nt = 128 if t < 16 else 4
        src = xp[t * 128:t * 128 + nt, :] if t < 16 else xs[:, :]
        P.dma("sync", xts[t % 2][0:nt, :], src, writes=["xt%d" % (t % 2)], slot="xt%d" % (t % 2))
        rms_to(nt, "gvec", hn[0:nt, :], "hn", xts[t % 2], "xt%d" % (t % 2), extra_reads=["hn"])
        produce_h(t, nt, VA["hT"], to_dram=False)
    for li in range(NL):
        kind = kinds[li]
        print('MARK layer', li, 'starts at op', len(P.ops), flush=True)
        nk_ = kinds[li + 1] if li + 1 < NL else -1
        if kind == 0:
            run_layer_A(li, li // 3, VA, nk_, VA["hst"] if nk_ in (1, 2) else VA["hT"])
            P.barrier()
        else:
            VB = views_BC()
            if nk_ == 0:
                hT_next = VB["hTst"]
                run_layer_BC(li, kind, VB, 1, hT_next)
                P.barrier()
                VA = layer_A(li + 1, (li + 1) // 3)
                for t in range(17):
                    nt = 128 if t < 16 else 4
                    P.dma("sync", VA["hT"][:, :, t * 128:t * 128 + nt], hTd[t, :, :, 0:nt], reads=["hTd%d" % t], writes=["hT%d" % t], slot="hTl")
            else:
                run_layer_BC(li, kind, VB, nk_, VB["hTst"])
                P.barrier()
    P.emit()
    for g in reversed(ctxs):
        g.__exit__(None, None, None)
    return nc, hc, P.stats


_CACHE = {}


def kernel(**inp):
    if "nc" not in _CACHE:
        _CACHE["nc"] = build()
    nc, hc, stats = _CACHE["nc"]
    f = lambda a: np.ascontiguousarray(np.asarray(a, dtype=np.float32))
    in_maps = []
    shared = dict(
        norm_g=f(inp["norm_g"]), final_g=f(inp["final_g"]), w_in_a=f(inp["w_in_a"]), w_out_a=f(inp["w_out_a"]),
        w_in_b=f(inp["w_in_b"][0]), gn_b=f(inp["gn_b"][0]), w_out_b=f(inp["w_out_b"][0]), w_in_c=f(inp["w_in_c"][0]),
        w_gate2=f(inp["w_gate2_c"][0]), b_gate=f(inp["b_gate_c"][0]).reshape(1, 1024), gn_c=f(inp["gn_c"][0]), w_out_c=f(inp["w_out_c"][0]))
    for kk, v in hc.items():
        shared["c_" + kk] = v
    for c in range(8):
        m = dict(shared)
        m["xp"] = f(inp["x_prompt"][c % 4])
        m["xs"] = f(inp["x_sample"][c])
        for g, nm in enumerate(("cache_a_kv1", "cache_a_kv2", "cache_a_kv3")):
            m["ckv%d" % g] = f(inp[nm][:, c]).reshape(2, LBUF[g], 4096)
        m["stb"] = f(inp["state_b"][0, c])
        m["stc"] = f(inp["state_c"][0, c])
        in_maps.append(m)
    res = run_bass_kernel_spmd(nc, in_maps, core_ids=list(range(8)))
    R = res.results
    y_prompt = np.stack([R[b]["y_p"] for b in range(4)])
    y_sample = np.stack([R[c]["y_s"] for c in range(8)])
    pkv = []
    for g in range(3):
        keep = min(LBUF[g], S)
        a = np.stack([R[b]["p_kv%d" % g] for b in range(4)], axis=1)
        pkv.append(a.reshape(2, 4, keep, 2, 16, 128))
    p_stb = np.stack([R[b]["p_stb"] for b in range(4)])[None]
    p_stc = np.stack([R[b]["p_stc"] for b in range(4)])[None]
    skv = np.stack([R[c]["s_kv"] for c in range(8)], axis=0)
    s_list = [np.ascontiguousarray(skv[:, :, g].transpose(1, 0, 2, 3)).reshape(2, 8, 4, 2, 16, 128) for g in range(3)]
    s_stb = np.stack([R[c]["s_stb"] for c in range(8)])[None]
    s_stc = np.stack([R[c]["s_stc"] for c in range(8)])[None]
    outs = (y_prompt, y_sample, pkv[0], pkv[1], pkv[2], p_stb, p_stc, s_list[0], s_list[1], s_list[2], s_stb, s_stc)
    return tuple(np.ascontiguousarray(o, dtype=np.float32) for o in outs)
```

```python
import math
import numpy as np
import ml_dtypes
import concourse.bass as bass
import concourse.mybir as mybir
from concourse.bass_utils import run_bass_kernel_spmd

F32 = mybir.dt.float32
BF16 = mybir.dt.bfloat16
ALU = mybir.AluOpType
AF = mybir.ActivationFunctionType
AX = mybir.AxisListType

ENGS = ("sync", "scalar", "vector", "gpsimd", "tensor")
EPOCH = 12000
S = 2048
DM = 2048
SS = 2052
PAST = 16384
EPS = 1e-6
DILS = (1, 4, 16)
LBUF = (128, 512, 2048)
NL_DEFAULT = 4


PSUM_KEYS = {"pj0", "pj1", "ptb0", "psc0", "psc1", "pod0", "pod1", "pg"}


class Prog:
    def __init__(self, nc):
        self.nc = nc
        self.ops = []

    def op(self, eng, fn, reads=(), writes=(), dma=None):
        self.ops.append(dict(eng=eng, fn=fn, reads=tuple(reads), writes=tuple(writes), dma=dma, bar=False))

    def dma(self, eng, out, in_, reads=(), writes=(), slot="d", **kw):
        self.op(eng, lambda e: e.dma_start(out=out, in_=in_, **kw), reads, writes, dma=slot)

    def barrier(self):
        self.ops.append(dict(bar=True))

    def emit(self):
        nc = self.nc
        import os
        allops = self.ops[:int(os.environ.get('MAXOPS', '100000000'))]
        ops = []
        last_w = {}
        readers = {}
        dma_count = {}
        last_eng = {}
        last_dma = {}
        pending_bar = {}
        for o in allops:
            if o["bar"]:
                deps = set(last_eng.values()) | set(last_dma.values())
                for e in ENGS:
                    pending_bar[e] = set(deps)
                last_w.clear()
                readers.clear()
                continue
            j = len(ops)
            ops.append(o)
            deps = set()
            for r in o["reads"]:
                if r in last_w:
                    deps.add(last_w[r])
                if r in PSUM_KEYS:
                    for rd in readers.get(r, ()):
                        if ops[rd]["eng"] != o["eng"]:
                            deps.add(rd)
            for w in o["writes"]:
                if w in last_w:
                    deps.add(last_w[w])
                for rd in readers.get(w, ()):
                    if ops[rd]["eng"] == o["eng"] and ops[rd]["dma"] is None and o["dma"] is None:
                        continue
                    deps.add(rd)
            if o["eng"] in pending_bar:
                deps |= pending_bar.pop(o["eng"])
            fin = set()
            for i in deps:
                if i == j:
                    continue
                if ops[i]["dma"] is None and o["dma"] is None and ops[i]["eng"] == "tensor" and o["eng"] == "tensor":
                    continue
                fin.add(i)
            o["deps"] = fin
            o["dma_snap"] = dict(dma_count)
            if o["dma"] is not None:
                dma_count[o["dma"]] = dma_count.get(o["dma"], 0) + 1
                o["dma_idx"] = dma_count[o["dma"]]
                last_dma[o["dma"]] = j
            else:
                last_eng[o["eng"]] = j
            for r in o["reads"]:
                readers.setdefault(r, []).append(j)
            for w in o["writes"]:
                last_w[w] = j
                readers[w] = []
        for o in ops:
            o["ms"] = False
        for o in ops:
            for i in o["deps"]:
                if ops[i]["dma"] is None:
                    ops[i]["ms"] = True
        cnt = {e: 0 for e in ENGS}
        for o in ops:
            if o["ms"]:
                cnt[o["eng"]] += 1
                o["ms_idx"] = cnt[o["eng"]]
        sems = {}
        stack = []

        def get_sem(key):
            if key not in sems:
                g = nc.semaphore("s%d" % len(sems))
                sems[key] = g.__enter__()
                stack.append(g)
            return sems[key]

        seen = {e: {} for e in ENGS}
        for o in ops:
            waits = {}
            for i in o["deps"]:
                d = ops[i]
                if d["dma"] is not None:
                    n = max(o["dma_snap"].get(d["dma"], 0), d["dma_idx"])
                    ep, v = divmod(n - 1, EPOCH // 16)
                    key = ("dma", d["dma"], ep)
                    waits[key] = max(waits.get(key, 0), (v + 1) * 16)
                else:
                    ep, v = divmod(d["ms_idx"] - 1, EPOCH)
                    key = ("eng", d["eng"], ep)
                    waits[key] = max(waits.get(key, 0), v + 1)
            wl = []
            sn = seen[o["eng"]]
            for key, v in waits.items():
                if sn.get(key, 0) >= v:
                    continue
                if key[0] == "eng" and any(k2[0] == "eng" and k2[1] == key[1] and k2[2] > key[2] for k2 in sn):
                    continue
                sn[key] = v
                wl.append((key, v))
            o["waits"] = wl
        for o in ops:
            for key, v in o["waits"]:
                get_sem(key)
            if o["ms"]:
                ep, v = divmod(o["ms_idx"] - 1, EPOCH)
                o["inc"] = (get_sem(("eng", o["eng"], ep)), 1)
            if o["dma"] is not None:
                ep, v = divmod(o["dma_idx"] - 1, EPOCH // 16)
                o["inc"] = (get_sem(("dma", o["dma"], ep)), 16)
        if os.environ.get('DUMPW'):
            a_, b_ = [int(v_) for v_ in os.environ['DUMPW'].split(':')]
            for i_ in range(a_, min(b_, len(ops))):
                o_ = ops[i_]
                print('OP', i_, o_['eng'], o_['dma'], 'deps', sorted(o_['deps'])[-6:], 'waits', o_['waits'], 'ms', o_.get('ms_idx') if o_['ms'] else None, 'dmaidx', o_.get('dma_idx'), flush=True)
        final = []
        for slot, n in dma_count.items():
            ep, v = divmod(n - 1, EPOCH // 16)
            final.append((get_sem(("dma", slot, ep)), (v + 1) * 16))

        with nc.Block() as block:
            def mk(engname):
                def body(e):
                    for o in ops:
                        if o["eng"] != engname:
                            continue
                        for key, v in o["waits"]:
                            e.wait_ge(sems[key], v)
                        ins = o["fn"](e)
                        if "inc" in o:
                            ins.then_inc(o["inc"][0], o["inc"][1])
                    if engname == "sync":
                        for s_, v in final:
                            e.wait_ge(s_, v)
                return body
            block.sync(mk("sync"))
            block.scalar(mk("scalar"))
            block.vector(mk("vector"))
            block.gpsimd(mk("gpsimd"))
            block.tensor(mk("tensor"))
        for g in reversed(stack):
            g.__exit__(None, None, None)
        self.stats = dict(n_ops=len(ops), ms=cnt, nsem=len(sems))


def host_consts():
    c = {}
    bf = ml_dtypes.bfloat16
    c["ident"] = np.eye(128, dtype=np.float32).astype(bf)
    k = np.arange(128)[:, None]
    q = np.arange(128)[None, :]
    A = (k >= q).astype(np.float32)
    B = (k <= q).astype(np.float32)
    c["maskAB"] = np.concatenate([A, B, A, B], axis=1).astype(bf)
    c["maskBf"] = B.astype(np.float32)
    c["ones_bf"] = np.ones((128, 128), np.float32).astype(bf)
    c["ones_f"] = np.ones((128, 8), np.float32)
    half = 16
    inv = (500000.0 ** (-np.arange(half, dtype=np.float32) / half)).astype(np.float32)
    cosA = np.zeros((128, 3, 17, 16), np.float32)
    sinA = np.zeros((128, 3, 17, 16), np.float32)
    for g, dil in enumerate(DILS):
        nb = 16 // dil
        for tau in range(16):
            r, n = divmod(tau, nb)
            pos = ((128 * n + np.arange(128)) * dil + r).astype(np.float32)
            ang = pos[:, None] * inv[None, :]
            cosA[:, g, tau] = np.cos(ang)
            sinA[:, g, tau] = np.sin(ang)
        pos = (PAST + np.arange(4)).astype(np.float32)
        ang = pos[:, None] * inv[None, :]
        cosA[0:4, g, 16] = np.cos(ang)
        sinA[0:4, g, 16] = np.sin(ang)
    c["cosA"] = cosA
    c["sinA"] = sinA
    invb = (10000.0 ** (-np.arange(128, dtype=np.float32) / 128)).astype(np.float32)
    cosB = np.zeros((17, 128, 128), np.float32)
    sinB = np.zeros((17, 128, 128), np.float32)
    for t in range(16):
        pos = (128 * t + np.arange(128)).astype(np.float32)
        ang = pos[:, None] * invb[None, :]
        cosB[t] = np.cos(ang)
        sinB[t] = np.sin(ang)
    pos = (PAST + np.arange(4)).astype(np.float32)
    ang = pos[:, None] * invb[None, :]
    cosB[16, 0:4] = np.cos(ang)
    sinB[16, 0:4] = np.sin(ang)
    c["cosB"] = cosB
    c["sinB"] = sinB
    lg = np.log(1.0 - 2.0 ** (-5.0 - np.arange(8, dtype=np.float64)))
    dec = np.zeros((128, 2, 8, 3), np.float64)
    i = np.arange(128, dtype=np.float64)[:, None]
    dec[:, 0, :, 0] = np.exp(lg[None, :] * (i + 1.0))
    dec[:, 0, :, 1] = np.exp(-lg[None, :] * (i + 1.0)) / 16.0
    dec[:, 0, :, 2] = np.exp(lg[None, :] * (127.0 - i)) / 16.0
    dec[:, 1, :, 0] = np.exp(lg[None, :] * (i + 1.0))
    dec[:, 1, :, 1] = np.exp(-lg[None, :] * (i + 1.0)) / 16.0
    dec[:, 1, :, 2] = np.exp(lg[None, :] * (3.0 - i)) / 16.0
    c["decB"] = dec.astype(np.float32)
    j = np.arange(128)[:, None]
    t = np.arange(4)[None, :]
    sm = np.zeros((128, 9, 4), np.float32)
    sm[:, 0, :] = (j >= t)
    for s_ in range(4):
        sm[:, 1 + s_, s_] = 1.0
    c["smask"] = sm.astype(bf)
    mn = np.zeros((4, 3, 4), np.float32)
    tp = np.arange(4)[:, None]
    mn[:, 0, :] = (tp <= t)
    mn[:, 1, :] = (tp == t)
    mn[:, 2, :] = (tp == t)
    c["mnew"] = mn.astype(bf)
    c["triU"] = (k <= q).astype(np.float32)
    return c


CONST_SHAPES = None


def build(NL=NL_DEFAULT, debug=False, NHA=16):
    nc = bass.Bass("TRN2", target_bir_lowering=False)
    hc = host_consts()

    def din(name, shape, dt=F32):
        return nc.dram_tensor(name, list(shape), dt, kind="ExternalInput").ap()

    def dout(name, shape, dt=F32):
        return nc.dram_tensor(name, list(shape), dt, kind="ExternalOutput").ap()

    xp = din("xp", [S, DM])
    xs = din("xs", [4, DM])
    ckv = [din("ckv%d" % g, [2, LBUF[g], 4096]) for g in range(3)]
    stb_in = din("stb", [8, 256, 512])
    stc_in = din("stc", [4, 256, 512])
    norm_g = din("norm_g", [4, DM])
    final_g = din("final_g", [DM])
    w_in_a = din("w_in_a", [2, DM, 20480])
    w_out_a = din("w_out_a", [2, 2048, DM])
    w_in_b = din("w_in_b", [DM, 12288])
    gn_b = din("gn_b", [8, 512])
    w_out_b = din("w_out_b", [4096, DM])
    w_in_c = din("w_in_c", [DM, 6160])
    w_gate2 = din("w_gate2", [16, 1024])
    b_gate = din("b_gate", [1, 1024])
    gn_c = din("gn_c", [4, 512])
    w_out_c = din("w_out_c", [2048, DM])
    cd = {}
    for kk, v in hc.items():
        cd[kk] = din("c_" + kk, v.shape, BF16 if v.dtype == ml_dtypes.bfloat16 else F32)

    y_p = dout("y_p", [S, DM])
    y_s = dout("y_s", [4, DM])
    p_kv = [dout("p_kv%d" % g, [2, LBUF[g] if LBUF[g] < S else S, 4096]) for g in range(3)]
    p_stb = dout("p_stb", [8, 256, 512])
    p_stc = dout("p_stc", [4, 256, 512])
    s_kv = dout("s_kv", [2, 3, 4, 4096])
    s_stb = dout("s_stb", [8, 256, 512])
    s_stc = dout("s_stc", [4, 256, 512])
    xres = nc.dram_tensor("xres", [SS, DM], F32, kind="Internal").ap()
    dbg_x = dout("dbg_x", [SS, DM]) if debug else None
    dbg_acc = dout("dbg_acc", [128, 2, SS]) if debug else None
    dbg_vg = dout("dbg_vg", [128, 17, 128], BF16) if debug else None
    dbg_qkT = dout("dbg_qkT", [128, 2, 17, 128], BF16) if debug else None
    dbg_yT = dout("dbg_yT", [128, 2, SS], BF16) if debug else None
    hTd = nc.dram_tensor("hTd", [17, 128, 16, 128], BF16, kind="Internal").ap()

    P = Prog(nc)
    ctxs = []

    def sb(name, shape, dt):
        g = nc.sbuf_tensor(name, list(shape), dt)
        t = g.__enter__()
        ctxs.append(g)
        return t

    def ps(name, shape, dt):
        g = nc.psum_tensor(name, list(shape), dt)
        t = g.__enter__()
        ctxs.append(g)
        return t

    pj = [ps("pj0", [128, 512], F32), ps("pj1", [128, 512], F32)]
    ptb = ps("ptb", [128, 512], BF16)
    psc = [ps("psc0", [128, 512], F32), ps("psc1", [128, 512], F32)]
    pod = [ps("pod0", [128, 512], F32), ps("pod1", [128, 512], F32)]
    pg = ps("pg", [128, 512], F32)

    ident = sb("ident", [128, 128], BF16)
    maskAB = sb("maskAB", [128, 512], BF16)
    maskBf = sb("maskBf", [128, 128], F32)
    ones_bf = sb("ones_bf", [128, 128], BF16)
    ones_f = sb("ones_f", [128, 8], F32)
    cosA = sb("cosA", [128, 3, 17, 16], F32)
    sinA = sb("sinA", [128, 3, 17, 16], F32)
    decB = sb("decB", [128, 2, 8, 3], F32)
    smask = sb("smask", [128, 9, 4], BF16)
    mnew = sb("mnew", [4, 3, 4], BF16)
    triU = sb("triU", [128, 128], F32)
    gvec = sb("gvec", [128, DM], F32)
    for nm, t in (("ident", ident), ("maskAB", maskAB), ("maskBf", maskBf), ("ones_bf", ones_bf), ("ones_f", ones_f),
                  ("cosA", cosA), ("sinA", sinA), ("decB", decB), ("smask", smask), ("mnew", mnew), ("triU", triU)):
        P.dma("sync", t[:], cd[nm], writes=[nm], slot="const")

    AR_BYTES = 154880
    arena = sb("arena", [128, AR_BYTES // 4], F32)
    ar_off = [0]

    def ar_reset():
        ar_off[0] = 0

    def av(shape, dt):
        n = int(np.prod(shape[1:])) * (2 if dt == BF16 else 4)
        n4 = (n + 3) // 4
        o = ar_off[0]
        assert o + n4 <= AR_BYTES // 4, ("arena overflow", o, n4)
        ar_off[0] = o + ((n4 + 7) // 8) * 8
        v = arena[0:shape[0], o:o + n4]
        if dt == BF16:
            v = v.bitcast(BF16)
        if len(shape) == 3:
            v = v.rearrange("p (a b) -> p a b", b=shape[2])
        elif len(shape) == 4:
            v = v.rearrange("p (a b c) -> p a b c", b=shape[2], c=shape[3])
        return v

    xts = [sb("xt0", [128, DM], F32), sb("xt1", [128, DM], F32)]
    hn = sb("hn", [128, DM], BF16)
    st4 = sb("st4", [128, 8], F32)
    qkf = [sb("qkf0", [128, 512], F32), sb("qkf1", [128, 256], F32)]
    vf = [sb("vf0", [128, 128], F32), sb("vf1", [128, 128], F32)]
    qkb = [sb("qkb0", [128, 1024], BF16), sb("qkb1", [128, 256], BF16)]
    rt = sb("rt", [128, 6, 256], F32)
    PT = [sb("PT0", [128, 512], BF16), sb("PT1", [128, 512], BF16)]
    sg = sb("sg", [128, 512], F32)
    rc = sb("rc", [128, 512], F32)

    HT_ALL = ["hT%d" % t for t in range(17)]
    scale_a = 1.0 / math.sqrt(128.0)
    cnt = {"pj": 0, "att": 0, "ptb": 0, "t": 0}

    def rms_to(nt, gsrc_key, out_ap, out_key, xt, xk, extra_reads=()):
        P.op("scalar", lambda e: e.activation(out=hn[0:nt, :], in_=xt[0:nt, :], func=AF.Square, accum_out=st4[0:nt, 0:1]),
             reads=[xk], writes=["hn", "st4"])
        P.op("scalar", lambda e: e.activation(out=st4[0:nt, 1:2], in_=st4[0:nt, 0:1], func=AF.Sqrt, scale=1.0 / DM, bias=EPS),
             reads=["st4"], writes=["st4"])
        P.op("vector", lambda e: e.reciprocal(out=st4[0:nt, 2:3], in_=st4[0:nt, 1:2]), reads=["st4"], writes=["st4"])
        P.op("vector", lambda e: e.scalar_tensor_tensor(out=out_ap, in0=xt[0:nt, :], scalar=st4[0:nt, 2:3], in1=gvec[0:nt, :],
                                                       op0=ALU.mult, op1=ALU.mult),
             reads=[xk, "st4", gsrc_key] + list(extra_reads), writes=[out_key])

    def load_gvec(src_row_ap):
        P.dma("sync", gvec[:], src_row_ap.partition_broadcast(128), writes=["gvec"], slot="gvec")

    def produce_h(t, nt, hT, to_dram):
        for q4 in range(4):
            sl = 0
            cnt["ptb"] += 1
            for j in range(4):
                kc = q4 * 4 + j
                P.op("tensor", lambda e, kc=kc, j=j, sl=sl: e.transpose(out=ptb[:, sl * 512 + j * 128: sl * 512 + j * 128 + max(nt, 32)],
                                                                         in_=hn[0:max(nt, 32), kc * 128:(kc + 1) * 128], identity=ident[0:max(nt, 32), 0:max(nt, 32)]),
                     reads=["hn", "ident"], writes=["ptb%d" % sl])
            src = ptb[:, sl * 512:(sl + 1) * 512].rearrange("p (a t) -> p a t", t=128)[:, :, 0:nt]
            if not to_dram:
                dst = hT[:, q4 * 4:(q4 + 1) * 4, t * 128:t * 128 + nt]
                P.op("scalar", lambda e, dst=dst, src=src: e.copy(out=dst, in_=src), reads=["ptb%d" % sl], writes=["hT%d" % t])
            else:
                dst = hT[:, q4 * 4:(q4 + 1) * 4, 0:nt]
                P.op("scalar", lambda e, dst=dst, src=src: e.copy(out=dst, in_=src), reads=["ptb%d" % sl], writes=["hTst"])
        if to_dram:
            P.dma("sync", hTd[t, :, :, 0:nt], hT[:, :, 0:nt], reads=["hTst"], writes=["hTd%d" % t], slot="hTd")

    def wout_pass(li, yT, wo, nk, first, last, next_kind, hT_next):
        for t in range(17):
            nt = 128 if t < 16 else 4
            r0 = t * 128
            xt = xts[t % 2]
            xk = "xt%d" % (t % 2)
            if li == 0 and first:
                src = xp[r0:r0 + nt, :] if t < 16 else xs[:, :]
                P.dma("sync", xt[0:nt, :], src, writes=[xk], slot=xk)
            else:
                P.dma("sync", xt[0:nt, :], xres[r0:r0 + nt, :], reads=["xres%d" % t], writes=[xk], slot=xk)
            banks = [pj[0], pj[1], psc[0], psc[1]]
            bkeys = ["pj0", "pj1", "psc0", "psc1"]
            for cb in range(4):
                for k in range(nk):
                    P.op("tensor", lambda e, cb=cb, k=k, nt=nt, r0=r0, banks=banks: e.matmul(banks[cb][0:nt, :], lhsT=yT[:, k, r0:r0 + nt],
                                                                   rhs=wo[:, k, cb * 512:(cb + 1) * 512], start=(k == 0), stop=(k == nk - 1)),
                         reads=["yT", "wo"], writes=[bkeys[cb]])
            for cb in range(4):
                P.op("vector", lambda e, cb=cb, nt=nt, banks=banks, xt=xt: e.tensor_tensor(out=xt[0:nt, cb * 512:(cb + 1) * 512], in0=xt[0:nt, cb * 512:(cb + 1) * 512],
                                                                 in1=banks[cb][0:nt, :], op=ALU.add),
                     reads=[xk, bkeys[cb]], writes=[xk])
            is_final = last and (li == NL - 1)
            if not is_final:
                P.dma("sync", xres[r0:r0 + nt, :], xt[0:nt, :], reads=[xk], writes=["xres%d" % t], slot="xres")
            if last:
                if is_final:
                    if debug:
                        P.dma("sync", dbg_x[r0:r0 + nt, :], xt[0:nt, :], reads=[xk], slot="dbg")
                    rms_to(nt, "gvec", xt[0:nt, :], xk, xt, xk)
                    dst = y_p[r0:r0 + nt, :] if t < 16 else y_s[:, :]
                    P.dma("sync", dst, xt[0:nt, :], reads=[xk], slot="yout")
                else:
                    rms_to(nt, "gvec", hn[0:nt, :], "hn", xt, xk, extra_reads=["hn"])
                    produce_h(t, nt, hT_next, to_dram=(next_kind != 0))

    def layer_A(li, slot):
        ar_reset()
        hT = av([128, 16, SS], BF16)
        wb = [av([128, 16, 384], BF16) for _ in range(2)]
        wo = av([128, 4, DM], BF16)
        yT = av([128, 4, SS], BF16)
        qkT = av([128, 2, 17, 128], BF16)
        vg = av([128, 17, 128], BF16)
        o_acc = ar_off[0]
        acc = av([128, 2, SS], F32)
        ck = av([128, 4, 2, 128], BF16)
        ckT = av([128, 128], BF16)
        PTs = av([128, 32], BF16)
        hst = arena[0:128, o_acc:o_acc + 1024].bitcast(BF16).rearrange("p (a b) -> p a b", b=128)
        return dict(hT=hT, wb=wb, wo=wo, yT=yT, qkT=qkT, vg=vg, acc=acc, ck=ck, ckT=ckT, PTs=PTs, hst=hst)

    def run_layer_A(li, slot, V, next_kind, hT_next):
        hT, wbs, wo, yT, qkT, vg, acc, ck, ckT, PTs = (V[k] for k in ("hT", "wb", "wo", "yT", "qkT", "vg", "acc", "ck", "ckT", "PTs"))
        wsrc = w_in_a[slot].rearrange("(kc p) (s c) -> p kc s c", p=128, c=2048)
        wi = 0
        for h in range(NHA):
            hs = h % 4
            for g in range(3):
                dil = DILS[g]
                nb = 16 // dil
                bi = wi % 2
                wi += 1
                wb = wbs[bi]
                wkeys = ["wb%d_%d" % (bi, kq) for kq in range(3)]
                for kq in range(3):
                    P.dma("gpsimd", wb[:, :, kq * 128:(kq + 1) * 128],
                          wsrc[:, :, 3 * g + kq, h * 128:(h + 1) * 128], writes=[wkeys[kq]], slot=wkeys[kq])
                pendT = None
                for tau in range(17):
                    nt = 128 if tau < 16 else 4
                    if tau < 16:
                        r, n = divmod(tau, nb)
                        st0 = n * 128 * dil + r
                        tsl = slice(st0, st0 + 127 * dil + 1, dil)
                    else:
                        tsl = slice(S, S + 4)
                    pi = cnt["pj"] % 2
                    cnt["pj"] += 1
                    pjt = pj[pi]
                    for kc in range(16):
                        P.op("tensor", lambda e, kc=kc, tsl=tsl, pjt=pjt, wb=wb, nt=nt: e.matmul(pjt[0:nt, 0:384], lhsT=hT[:, kc, tsl], rhs=wb[:, kc, :],
                                                                                          start=(kc == 0), stop=(kc == 15)),
                             reads=HT_ALL + wkeys, writes=["pj%d" % pi])
                    ti = cnt["t"] % 2
                    cnt["t"] += 1
                    qf = qkf[ti]
                    P.op("scalar", lambda e, qf=qf, pjt=pjt, nt=nt: e.copy(out=qf[0:nt, 0:256], in_=pjt[0:nt, 0:256]),
                         reads=["pj%d" % pi], writes=["qkf%d" % ti])
                    P.op("vector", lambda e, pjt=pjt, nt=nt, tau=tau: e.tensor_copy(out=vg[0:nt, tau, :], in_=pjt[0:nt, 256:384]),
                         reads=["pj%d" % pi], writes=["vg%d" % tau])
                    if tau == 16:
                        need = True
                    elif g == 0:
                        need = (tau == 15)
                    elif g == 1:
                        need = (tau % nb == nb - 1)
                    else:
                        need = True
                    if need:
                        P.op("scalar", lambda e, pjt=pjt, nt=nt, ti=ti: e.copy(out=vf[ti][0:nt, :], in_=pjt[0:nt, 256:384]),
                             reads=["pj%d" % pi], writes=["vf%d" % ti])
                    q3 = qf[0:nt, 0:256].rearrange("p (a d) -> p a d", d=128)
                    x1 = q3[:, :, 0:16]
                    x2 = q3[:, :, 16:32]
                    cs = cosA[0:nt, g, tau:tau + 1, :].broadcast_to([nt, 2, 16])
                    sn = sinA[0:nt, g, tau:tau + 1, :].broadcast_to([nt, 2, 16])
                    tv = [rt[0:nt, i, 0:32].rearrange("p (a d) -> p a d", d=16) for i in range(4)]
                    kq_ = "qkf%d" % ti
                    P.op("gpsimd", lambda e, tv=tv, x1=x1, cs=cs: e.tensor_tensor(out=tv[0], in0=x1, in1=cs, op=ALU.mult), reads=[kq_, "cosA"], writes=["rt0"])
                    P.op("gpsimd", lambda e, tv=tv, x2=x2, sn=sn: e.tensor_tensor(out=tv[1], in0=x2, in1=sn, op=ALU.mult), reads=[kq_, "sinA"], writes=["rt1"])
                    P.op("gpsimd", lambda e, tv=tv, x1=x1, sn=sn: e.tensor_tensor(out=tv[2], in0=x1, in1=sn, op=ALU.mult), reads=[kq_, "sinA"], writes=["rt2"])
                    P.op("gpsimd", lambda e, tv=tv, x2=x2, cs=cs: e.tensor_tensor(out=tv[3], in0=x2, in1=cs, op=ALU.mult), reads=[kq_, "cosA"], writes=["rt3"])
                    P.op("gpsimd", lambda e, tv=tv, x1=x1: e.tensor_tensor(out=x1, in0=tv[0], in1=tv[1], op=ALU.subtract), reads=["rt0", "rt1", kq_], writes=[kq_])
                    P.op("gpsimd", lambda e, tv=tv, x2=x2: e.tensor_tensor(out=x2, in0=tv[2], in1=tv[3], op=ALU.add), reads=["rt2", "rt3", kq_], writes=[kq_])
                    qb = qkb[ti]
                    P.op("vector", lambda e, qb=qb, qf=qf, nt=nt: e.tensor_copy(out=qb[0:nt, 0:256], in_=qf[0:nt, 0:256]),
                         reads=[kq_], writes=["qkb%d" % ti])
                    def emit_T(tau, nt, ti, qb):
                        sl = 0
                        for a in range(2):
                            P.op("tensor", lambda e, a=a, sl=sl, qb=qb, nt=nt: e.transpose(out=ptb[:, sl * 512 + a * 128: sl * 512 + a * 128 + max(nt, 32)],
                                                                                        in_=qb[0:max(nt, 32), a * 128:(a + 1) * 128], identity=ident[0:max(nt, 32), 0:max(nt, 32)]),
                                 reads=["qkb%d" % ti, "ident"], writes=["ptb%d" % sl])
                        srcT = ptb[:, sl * 512:sl * 512 + 256].rearrange("p (a t) -> p a t", t=128)[:, :, 0:nt]
                        P.op("scalar", lambda e, srcT=srcT, tau=tau, nt=nt: e.copy(out=qkT[:, :, tau, 0:nt], in_=srcT),
                             reads=["ptb%d" % sl], writes=["qkT%d" % tau])
                    if pendT is not None:
                        emit_T(*pendT)
                    pendT = (tau, nt, ti, qb)
                    if need:
                        if tau == 16:
                            dk = s_kv[slot, g, :, h * 128:(h + 1) * 128]
                            dv = s_kv[slot, g, :, 2048 + h * 128:2048 + (h + 1) * 128]
                        else:
                            r, n = divmod(tau, nb)
                            keep = min(LBUF[g], S)
                            base = (n * 128 * dil + r) - (S - keep)
                            rows = slice(base, base + 127 * dil + 1, dil)
                            dk = p_kv[g][slot, rows, h * 128:(h + 1) * 128]
                            dv = p_kv[g][slot, rows, 2048 + h * 128:2048 + (h + 1) * 128]
                        P.dma("sync", dk, qf[0:nt, 128:256], reads=[kq_], slot="okv%d" % ti)
                        P.dma("sync", dv, vf[ti][0:nt, :], reads=["vf%d" % ti], slot="okv%d" % ti)
                emit_T(*pendT)
                def emit_S(pair):
                    ai = cnt["att"] % 2
                    cnt["att"] += 1
                    pst, pot, ptt = psc[ai], pod[ai], PT[ai]
                    ks_, ko_, kp_ = "psc%d" % ai, "pod%d" % ai, "PT%d" % ai
                    for b in range(2):
                        tau = 2 * pair + b
                        r, n = divmod(tau, nb)
                        if n > 0:
                            P.op("tensor", lambda e, tau=tau, b=b, pst=pst: e.matmul(pst[:, (2 * b) * 128:(2 * b + 1) * 128], lhsT=qkT[:, 1, tau - 1, :],
                                                                                   rhs=qkT[:, 0, tau, :], start=True, stop=True),
                                 reads=["qkT%d" % (tau - 1), "qkT%d" % tau], writes=[ks_])
                        P.op("tensor", lambda e, tau=tau, b=b, pst=pst: e.matmul(pst[:, (2 * b + 1) * 128:(2 * b + 2) * 128], lhsT=qkT[:, 1, tau, :],
                                                                               rhs=qkT[:, 0, tau, :], start=True, stop=True),
                             reads=["qkT%d" % tau], writes=[ks_])
                    P.op("scalar", lambda e, pst=pst, ptt=ptt: e.activation(out=ptt[:, :], in_=pst[:, :], func=AF.Exp, scale=scale_a),
                         reads=[ks_], writes=[kp_])
                    P.op("vector", lambda e, ptt=ptt: e.tensor_tensor(out=ptt[:, :], in0=ptt[:, :], in1=maskAB[:, :], op=ALU.mult),
                         reads=[kp_, "maskAB"], writes=[kp_])
                    P.op("vector", lambda e, pot=pot: e.memset(pot[:, :], 0.0), writes=[ko_])
                    return (pot, ptt, ko_, kp_)

                def emit_PV(pair, pot, ptt, ko_, kp_):
                    for b in range(2):
                        tau = 2 * pair + b
                        r, n = divmod(tau, nb)
                        oreg = pot[:, b * 256:b * 256 + 128]
                        dreg = pot[:, b * 256 + 128:b * 256 + 256]
                        parts = ([(tau - 1, 2 * b)] if n > 0 else []) + [(tau, 2 * b + 1)]
                        for (kt, pslot) in parts:
                            P.op("tensor", lambda e, kt=kt, pslot=pslot, oreg=oreg, ptt=ptt: e.matmul(oreg, lhsT=vg[:, kt, :], rhs=ptt[:, pslot * 128:(pslot + 1) * 128],
                                                                                                    start=False, stop=False, skip_group_check=True),
                                 reads=["vg%d" % kt, kp_, ko_], writes=[ko_])
                            P.op("tensor", lambda e, pslot=pslot, dreg=dreg, ptt=ptt: e.matmul(dreg, lhsT=ones_bf[:, :], rhs=ptt[:, pslot * 128:(pslot + 1) * 128],
                                                                                             start=False, stop=False, skip_group_check=True),
                                 reads=["ones_bf", kp_, ko_], writes=[ko_])
                    for b in range(2):
                        tau = 2 * pair + b
                        r, n = divmod(tau, nb)
                        st0 = n * 128 * dil + r
                        dst = acc[:, :, st0:st0 + 127 * dil + 1:dil]
                        src = pot[:, b * 256:(b + 1) * 256].rearrange("p (a t) -> p a t", t=128)
                        if g == 0:
                            P.op("scalar", lambda e, dst=dst, src=src: e.copy(out=dst, in_=src), reads=[ko_], writes=["acc"])
                        else:
                            P.op("vector", lambda e, dst=dst, src=src: e.tensor_tensor(out=dst, in0=dst, in1=src, op=ALU.add), reads=[ko_, "acc"], writes=["acc"])

                pendA = None
                for pair in range(8):
                    ctx = emit_S(pair)
                    if pendA is not None:
                        emit_PV(*pendA)
                    pendA = (pair,) + ctx
                emit_PV(*pendA)
                nset = 1 if g == 0 else 4
                for kv in range(2):
                    if g == 0:
                        srcc = ckv[0][slot, :, kv * 2048 + h * 128: kv * 2048 + (h + 1) * 128]
                        P.dma("gpsimd", ck[:, 0, kv, :], srcc, writes=["ck%d" % kv], slot="ck%d" % kv)
                    else:
                        srcc = ckv[g][slot, :, :].rearrange("(i t) c -> i t c", t=dil)[:, 0:4, kv * 2048 + h * 128: kv * 2048 + (h + 1) * 128]
                        P.dma("gpsimd", ck[:, :, kv, :], srcc, writes=["ck%d" % kv], slot="ck%d" % kv)
                ai = cnt["att"] % 2
                cnt["att"] += 1
                pst, pot = psc[ai], pod[ai]
                ks_, ko_ = "psc%d" % ai, "pod%d" % ai
                P.op("vector", lambda e, pot=pot: e.memset(pot[:, 0:8], 0.0), writes=[ko_])
                for s_ in range(nset):
                    sl = 0
                    cnt["ptb"] += 1
                    P.op("tensor", lambda e, s_=s_, sl=sl: e.transpose(out=ptb[:, sl * 512:sl * 512 + 128], in_=ck[:, s_, 0, :], identity=ident[:, :]),
                         reads=["ck0", "ident"], writes=["ptb%d" % sl])
                    P.op("scalar", lambda e, sl=sl: e.copy(out=ckT[:, :], in_=ptb[:, sl * 512:sl * 512 + 128]), reads=["ptb%d" % sl], writes=["ckT"])
                    P.op("tensor", lambda e, s_=s_, pst=pst: e.matmul(pst[:, s_ * 4:s_ * 4 + 4], lhsT=ckT[:, :], rhs=qkT[:, 0, 16, 0:4], start=True, stop=True),
                         reads=["ckT", "qkT16"], writes=[ks_])
                    P.op("scalar", lambda e, s_=s_, pst=pst: e.activation(out=PTs[:, s_ * 4:s_ * 4 + 4], in_=pst[:, s_ * 4:s_ * 4 + 4], func=AF.Exp, scale=scale_a),
                         reads=[ks_], writes=["PTs"])
                    mi = 0 if g == 0 else 1 + s_
                    P.op("vector", lambda e, s_=s_, mi=mi: e.tensor_tensor(out=PTs[:, s_ * 4:s_ * 4 + 4], in0=PTs[:, s_ * 4:s_ * 4 + 4], in1=smask[:, mi, :], op=ALU.mult),
                         reads=["PTs", "smask"], writes=["PTs"])
                    P.op("tensor", lambda e, s_=s_, pot=pot: e.matmul(pot[:, 0:4], lhsT=ck[:, s_, 1, :], rhs=PTs[:, s_ * 4:s_ * 4 + 4], start=False, stop=False, skip_group_check=True),
                         reads=["ck1", "PTs", ko_], writes=[ko_])
                    P.op("tensor", lambda e, s_=s_, pot=pot: e.matmul(pot[:, 4:8], lhsT=ones_bf[:, :], rhs=PTs[:, s_ * 4:s_ * 4 + 4], start=False, stop=False, skip_group_check=True),
                         reads=["ones_bf", "PTs", ko_], writes=[ko_])
                P.op("tensor", lambda e, pst=pst: e.matmul(pst[0:4, 16:20], lhsT=qkT[:, 1, 16, 0:4], rhs=qkT[:, 0, 16, 0:4], start=True, stop=True),
                     reads=["qkT16"], writes=[ks_])
                P.op("scalar", lambda e, pst=pst: e.activation(out=PTs[0:4, 16:20], in_=pst[0:4, 16:20], func=AF.Exp, scale=scale_a), reads=[ks_], writes=["PTs2"])
                P.op("vector", lambda e, g=g: e.tensor_tensor(out=PTs[0:4, 16:20], in0=PTs[0:4, 16:20], in1=mnew[0:4, g, :], op=ALU.mult),
                     reads=["PTs2", "mnew"], writes=["PTs2"])
                P.op("tensor", lambda e, pot=pot: e.matmul(pot[:, 0:4], lhsT=vg[0:4, 16, :], rhs=PTs[0:4, 16:20], start=False, stop=False, skip_group_check=True),
                     reads=["vg16", "PTs2", ko_], writes=[ko_])
                P.op("tensor", lambda e, pot=pot: e.matmul(pot[:, 4:8], lhsT=ones_bf[0:4, :], rhs=PTs[0:4, 16:20], start=False, stop=False, skip_group_check=True),
                     reads=["ones_bf", "PTs2", ko_], writes=[ko_])
                dst = acc[:, :, S:S + 4]
                src = pot[:, 0:8].rearrange("p (a t) -> p a t", t=4)
                if g == 0:
                    P.op("scalar", lambda e, dst=dst, src=src: e.copy(out=dst, in_=src), reads=[ko_], writes=["acc"])
                else:
                    P.op("vector", lambda e, dst=dst, src=src: e.tensor_tensor(out=dst, in0=dst, in1=src, op=ALU.add), reads=[ko_, "acc"], writes=["acc"])
            if debug and h == 0:
                P.dma("sync", dbg_acc, acc[:, :, :], reads=["acc"], slot="dbg")
                P.dma("sync", dbg_vg, vg[:, :, :], reads=["vg%d" % i for i in range(17)], slot="dbg")
                P.dma("sync", dbg_qkT.rearrange("p a t d -> p (a t) d"), qkT[:, :, :, :].rearrange("p a t d -> p (a t) d"), reads=["qkT%d" % i for i in range(17)], slot="dbg")
            bi = wi % 2
            wi += 1
            wgt = wbs[bi]
            gk = "wb%d_0" % bi
            P.dma("gpsimd", wgt[:, :, 0:128], wsrc[:, :, 9, h * 128:(h + 1) * 128], writes=[gk], slot=gk)
            for c in range(5):
                c0 = c * 512
                n_ = 512 if c < 4 else 4
                for kc in range(16):
                    P.op("tensor", lambda e, kc=kc, c0=c0, n_=n_, wgt=wgt: e.matmul(pg[:, 0:n_], lhsT=wgt[:, kc, 0:128], rhs=hT[:, kc, c0:c0 + n_],
                                                                                  start=(kc == 0), stop=(kc == 15)),
                         reads=HT_ALL + [gk], writes=["pg"])
                P.op("scalar", lambda e, n_=n_: e.activation(out=sg[:, 0:n_], in_=pg[:, 0:n_], func=AF.Silu), reads=["pg"], writes=["sg"])
                P.op("vector", lambda e, c0=c0, n_=n_: e.reciprocal(out=rc[:, 0:n_], in_=acc[:, 1, c0:c0 + n_]), reads=["acc"], writes=["rc"])
                P.op("vector", lambda e, c0=c0, n_=n_: e.tensor_tensor(out=rc[:, 0:n_], in0=rc[:, 0:n_], in1=acc[:, 0, c0:c0 + n_], op=ALU.mult),
                     reads=["rc", "acc"], writes=["rc"])
                P.op("vector", lambda e, c0=c0, n_=n_, hs=hs: e.tensor_tensor(out=yT[:, hs, c0:c0 + n_], in0=rc[:, 0:n_], in1=sg[:, 0:n_], op=ALU.mult),
                     reads=["rc", "sg"], writes=["yT"])
            if hs == 3:
                P.dma("gpsimd", wo[:, :, :], w_out_a[slot, (h - 3) * 128:(h + 1) * 128, :].rearrange("(k p) n -> p k n", p=128), writes=["wo"], slot="wo")
                if h == NHA - 1:
                    load_gvec(norm_g[li + 1, :] if li < NL - 1 else final_g)
                if h == NHA - 1 and next_kind != 0 and li < NL - 1:
                    P.barrier()
                wout_pass(li, yT, wo, 4, first=(h == 3), last=(h == NHA - 1), next_kind=next_kind, hT_next=hT_next)

    def views_BC():
        ar_reset()
        V = {}
        V["hTt"] = [av([128, 16, 128], BF16) for _ in range(2)]
        V["hTst"] = av([128, 16, 128], BF16)
        V["wb"] = [av([128, 16, 512], BF16) for _ in range(3)]
        V["wo"] = av([128, 4, DM], BF16)
        V["yT"] = av([128, 4, SS], BF16)
        V["stf"] = [av([128, 2, 512], F32) for _ in range(2)]
        V["stb"] = [av([128, 2, 512], BF16) for _ in range(2)]
        V["cs"] = [av([128, 2, 128], F32) for _ in range(2)]
        V["qkT"] = av([128, 4, 128], BF16)
        V["vb"] = [av([128, 512], BF16) for _ in range(2)]
        V["scT"] = av([128, 128], BF16)
        V["of"] = av([128, 512], F32)
        V["gf"] = [av([128, 512], F32) for _ in range(2)]
        V["gnv"] = av([128, 512], F32)
        V["yb"] = [av([128, 512], BF16) for _ in range(2)]
        V["qb"] = [av([128, 768], BF16) for _ in range(2)]
        V["bst"] = av([128, 4, 6], F32)
        V["mv"] = av([128, 8], F32)
        V["wlr"] = av([128, 16, 16], BF16)
        V["lrT"] = av([32, 128], F32)
        V["wg2"] = av([17, 1024], F32)
        V["sp"] = av([128, 256], F32)
        V["eb"] = av([128, 2, 256], F32)
        V["ebl"] = av([128, 2], F32)
        return V

    def run_layer_BC(li, kind, V, next_kind, hT_next):
        NH = 8 if kind == 1 else 4
        w_in = w_in_b if kind == 1 else w_in_c
        w_out = w_out_b if kind == 1 else w_out_c
        gn = gn_b if kind == 1 else gn_c
        wsrc = w_in.rearrange("(kc p) n -> p kc n", p=128)
        hTt, wbs, wo, yT = V["hTt"], V["wb"], V["wo"], V["yT"]
        stf, stbf, cs, qkT, vbs, scT = V["stf"], V["stb"], V["cs"], V["qkT"], V["vb"], V["scT"]
        qbs = V["qb"]
        of, gfs, gnv, ybs, bst, mv = V["of"], V["gf"], V["gnv"], V["yb"], V["bst"], V["mv"]
        wlr, lrT, wg2, sp, eb, ebl = V["wlr"], V["lrT"], V["wg2"], V["sp"], V["eb"], V["ebl"]
        lgam = [math.log(1.0 - 2.0 ** (-5.0 - hh)) for hh in range(8)]
        if kind == 2:
            P.dma("gpsimd", wlr[:, :, :], wsrc[:, :, 6144:6160], writes=["wlr"], slot="wlr")
            P.dma("sync", wg2[0:16, :], w_gate2, writes=["wg2a"], slot="wg2")
            P.dma("sync", wg2[16:17, :], b_gate, writes=["wg2b"], slot="wg2")
        hcnt = 0
        for h in range(NH):
            if kind == 1:
                cols = [(h * 256, 256, 0), (2048 + h * 256, 256, 256)]
                vcol = 4096 + h * 512
                gcol = 8192 + h * 512
            else:
                cols = [(h * 256, 256, 0), (1024 + h * 256, 256, 256)]
                vcol = 2048 + h * 512
                gcol = 4096 + h * 512
            for (c0, n_, d0) in cols:
                for kq in range(2):
                    P.dma("gpsimd", wbs[0][:, kq * 8:(kq + 1) * 8, d0:d0 + n_], wsrc[:, kq * 8:(kq + 1) * 8, c0:c0 + n_], writes=["wq%d_%d" % (d0, kq)], slot="wq%d_%d" % (d0, kq))
            for kq in range(2):
                P.dma("gpsimd", wbs[1][:, kq * 8:(kq + 1) * 8, :], wsrc[:, kq * 8:(kq + 1) * 8, vcol:vcol + 512], writes=["wv_%d" % kq], slot="wv_%d" % kq)
                P.dma("gpsimd", wbs[2][:, kq * 8:(kq + 1) * 8, :], wsrc[:, kq * 8:(kq + 1) * 8, gcol:gcol + 512], writes=["wg_%d" % kq], slot="wg_%d" % kq)
            WQK = ["wq0_0", "wq0_1", "wq256_0", "wq256_1"]
            P.dma("sync", gnv[:, :], gn[h, :].partition_broadcast(128), writes=["gnv"], slot="gnv")
            P.op("gpsimd", lambda e: e.memset(stf[0][:, :, :], 0.0), writes=["stf0"])
            P.op("gpsimd", lambda e: e.memset(stbf[0][:, :, :], 0.0), writes=["stb0"])
            st_in = stb_in if kind == 1 else stc_in
            P.dma("sync", stf[1][:, :, :], st_in[h].rearrange("(c p) v -> p c v", p=128), writes=["stf1"], slot="stin")
            P.dma("gpsimd", stbf[1][:, :, :], st_in[h].rearrange("(c p) v -> p c v", p=128), writes=["stb1"], slot="stin2")
            def f_proj(t):
                    nt = 128 if t < 16 else 4
                    si = 0 if t < 16 else 1
                    hi = t % 2
                    par = t % 2
                    ht = hTt[hi]
                    hk = "hTt%d" % hi
                    vb = vbs[par]; gf = gfs[par]; yb = ybs[par]; qb = qbs[par]
                    kvb = "vb%d" % par; kgf = "gf%d" % par; kyb = "yb%d" % par; kqb = "qbB%d" % par
                    gC = math.exp(lgam[h] * (128.0 if t < 16 else 4.0)) if kind == 1 else 1.0

                    P.dma("sync", ht[:, :, 0:nt], hTd[t, :, :, 0:nt], reads=["hTd%d" % t], writes=[hk], slot=hk)
                    if kind == 1:
                        P.dma("sync", cs[hi][0:nt, 0, :], cd["cosB"][t, 0:nt, :], writes=["cs%da" % hi], slot="cs%da" % hi)
                        P.dma("sync", cs[hi][0:nt, 1, :], cd["sinB"][t, 0:nt, :], writes=["cs%db" % hi], slot="cs%db" % hi)
                    for kc in range(16):
                        P.op("tensor", lambda e, kc=kc, ht=ht, nt=nt: e.matmul(pj[0][0:nt, :], lhsT=ht[:, kc, 0:nt], rhs=wbs[0][:, kc, :], start=(kc == 0), stop=(kc == 15)),
                             reads=[hk] + WQK, writes=["pj0"])
                    for kc in range(16):
                        P.op("tensor", lambda e, kc=kc, ht=ht, nt=nt: e.matmul(pj[1][0:nt, :], lhsT=ht[:, kc, 0:nt], rhs=wbs[1][:, kc, :], start=(kc == 0), stop=(kc == 15)),
                             reads=[hk, "wv_0", "wv_1"], writes=["pj1"])
                    for kc in range(16):
                        P.op("tensor", lambda e, kc=kc, ht=ht, nt=nt: e.matmul(pg[0:nt, :], lhsT=ht[:, kc, 0:nt], rhs=wbs[2][:, kc, :], start=(kc == 0), stop=(kc == 15)),
                             reads=[hk, "wg_0", "wg_1"], writes=["pg"])

            def f_evac(t):
                    nt = 128 if t < 16 else 4
                    si = 0 if t < 16 else 1
                    hi = t % 2
                    par = t % 2
                    ht = hTt[hi]
                    hk = "hTt%d" % hi
                    vb = vbs[par]; gf = gfs[par]; yb = ybs[par]; qb = qbs[par]
                    kvb = "vb%d" % par; kgf = "gf%d" % par; kyb = "yb%d" % par; kqb = "qbB%d" % par
                    gC = math.exp(lgam[h] * (128.0 if t < 16 else 4.0)) if kind == 1 else 1.0

                    P.op("vector", lambda e, nt=nt: e.tensor_copy(out=vb[0:nt, :], in_=pj[1][0:nt, :]), reads=["pj1"], writes=[kvb])
                    P.op("scalar", lambda e, nt=nt: e.activation(out=gf[0:nt, :], in_=pg[0:nt, :], func=AF.Silu), reads=["pg"], writes=[kgf])
                    if kind == 1:
                        qf = qkf[0]
                        P.op("scalar", lambda e, nt=nt, qf=qf: e.copy(out=qf[0:nt, :], in_=pj[0][0:nt, :]), reads=["pj0"], writes=["qkf0"])
                        q4 = qf[0:nt, :].rearrange("p (a b d) -> p a b d", a=2, b=2)
                        x1 = q4[:, :, 0, :]
                        x2 = q4[:, :, 1, :]
                        cs_ = cs[hi][0:nt, 0:1, :].broadcast_to([nt, 2, 128])
                        sn_ = cs[hi][0:nt, 1:2, :].broadcast_to([nt, 2, 128])
                        tv = [rt[0:nt, i, :].rearrange("p (a d) -> p a d", d=128) for i in range(6)]
                        ck_ = "cs%d" % hi
                        P.op("gpsimd", lambda e, tv=tv, x1=x1, cs_=cs_: e.tensor_tensor(out=tv[0], in0=x1, in1=cs_, op=ALU.mult), reads=["qkf0", ck_ + "a", ck_ + "b"], writes=["rt0"])
                        P.op("gpsimd", lambda e, tv=tv, x2=x2, sn_=sn_: e.tensor_tensor(out=tv[1], in0=x2, in1=sn_, op=ALU.mult), reads=["qkf0", ck_ + "a", ck_ + "b"], writes=["rt1"])
                        P.op("gpsimd", lambda e, tv=tv, x1=x1, sn_=sn_: e.tensor_tensor(out=tv[2], in0=x1, in1=sn_, op=ALU.mult), reads=["qkf0", ck_ + "a", ck_ + "b"], writes=["rt2"])
                        P.op("gpsimd", lambda e, tv=tv, x2=x2, cs_=cs_: e.tensor_tensor(out=tv[3], in0=x2, in1=cs_, op=ALU.mult), reads=["qkf0", ck_ + "a", ck_ + "b"], writes=["rt3"])
                        P.op("gpsimd", lambda e, tv=tv: e.tensor_tensor(out=tv[4], in0=tv[0], in1=tv[1], op=ALU.subtract), reads=["rt0", "rt1"], writes=["rt4"])
                        P.op("gpsimd", lambda e, tv=tv: e.tensor_tensor(out=tv[5], in0=tv[2], in1=tv[3], op=ALU.add), reads=["rt2", "rt3"], writes=["rt5"])
                        for (a, dsti, di) in ((0, 0, 0), (1, 1, 1), (1, 2, 2)):
                            for hf in range(2):
                                P.op("vector", lambda e, a=a, dsti=dsti, di=di, hf=hf, tv=tv, nt=nt, si=si, h=h: e.tensor_scalar(
                                    out=qb[0:nt, dsti * 256 + hf * 128: dsti * 256 + (hf + 1) * 128], in0=tv[4 + hf][:, a, :],
                                    scalar1=decB[0:nt, si, h, di:di + 1], scalar2=None, op0=ALU.mult),
                                    reads=["rt4", "rt5", "decB"], writes=[kqb])
                    else:
                        for kc in range(16):
                            P.op("tensor", lambda e, kc=kc, ht=ht, nt=nt: e.matmul(psc[1][0:16, 0:nt], lhsT=wlr[:, kc, :], rhs=ht[:, kc, 0:nt], start=(kc == 0), stop=(kc == 15)),
                                 reads=[hk, "wlr"], writes=["psc1"])
                        P.op("vector", lambda e: e.memset(lrT[:, :], 1.0), writes=["lrT"])
                        P.op("vector", lambda e, nt=nt: e.tensor_copy(out=lrT[0:16, 0:nt], in_=psc[1][0:16, 0:nt]), reads=["psc1", "lrT"], writes=["lrT"])
                        P.op("tensor", lambda e, nt=nt, h=h: e.matmul(pod[1][0:nt, 0:256], lhsT=lrT[0:17, 0:nt], rhs=wg2[0:17, h * 256:(h + 1) * 256], start=True, stop=True),
                             reads=["lrT", "wg2a", "wg2b"], writes=["pod1"])
                        P.op("scalar", lambda e, nt=nt: e.activation(out=sp[0:nt, :], in_=pod[1][0:nt, 0:256], func=AF.Exp, scale=-1.0), reads=["pod1"], writes=["sp"])
                        P.op("scalar", lambda e, nt=nt: e.activation(out=sp[0:nt, :], in_=sp[0:nt, :], func=AF.Ln, bias=1.0), reads=["sp"], writes=["sp"])
                        P.op("tensor", lambda e, nt=nt: e.matmul(pod[1][0:nt, 256:512], lhsT=triU[0:nt, 0:nt], rhs=sp[0:nt, :], start=True, stop=True),
                             reads=["sp", "triU", "pod1"], writes=["pod1"])
                        P.op("scalar", lambda e, nt=nt: e.activation(out=eb[0:nt, 0, :], in_=pod[1][0:nt, 256:512], func=AF.Exp, scale=-1.0 / 16.0), reads=["pod1"], writes=["eb"])
                        P.op("scalar", lambda e, nt=nt: e.activation(out=eb[0:nt, 1, :], in_=pod[1][0:nt, 256:512], func=AF.Exp, scale=1.0 / 16.0), reads=["pod1"], writes=["eb"])
                        for ch in range(2):
                            P.op("tensor", lambda e, nt=nt, ch=ch: e.matmul(psc[1][:, 128 + ch:129 + ch], lhsT=sp[0:nt, ch * 128:(ch + 1) * 128], rhs=ones_f[0:nt, 0:1], start=True, stop=True),
                                 reads=["sp", "ones_f", "psc1"], writes=["psc1"])
                        P.op("scalar", lambda e: e.activation(out=ebl[:, 0:2], in_=psc[1][:, 128:130], func=AF.Exp, scale=-1.0 / 16.0), reads=["psc1"], writes=["ebl"])
                        P.op("vector", lambda e, nt=nt: e.scalar_tensor_tensor(out=qb[0:nt, 0:256], in0=pj[0][0:nt, 0:256], scalar=1.0 / 16.0, in1=eb[0:nt, 0, :], op0=ALU.mult, op1=ALU.mult),
                             reads=["pj0", "eb"], writes=[kqb])
                        P.op("vector", lambda e, nt=nt: e.tensor_tensor(out=qb[0:nt, 256:512], in0=pj[0][0:nt, 256:512], in1=eb[0:nt, 1, :], op=ALU.mult),
                             reads=["pj0", "eb"], writes=[kqb])

            def f_rest(t):
                    nt = 128 if t < 16 else 4
                    si = 0 if t < 16 else 1
                    hi = t % 2
                    par = t % 2
                    ht = hTt[hi]
                    hk = "hTt%d" % hi
                    vb = vbs[par]; gf = gfs[par]; yb = ybs[par]; qb = qbs[par]
                    kvb = "vb%d" % par; kgf = "gf%d" % par; kyb = "yb%d" % par; kqb = "qbB%d" % par
                    gC = math.exp(lgam[h] * (128.0 if t < 16 else 4.0)) if kind == 1 else 1.0

                    kd0 = 512 if kind == 1 else 256
                    sl = 0
                    cnt["ptb"] += 1
                    for j in range(4):
                        P.op("tensor", lambda e, j=j, sl=sl, nt=nt: e.transpose(out=ptb[:, sl * 512 + j * 128: sl * 512 + j * 128 + max(nt, 32)], in_=qb[0:max(nt, 32), j * 128:(j + 1) * 128], identity=ident[0:max(nt, 32), 0:max(nt, 32)]),
                             reads=[kqb, "ident"], writes=["ptb%d" % sl])
                    srcT = ptb[:, sl * 512:(sl + 1) * 512].rearrange("p (a t) -> p a t", t=128)[:, :, 0:nt]
                    P.op("scalar", lambda e, srcT=srcT, nt=nt: e.copy(out=qkT[:, :, 0:nt], in_=srcT), reads=["ptb%d" % sl], writes=["qkT"])
                    for ch in range(2):
                        P.op("tensor", lambda e, ch=ch, nt=nt: e.matmul(psc[0][0:nt, 0:nt], lhsT=qkT[:, 2 + ch, 0:nt], rhs=qkT[:, ch, 0:nt], start=(ch == 0), stop=(ch == 1)),
                             reads=["qkT"], writes=["psc0"])
                    P.op("vector", lambda e, nt=nt: e.tensor_tensor(out=scT[0:nt, 0:nt], in0=psc[0][0:nt, 0:nt], in1=maskBf[0:nt, 0:nt], op=ALU.mult),
                         reads=["psc0", "maskBf"], writes=["scT"])
                    P.op("tensor", lambda e, nt=nt: e.matmul(pod[0][0:nt, :], lhsT=scT[0:nt, 0:nt], rhs=vb[0:nt, :], start=True, stop=False),
                         reads=["scT", kvb], writes=["pod0"])
                    for ch in range(2):
                        P.op("tensor", lambda e, ch=ch, nt=nt, si=si: e.matmul(pod[0][0:nt, :], lhsT=qkT[:, ch, 0:nt], rhs=stbf[si][:, ch, :], start=False, stop=(ch == 1)),
                             reads=["qkT", "stb%d" % si], writes=["pod0"])
                    for ch in range(2):
                        pstt = psc[1] if ch == 0 else pod[1]
                        pk = "psc1" if ch == 0 else "pod1"
                        P.op("tensor", lambda e, ch=ch, nt=nt, pstt=pstt: e.matmul(pstt[:, :], lhsT=qb[0:nt, kd0 + ch * 128: kd0 + (ch + 1) * 128], rhs=vb[0:nt, :], start=True, stop=True),
                             reads=[kqb, kvb, pk], writes=[pk])
                        if kind == 1:
                            P.op("vector", lambda e, ch=ch, si=si, pstt=pstt, gC=gC: e.scalar_tensor_tensor(out=stf[si][:, ch, :], in0=stf[si][:, ch, :], scalar=gC, in1=pstt[:, :], op0=ALU.mult, op1=ALU.add),
                                 reads=["stf%d" % si, pk], writes=["stf%d" % si])
                        else:
                            P.op("vector", lambda e, ch=ch, si=si, pstt=pstt: e.tensor_tensor(out=stf[si][:, ch, :], in0=stf[si][:, ch, :], in1=pstt[:, :], op=ALU.add),
                                 reads=["stf%d" % si, pk], writes=["stf%d" % si])
                            P.op("vector", lambda e, ch=ch, si=si: e.tensor_scalar(out=stf[si][:, ch, :], in0=stf[si][:, ch, :], scalar1=ebl[:, ch:ch + 1], scalar2=None, op0=ALU.mult),
                                 reads=["stf%d" % si, "ebl"], writes=["stf%d" % si])
                        P.op("scalar", lambda e, ch=ch, si=si: e.copy(out=stbf[si][:, ch, :], in_=stf[si][:, ch, :]), reads=["stf%d" % si], writes=["stb%d" % si])
                    if kind == 1:
                        for c4 in range(4):
                            P.op("vector", lambda e, c4=c4, nt=nt: e.bn_stats(out=bst[0:nt, c4, :], in_=pod[0][0:nt, c4 * 128:(c4 + 1) * 128]), reads=["pod0"], writes=["bst"])
                        P.op("vector", lambda e, nt=nt: e.bn_aggr(out=mv[0:nt, 0:2], in_=bst[0:nt, :, :]), reads=["bst"], writes=["mv"])
                        P.op("scalar", lambda e, nt=nt: e.activation(out=mv[0:nt, 2:3], in_=mv[0:nt, 1:2], func=AF.Sqrt, bias=EPS), reads=["mv"], writes=["mv"])
                        P.op("vector", lambda e, nt=nt: e.reciprocal(out=mv[0:nt, 3:4], in_=mv[0:nt, 2:3]), reads=["mv"], writes=["mv"])
                        P.op("vector", lambda e, nt=nt: e.tensor_scalar(out=mv[0:nt, 4:5], in0=mv[0:nt, 0:1], scalar1=mv[0:nt, 3:4], scalar2=-1.0, op0=ALU.mult, op1=ALU.mult),
                             reads=["mv"], writes=["mv"])
                        P.op("scalar", lambda e, nt=nt: e.activation(out=of[0:nt, :], in_=pod[0][0:nt, :], func=AF.Identity, scale=mv[0:nt, 3:4], bias=mv[0:nt, 4:5]),
                             reads=["pod0", "mv"], writes=["of"])
                    else:
                        P.op("scalar", lambda e, nt=nt: e.activation(out=of[0:nt, :], in_=pod[0][0:nt, :], func=AF.Square, accum_out=mv[0:nt, 0:1]), reads=["pod0"], writes=["of", "mv"])
                        P.op("scalar", lambda e, nt=nt: e.activation(out=mv[0:nt, 2:3], in_=mv[0:nt, 0:1], func=AF.Sqrt, scale=1.0 / 512.0, bias=EPS), reads=["mv"], writes=["mv"])
                        P.op("vector", lambda e, nt=nt: e.reciprocal(out=mv[0:nt, 3:4], in_=mv[0:nt, 2:3]), reads=["mv"], writes=["mv"])
                        P.op("scalar", lambda e, nt=nt: e.activation(out=of[0:nt, :], in_=pod[0][0:nt, :], func=AF.Identity, scale=mv[0:nt, 3:4]), reads=["pod0", "mv", "of"], writes=["of"])
                    P.op("gpsimd", lambda e, nt=nt: e.tensor_tensor(out=of[0:nt, :], in0=of[0:nt, :], in1=gnv[0:nt, :], op=ALU.mult), reads=["of", "gnv"], writes=["of"])
                    P.op("vector", lambda e, nt=nt: e.tensor_tensor(out=yb[0:nt, :], in0=of[0:nt, :], in1=gf[0:nt, :], op=ALU.mult), reads=["of", kgf], writes=[kyb])

            def f_yt(t):
                    nt = 128 if t < 16 else 4
                    si = 0 if t < 16 else 1
                    hi = t % 2
                    par = t % 2
                    ht = hTt[hi]
                    hk = "hTt%d" % hi
                    vb = vbs[par]; gf = gfs[par]; yb = ybs[par]; qb = qbs[par]
                    kvb = "vb%d" % par; kgf = "gf%d" % par; kyb = "yb%d" % par; kqb = "qbB%d" % par
                    gC = math.exp(lgam[h] * (128.0 if t < 16 else 4.0)) if kind == 1 else 1.0

                    sl = 0
                    cnt["ptb"] += 1
                    for j in range(4):
                        P.op("tensor", lambda e, j=j, sl=sl, nt=nt: e.transpose(out=ptb[:, sl * 512 + j * 128: sl * 512 + j * 128 + max(nt, 32)], in_=yb[0:max(nt, 32), j * 128:(j + 1) * 128], identity=ident[0:max(nt, 32), 0:max(nt, 32)]),
                             reads=[kyb, "ident"], writes=["ptb%d" % sl])
                    srcT = ptb[:, sl * 512:(sl + 1) * 512].rearrange("p (a t) -> p a t", t=128)[:, :, 0:nt]
                    P.op("scalar", lambda e, srcT=srcT, nt=nt, t=t: e.copy(out=yT[:, :, t * 128:t * 128 + nt], in_=srcT), reads=["ptb%d" % sl], writes=["yT"])
            f_proj(0)
            f_evac(0)
            for t in range(17):
                if t + 1 < 17:
                    f_proj(t + 1)
                if t >= 1:
                    f_yt(t - 1)
                f_rest(t)
                if t + 1 < 17:
                    f_evac(t + 1)
            f_yt(16)

            po = p_stb if kind == 1 else p_stc
            so = s_stb if kind == 1 else s_stc
            P.dma("sync", po[h].rearrange("(c p) v -> p c v", p=128), stf[0][:, :, :], reads=["stf0"], slot="sto")
            P.dma("sync", so[h].rearrange("(c p) v -> p c v", p=128), stf[1][:, :, :], reads=["stf1"], slot="sto")
            P.dma("gpsimd", wo[:, :, :], w_out[h * 512:(h + 1) * 512, :].rearrange("(k p) n -> p k n", p=128), writes=["wo"], slot="wo")
            if h == NH - 1:
                load_gvec(norm_g[li + 1, :] if li < NL - 1 else final_g)
            wout_pass(li, yT, wo, 4, first=(h == 0), last=(h == NH - 1), next_kind=next_kind, hT_next=hT_next)

    kinds = [i % 3 for i in range(NL)]
    VA = layer_A(0, 0)
    load_gvec(norm_g[0, :])
    for t in range(17):
        nt = 128 if t < 16 else 4
        src = xp[t * 128:t * 128 + nt, :] if t < 16 else xs[:, :]
        P.dma("sync", xts[t % 2][0:nt, :], src, writes=["xt%d" % (t % 2)], slot="xt%d" % (t % 2))
        rms_to(nt, "gvec", hn[0:nt, :], "hn", xts[t % 2], "xt%d" % (t % 2), extra_reads=["hn"])
        produce_h(t, nt, VA["hT"], to_dram=False)
    for li in range(NL):
        kind = kinds[li]
        print('MARK layer', li, 'starts at op', len(P.ops), flush=True)
        nk_ = kinds[li + 1] if li + 1 < NL else -1
        if kind == 0:
            run_layer_A(li, li // 3, VA, nk_, VA["hst"] if nk_ in (1, 2) else VA["hT"])
            P.barrier()
        else:
            VB = views_BC()
            if nk_ == 0:
                hT_next = VB["hTst"]
                run_layer_BC(li, kind, VB, 1, hT_next)
                P.barrier()
                VA = layer_A(li + 1, (li + 1) // 3)
                for t in range(17):
                    nt = 128 if t < 16 else 4
                    P.dma("sync", VA["hT"][:, :, t * 128:t * 128 + nt], hTd[t, :, :, 0:nt], reads=["hTd%d" % t], writes=["hT%d" % t], slot="hTl")
            else:
                run_layer_BC(li, kind, VB, nk_, VB["hTst"])
                P.barrier()
    P.emit()
    for g in reversed(ctxs):
        g.__exit__(None, None, None)
    return nc, hc, P.stats


_CACHE = {}


def kernel(**inp):
    if "nc" not in _CACHE:
        _CACHE["nc"] = build()
    nc, hc, stats = _CACHE["nc"]
    f = lambda a: np.ascontiguousarray(np.asarray(a, dtype=np.float32))
    in_maps = []
    shared = dict(
        norm_g=f(inp["norm_g"]), final_g=f(inp["final_g"]), w_in_a=f(inp["w_in_a"]), w_out_a=f(inp["w_out_a"]),
        w_in_b=f(inp["w_in_b"][0]), gn_b=f(inp["gn_b"][0]), w_out_b=f(inp["w_out_b"][0]), w_in_c=f(inp["w_in_c"][0]),
        w_gate2=f(inp["w_gate2_c"][0]), b_gate=f(inp["b_gate_c"][0]).reshape(1, 1024), gn_c=f(inp["gn_c"][0]), w_out_c=f(inp["w_out_c"][0]))
    for kk, v in hc.items():
        shared["c_" + kk] = v
    for c in range(8):
        m = dict(shared)
        m["xp"] = f(inp["x_prompt"][c % 4])
        m["xs"] = f(inp["x_sample"][c])
        for g, nm in enumerate(("cache_a_kv1", "cache_a_kv2", "cache_a_kv3")):
            m["ckv%d" % g] = f(inp[nm][:, c]).reshape(2, LBUF[g], 4096)
        m["stb"] = f(inp["state_b"][0, c])
        m["stc"] = f(inp["state_c"][0, c])
        in_maps.append(m)
    res = run_bass_kernel_spmd(nc, in_maps, core_ids=list(range(8)))
    R = res.results
    y_prompt = np.stack([R[b]["y_p"] for b in range(4)])
    y_sample = np.stack([R[c]["y_s"] for c in range(8)])
    pkv = []
    for g in range(3):
        keep = min(LBUF[g], S)
        a = np.stack([R[b]["p_kv%d" % g] for b in range(4)], axis=1)
        pkv.append(a.reshape(2, 4, keep, 2, 16, 128))
    p_stb = np.stack([R[b]["p_stb"] for b in range(4)])[None]
    p_stc = np.stack([R[b]["p_stc"] for b in range(4)])[None]
    skv = np.stack([R[c]["s_kv"] for c in range(8)], axis=0)
    s_list = [np.ascontiguousarray(skv[:, :, g].transpose(1, 0, 2, 3)).reshape(2, 8, 4, 2, 16, 128) for g in range(3)]
    s_stb = np.stack([R[c]["s_stb"] for c in range(8)])[None]
    s_stc = np.stack([R[c]["s_stc"] for c in range(8)])[None]
    outs = (y_prompt, y_sample, pkv[0], pkv[1], pkv[2], p_stb, p_stc, s_list[0], s_list[1], s_list[2], s_stb, s_stc)
    return tuple(np.ascontiguousarray(o, dtype=np.float32) for o in outs)
```

```python
import math
import numpy as np
import ml_dtypes
import concourse.bass as bass
import concourse.mybir as mybir
from concourse.bass_utils import run_bass_kernel_spmd

F32 = mybir.dt.float32
BF16 = mybir.dt.bfloat16
ALU = mybir.AluOpType
AF = mybir.ActivationFunctionType
AX = mybir.AxisListType

ENGS = ("sync", "scalar", "vector", "gpsimd", "tensor")
EPOCH = 12000
S = 2048
DM = 2048
SS = 2052
PAST = 16384
EPS = 1e-6
DILS = (1, 4, 16)
LBUF = (128, 512, 2048)
NL_DEFAULT = 4


PSUM_KEYS = {"pj0", "pj1", "ptb0", "psc0", "psc1", "pod0", "pod1", "pg"}


class Prog:
    def __init__(self, nc):
        self.nc = nc
        self.ops = []

    def op(self, eng, fn, reads=(), writes=(), dma=None):
        self.ops.append(dict(eng=eng, fn=fn, reads=tuple(reads), writes=tuple(writes), dma=dma, bar=False))

    def dma(self, eng, out, in_, reads=(), writes=(), slot="d", **kw):
        self.op(eng, lambda e: e.dma_start(out=out, in_=in_, **kw), reads, writes, dma=slot)

    def barrier(self):
        self.ops.append(dict(bar=True))

    def emit(self):
        nc = self.nc
        import os
        allops = self.ops[:int(os.environ.get('MAXOPS', '100000000'))]
        ops = []
        last_w = {}
        readers = {}
        dma_count = {}
        last_eng = {}
        last_dma = {}
        pending_bar = {}
        for o in allops:
            if o["bar"]:
                deps = set(last_eng.values()) | set(last_dma.values())
                for e in ENGS:
                    pending_bar[e] = set(deps)
                last_w.clear()
                readers.clear()
                continue
            j = len(ops)
            ops.append(o)
            deps = set()
            for r in o["reads"]:
                if r in last_w:
                    deps.add(last_w[r])
                if r in PSUM_KEYS:
                    for rd in readers.get(r, ()):
                        if ops[rd]["eng"] != o["eng"]:
                            deps.add(rd)
            for w in o["writes"]:
                if w in last_w:
                    deps.add(last_w[w])
                for rd in readers.get(w, ()):
                    if ops[rd]["eng"] == o["eng"] and ops[rd]["dma"] is None and o["dma"] is None:
                        continue
                    deps.add(rd)
            if o["eng"] in pending_bar:
                deps |= pending_bar.pop(o["eng"])
            fin = set()
            for i in deps:
                if i == j:
                    continue
                if ops[i]["dma"] is None and o["dma"] is None and ops[i]["eng"] == "tensor" and o["eng"] == "tensor":
                    continue
                fin.add(i)
            o["deps"] = fin
            o["dma_snap"] = dict(dma_count)
            if o["dma"] is not None:
                dma_count[o["dma"]] = dma_count.get(o["dma"], 0) + 1
                o["dma_idx"] = dma_count[o["dma"]]
                last_dma[o["dma"]] = j
            else:
                last_eng[o["eng"]] = j
            for r in o["reads"]:
                readers.setdefault(r, []).append(j)
            for w in o["writes"]:
                last_w[w] = j
                readers[w] = []
        for o in ops:
            o["ms"] = False
        for o in ops:
            for i in o["deps"]:
                if ops[i]["dma"] is None:
                    ops[i]["ms"] = True
        cnt = {e: 0 for e in ENGS}
        for o in ops:
            if o["ms"]:
                cnt[o["eng"]] += 1
                o["ms_idx"] = cnt[o["eng"]]
        sems = {}
        stack = []

        def get_sem(key):
            if key not in sems:
                g = nc.semaphore("s%d" % len(sems))
                sems[key] = g.__enter__()
                stack.append(g)
            return sems[key]

        seen = {e: {} for e in ENGS}
        for o in ops:
            waits = {}
            for i in o["deps"]:
                d = ops[i]
                if d["dma"] is not None:
                    n = max(o["dma_snap"].get(d["dma"], 0), d["dma_idx"])
                    ep, v = divmod(n - 1, EPOCH // 16)
                    key = ("dma", d["dma"], ep)
                    waits[key] = max(waits.get(key, 0), (v + 1) * 16)
                else:
                    ep, v = divmod(d["ms_idx"] - 1, EPOCH)
                    key = ("eng", d["eng"], ep)
                    waits[key] = max(waits.get(key, 0), v + 1)
            wl = []
            sn = seen[o["eng"]]
            for key, v in waits.items():
                if sn.get(key, 0) >= v:
                    continue
                if key[0] == "eng" and any(k2[0] == "eng" and k2[1] == key[1] and k2[2] > key[2] for k2 in sn):
                    continue
                sn[key] = v
                wl.append((key, v))
            o["waits"] = wl
        for o in ops:
            for key, v in o["waits"]:
                get_sem(key)
            if o["ms"]:
                ep, v = divmod(o["ms_idx"] - 1, EPOCH)
                o["inc"] = (get_sem(("eng", o["eng"], ep)), 1)
            if o["dma"] is not None:
                ep, v = divmod(o["dma_idx"] - 1, EPOCH // 16)
                o["inc"] = (get_sem(("dma", o["dma"], ep)), 16)
        if os.environ.get('DUMPW'):
            a_, b_ = [int(v_) for v_ in os.environ['DUMPW'].split(':')]
            for i_ in range(a_, min(b_, len(ops))):
                o_ = ops[i_]
                print('OP', i_, o_['eng'], o_['dma'], 'deps', sorted(o_['deps'])[-6:], 'waits', o_['waits'], 'ms', o_.get('ms_idx') if o_['ms'] else None, 'dmaidx', o_.get('dma_idx'), flush=True)
        final = []
        for slot, n in dma_count.items():
            ep, v = divmod(n - 1, EPOCH // 16)
            final.append((get_sem(("dma", slot, ep)), (v + 1) * 16))

        with nc.Block() as block:
            def mk(engname):
                def body(e):
                    for o in ops:
                        if o["eng"] != engname:
                            continue
                        for key, v in o["waits"]:
                            e.wait_ge(sems[key], v)
                        ins = o["fn"](e)
                        if "inc" in o:
                            ins.then_inc(o["inc"][0], o["inc"][1])
                    if engname == "sync":
                        for s_, v in final:
                            e.wait_ge(s_, v)
                return body
            block.sync(mk("sync"))
            block.scalar(mk("scalar"))
            block.vector(mk("vector"))
            block.gpsimd(mk("gpsimd"))
            block.tensor(mk("tensor"))
        for g in reversed(stack):
            g.__exit__(None, None, None)
        self.stats = dict(n_ops=len(ops), ms=cnt, nsem=len(sems))


def host_consts():
    c = {}
    bf = ml_dtypes.bfloat16
    c["ident"] = np.eye(128, dtype=np.float32).astype(bf)
    k = np.arange(128)[:, None]
    q = np.arange(128)[None, :]
    A = (k >= q).astype(np.float32)
    B = (k <= q).astype(np.float32)
    c["maskAB"] = np.concatenate([A, B, A, B], axis=1).astype(bf)
    c["maskBf"] = B.astype(np.float32)
    c["ones_bf"] = np.ones((128, 128), np.float32).astype(bf)
    c["ones_f"] = np.ones((128, 8), np.float32)
    half = 16
    inv = (500000.0 ** (-np.arange(half, dtype=np.float32) / half)).astype(np.float32)
    cosA = np.zeros((128, 3, 17, 16), np.float32)
    sinA = np.zeros((128, 3, 17, 16), np.float32)
    for g, dil in enumerate(DILS):
        nb = 16 // dil
        for tau in range(16):
            r, n = divmod(tau, nb)
            pos = ((128 * n + np.arange(128)) * dil + r).astype(np.float32)
            ang = pos[:, None] * inv[None, :]
            cosA[:, g, tau] = np.cos(ang)
            sinA[:, g, tau] = np.sin(ang)
        pos = (PAST + np.arange(4)).astype(np.float32)
        ang = pos[:, None] * inv[None, :]
        cosA[0:4, g, 16] = np.cos(ang)
        sinA[0:4, g, 16] = np.sin(ang)
    c["cosA"] = cosA
    c["sinA"] = sinA
    invb = (10000.0 ** (-np.arange(128, dtype=np.float32) / 128)).astype(np.float32)
    cosB = np.zeros((17, 128, 128), np.float32)
    sinB = np.zeros((17, 128, 128), np.float32)
    for t in range(16):
        pos = (128 * t + np.arange(128)).astype(np.float32)
        ang = pos[:, None] * invb[None, :]
        cosB[t] = np.cos(ang)
        sinB[t] = np.sin(ang)
    pos = (PAST + np.arange(4)).astype(np.float32)
    ang = pos[:, None] * invb[None, :]
    cosB[16, 0:4] = np.cos(ang)
    sinB[16, 0:4] = np.sin(ang)
    c["cosB"] = cosB
    c["sinB"] = sinB
    lg = np.log(1.0 - 2.0 ** (-5.0 - np.arange(8, dtype=np.float64)))
    dec = np.zeros((128, 2, 8, 3), np.float64)
    i = np.arange(128, dtype=np.float64)[:, None]
    dec[:, 0, :, 0] = np.exp(lg[None, :] * (i + 1.0))
    dec[:, 0, :, 1] = np.exp(-lg[None, :] * (i + 1.0)) / 16.0
    dec[:, 0, :, 2] = np.exp(lg[None, :] * (127.0 - i)) / 16.0
    dec[:, 1, :, 0] = np.exp(lg[None, :] * (i + 1.0))
    dec[:, 1, :, 1] = np.exp(-lg[None, :] * (i + 1.0)) / 16.0
    dec[:, 1, :, 2] = np.exp(lg[None, :] * (3.0 - i)) / 16.0
    c["decB"] = dec.astype(np.float32)
    j = np.arange(128)[:, None]
    t = np.arange(4)[None, :]
    sm = np.zeros((128, 9, 4), np.float32)
    sm[:, 0, :] = (j >= t)
    for s_ in range(4):
        sm[:, 1 + s_, s_] = 1.0
    c["smask"] = sm.astype(bf)
    mn = np.zeros((4, 3, 4), np.float32)
    tp = np.arange(4)[:, None]
    mn[:, 0, :] = (tp <= t)
    mn[:, 1, :] = (tp == t)
    mn[:, 2, :] = (tp == t)
    c["mnew"] = mn.astype(bf)
    c["triU"] = (k <= q).astype(np.float32)
    return c


CONST_SHAPES = None


def build(NL=NL_DEFAULT, debug=False, NHA=16):
    nc = bass.Bass("TRN2", target_bir_lowering=False)
    hc = host_consts()

    def din(name, shape, dt=F32):
        return nc.dram_tensor(name, list(shape), dt, kind="ExternalInput").ap()

    def dout(name, shape, dt=F32):
        return nc.dram_tensor(name, list(shape), dt, kind="ExternalOutput").ap()

    xp = din("xp", [S, DM])
    xs = din("xs", [4, DM])
    ckv = [din("ckv%d" % g, [2, LBUF[g], 4096]) for g in range(3)]
    stb_in = din("stb", [8, 256, 512])
    stc_in = din("stc", [4, 256, 512])
    norm_g = din("norm_g", [4, DM])
    final_g = din("final_g", [DM])
    w_in_a = din("w_in_a", [2, DM, 20480])
    w_out_a = din("w_out_a", [2, 2048, DM])
    w_in_b = din("w_in_b", [DM, 12288])
    gn_b = din("gn_b", [8, 512])
    w_out_b = din("w_out_b", [4096, DM])
    w_in_c = din("w_in_c", [DM, 6160])
    w_gate2 = din("w_gate2", [16, 1024])
    b_gate = din("b_gate", [1, 1024])
    gn_c = din("gn_c", [4, 512])
    w_out_c = din("w_out_c", [2048, DM])
    cd = {}
    for kk, v in hc.items():
        cd[kk] = din("c_" + kk, v.shape, BF16 if v.dtype == ml_dtypes.bfloat16 else F32)

    y_p = dout("y_p", [S, DM])
    y_s = dout("y_s", [4, DM])
    p_kv = [dout("p_kv%d" % g, [2, LBUF[g] if LBUF[g] < S else S, 4096]) for g in range(3)]
    p_stb = dout("p_stb", [8, 256, 512])
    p_stc = dout("p_stc", [4, 256, 512])
    s_kv = dout("s_kv", [2, 3, 4, 4096])
    s_stb = dout("s_stb", [8, 256, 512])
    s_stc = dout("s_stc", [4, 256, 512])
    xres = nc.dram_tensor("xres", [SS, DM], F32, kind="Internal").ap()
    dbg_x = dout("dbg_x", [SS, DM]) if debug else None
    dbg_acc = dout("dbg_acc", [128, 2, SS]) if debug else None
    dbg_vg = dout("dbg_vg", [128, 17, 128], BF16) if debug else None
    dbg_qkT = dout("dbg_qkT", [128, 2, 17, 128], BF16) if debug else None
    dbg_yT = dout("dbg_yT", [128, 2, SS], BF16) if debug else None
    hTd = nc.dram_tensor("hTd", [17, 128, 16, 128], BF16, kind="Internal").ap()

    P = Prog(nc)
    ctxs = []

    def sb(name, shape, dt):
        g = nc.sbuf_tensor(name, list(shape), dt)
        t = g.__enter__()
        ctxs.append(g)
        return t

    def ps(name, shape, dt):
        g = nc.psum_tensor(name, list(shape), dt)
        t = g.__enter__()
        ctxs.append(g)
        return t

    pj = [ps("pj0", [128, 512], F32), ps("pj1", [128, 512], F32)]
    ptb = ps("ptb", [128, 512], BF16)
    psc = [ps("psc0", [128, 512], F32), ps("psc1", [128, 512], F32)]
    pod = [ps("pod0", [128, 512], F32), ps("pod1", [128, 512], F32)]
    pg = ps("pg", [128, 512], F32)

    ident = sb("ident", [128, 128], BF16)
    maskAB = sb("maskAB", [128, 512], BF16)
    maskBf = sb("maskBf", [128, 128], F32)
    ones_bf = sb("ones_bf", [128, 128], BF16)
    ones_f = sb("ones_f", [128, 8], F32)
    cosA = sb("cosA", [128, 3, 17, 16], F32)
    sinA = sb("sinA", [128, 3, 17, 16], F32)
    decB = sb("decB", [128, 2, 8, 3], F32)
    smask = sb("smask", [128, 9, 4], BF16)
    mnew = sb("mnew", [4, 3, 4], BF16)
    triU = sb("triU", [128, 128], F32)
    gvec = sb("gvec", [128, DM], F32)
    for nm, t in (("ident", ident), ("maskAB", maskAB), ("maskBf", maskBf), ("ones_bf", ones_bf), ("ones_f", ones_f),
                  ("cosA", cosA), ("sinA", sinA), ("decB", decB), ("smask", smask), ("mnew", mnew), ("triU", triU)):
        P.dma("sync", t[:], cd[nm], writes=[nm], slot="const")

    AR_BYTES = 154880
    arena = sb("arena", [128, AR_BYTES // 4], F32)
    ar_off = [0]

    def ar_reset():
        ar_off[0] = 0

    def av(shape, dt):
        n = int(np.prod(shape[1:])) * (2 if dt == BF16 else 4)
        n4 = (n + 3) // 4
        o = ar_off[0]
        assert o + n4 <= AR_BYTES // 4, ("arena overflow", o, n4)
        ar_off[0] = o + ((n4 + 7) // 8) * 8
        v = arena[0:shape[0], o:o + n4]
        if dt == BF16:
            v = v.bitcast(BF16)
        if len(shape) == 3:
            v = v.rearrange("p (a b) -> p a b", b=shape[2])
        elif len(shape) == 4:
            v = v.rearrange("p (a b c) -> p a b c", b=shape[2], c=shape[3])
        return v

    xts = [sb("xt0", [128, DM], F32), sb("xt1", [128, DM], F32)]
    hn = sb("hn", [128, DM], BF16)
    st4 = sb("st4", [128, 8], F32)
    qkf = [sb("qkf0", [128, 512], F32), sb("qkf1", [128, 256], F32)]
    vf = [sb("vf0", [128, 128], F32), sb("vf1", [128, 128], F32)]
    qkb = [sb("qkb0", [128, 1024], BF16), sb("qkb1", [128, 256], BF16)]
    rt = sb("rt", [128, 6, 256], F32)
    PT = [sb("PT0", [128, 512], BF16), sb("PT1", [128, 512], BF16)]
    sg = sb("sg", [128, 512], F32)
    rc = sb("rc", [128, 512], F32)

    HT_ALL = ["hT%d" % t for t in range(17)]
    scale_a = 1.0 / math.sqrt(128.0)
    cnt = {"pj": 0, "att": 0, "ptb": 0, "t": 0}

    def rms_to(nt, gsrc_key, out_ap, out_key, xt, xk, extra_reads=()):
        P.op("scalar", lambda e: e.activation(out=hn[0:nt, :], in_=xt[0:nt, :], func=AF.Square, accum_out=st4[0:nt, 0:1]),
             reads=[xk], writes=["hn", "st4"])
        P.op("scalar", lambda e: e.activation(out=st4[0:nt, 1:2], in_=st4[0:nt, 0:1], func=AF.Sqrt, scale=1.0 / DM, bias=EPS),
             reads=["st4"], writes=["st4"])
        P.op("vector", lambda e: e.reciprocal(out=st4[0:nt, 2:3], in_=st4[0:nt, 1:2]), reads=["st4"], writes=["st4"])
        P.op("vector", lambda e: e.scalar_tensor_tensor(out=out_ap, in0=xt[0:nt, :], scalar=st4[0:nt, 2:3], in1=gvec[0:nt, :],
                                                       op0=ALU.mult, op1=ALU.mult),
             reads=[xk, "st4", gsrc_key] + list(extra_reads), writes=[out_key])

    def load_gvec(src_row_ap):
        P.dma("sync", gvec[:], src_row_ap.partition_broadcast(128), writes=["gvec"], slot="gvec")

    def produce_h(t, nt, hT, to_dram):
        for q4 in range(4):
            sl = 0
            cnt["ptb"] += 1
            for j in range(4):
                kc = q4 * 4 + j
                P.op("tensor", lambda e, kc=kc, j=j, sl=sl: e.transpose(out=ptb[:, sl * 512 + j * 128: sl * 512 + j * 128 + max(nt, 32)],
                                                                         in_=hn[0:max(nt, 32), kc * 128:(kc + 1) * 128], identity=ident[0:max(nt, 32), 0:max(nt, 32)]),
                     reads=["hn", "ident"], writes=["ptb%d" % sl])
            src = ptb[:, sl * 512:(sl + 1) * 512].rearrange("p (a t) -> p a t", t=128)[:, :, 0:nt]
            if not to_dram:
                dst = hT[:, q4 * 4:(q4 + 1) * 4, t * 128:t * 128 + nt]
                P.op("scalar", lambda e, dst=dst, src=src: e.copy(out=dst, in_=src), reads=["ptb%d" % sl], writes=["hT%d" % t])
            else:
                dst = hT[:, q4 * 4:(q4 + 1) * 4, 0:nt]
                P.op("scalar", lambda e, dst=dst, src=src: e.copy(out=dst, in_=src), reads=["ptb%d" % sl], writes=["hTst"])
        if to_dram:
            P.dma("sync", hTd[t, :, :, 0:nt], hT[:, :, 0:nt], reads=["hTst"], writes=["hTd%d" % t], slot="hTd")

    def wout_pass(li, yT, wo, nk, first, last, next_kind, hT_next):
        def emit_load(t):
            nt = 128 if t < 16 else 4
            r0 = t * 128
            xt = xts[t % 2]
            xk = "xt%d" % (t % 2)
            if li == 0 and first:
                src = xp[r0:r0 + nt, :] if t < 16 else xs[:, :]
                P.dma("sync", xt[0:nt, :], src, writes=[xk], slot=xk)
            else:
                P.dma("sync", xt[0:nt, :], xres[r0:r0 + nt, :], reads=["xres%d" % t], writes=[xk], slot=xk)

        emit_load(0)
        for t in range(17):
            nt = 128 if t < 16 else 4
            r0 = t * 128
            xt = xts[t % 2]
            xk = "xt%d" % (t % 2)
            if t + 1 < 17:
                emit_load(t + 1)
            banks = [pj[0], pj[1], psc[0], psc[1]]
            bkeys = ["pj0", "pj1", "psc0", "psc1"]
            for cb in range(4):
                for k in range(nk):
                    P.op("tensor", lambda e, cb=cb, k=k, nt=nt, r0=r0, banks=banks: e.matmul(banks[cb][0:nt, :], lhsT=yT[:, k, r0:r0 + nt],
                                                                   rhs=wo[:, k, cb * 512:(cb + 1) * 512], start=(k == 0), stop=(k == nk - 1)),
                         reads=["yT", "wo"], writes=[bkeys[cb]])
            for cb in range(4):
                P.op("vector", lambda e, cb=cb, nt=nt, banks=banks, xt=xt: e.tensor_tensor(out=xt[0:nt, cb * 512:(cb + 1) * 512], in0=xt[0:nt, cb * 512:(cb + 1) * 512],
                                                                 in1=banks[cb][0:nt, :], op=ALU.add),
                     reads=[xk, bkeys[cb]], writes=[xk])
            is_final = last and (li == NL - 1)
            if not is_final:
                P.dma("sync", xres[r0:r0 + nt, :], xt[0:nt, :], reads=[xk], writes=["xres%d" % t], slot="xres")
            if last:
                if is_final:
                    if debug:
                        P.dma("sync", dbg_x[r0:r0 + nt, :], xt[0:nt, :], reads=[xk], slot="dbg")
                    rms_to(nt, "gvec", xt[0:nt, :], xk, xt, xk)
                    dst = y_p[r0:r0 + nt, :] if t < 16 else y_s[:, :]
                    P.dma("sync", dst, xt[0:nt, :], reads=[xk], slot="yout")
                else:
                    rms_to(nt, "gvec", hn[0:nt, :], "hn", xt, xk, extra_reads=["hn"])
                    produce_h(t, nt, hT_next, to_dram=(next_kind != 0))

    def layer_A(li, slot):
        ar_reset()
        hT = av([128, 16, SS], BF16)
        wb = [av([128, 16, 384], BF16) for _ in range(2)]
        wo = av([128, 4, DM], BF16)
        yT = av([128, 4, SS], BF16)
        qkT = av([128, 2, 17, 128], BF16)
        vg = av([128, 17, 128], BF16)
        o_acc = ar_off[0]
        acc = av([128, 2, SS], F32)
        ck = av([128, 4, 2, 128], BF16)
        ckT = av([128, 128], BF16)
        PTs = av([128, 32], BF16)
        hst = arena[0:128, o_acc:o_acc + 1024].bitcast(BF16).rearrange("p (a b) -> p a b", b=128)
        return dict(hT=hT, wb=wb, wo=wo, yT=yT, qkT=qkT, vg=vg, acc=acc, ck=ck, ckT=ckT, PTs=PTs, hst=hst)

    def run_layer_A(li, slot, V, next_kind, hT_next):
        hT, wbs, wo, yT, qkT, vg, acc, ck, ckT, PTs = (V[k] for k in ("hT", "wb", "wo", "yT", "qkT", "vg", "acc", "ck", "ckT", "PTs"))
        wsrc = w_in_a[slot].rearrange("(kc p) (s c) -> p kc s c", p=128, c=2048)
        wi = 0
        for h in range(NHA):
            hs = h % 4
            for g in range(3):
                dil = DILS[g]
                nb = 16 // dil
                bi = wi % 2
                wi += 1
                wb = wbs[bi]
                wkeys = ["wb%d_%d" % (bi, kq) for kq in range(3)]
                for kq in range(3):
                    P.dma("gpsimd", wb[:, :, kq * 128:(kq + 1) * 128],
                          wsrc[:, :, 3 * g + kq, h * 128:(h + 1) * 128], writes=[wkeys[kq]], slot=wkeys[kq])
                pendT = None
                for tau in range(17):
                    nt = 128 if tau < 16 else 4
                    if tau < 16:
                        r, n = divmod(tau, nb)
                        st0 = n * 128 * dil + r
                        tsl = slice(st0, st0 + 127 * dil + 1, dil)
                    else:
                        tsl = slice(S, S + 4)
                    pi = cnt["pj"] % 2
                    cnt["pj"] += 1
                    pjt = pj[pi]
                    for kc in range(16):
                        P.op("tensor", lambda e, kc=kc, tsl=tsl, pjt=pjt, wb=wb, nt=nt: e.matmul(pjt[0:nt, 0:384], lhsT=hT[:, kc, tsl], rhs=wb[:, kc, :],
                                                                                          start=(kc == 0), stop=(kc == 15)),
                             reads=HT_ALL + wkeys, writes=["pj%d" % pi])
                    ti = cnt["t"] % 2
                    cnt["t"] += 1
                    qf = qkf[ti]
                    P.op("scalar", lambda e, qf=qf, pjt=pjt, nt=nt: e.copy(out=qf[0:nt, 0:256], in_=pjt[0:nt, 0:256]),
                         reads=["pj%d" % pi], writes=["qkf%d" % ti])
                    P.op("vector", lambda e, pjt=pjt, nt=nt, tau=tau: e.tensor_copy(out=vg[0:nt, tau, :], in_=pjt[0:nt, 256:384]),
                         reads=["pj%d" % pi], writes=["vg%d" % tau])
                    if tau == 16:
                        need = True
                    elif g == 0:
                        need = (tau == 15)
                    elif g == 1:
                        need = (tau % nb == nb - 1)
                    else:
                        need = True
                    if need:
                        P.op("scalar", lambda e, pjt=pjt, nt=nt, ti=ti: e.copy(out=vf[ti][0:nt, :], in_=pjt[0:nt, 256:384]),
                             reads=["pj%d" % pi], writes=["vf%d" % ti])
                    q3 = qf[0:nt, 0:256].rearrange("p (a d) -> p a d", d=128)
                    x1 = q3[:, :, 0:16]
                    x2 = q3[:, :, 16:32]
                    cs = cosA[0:nt, g, tau:tau + 1, :].broadcast_to([nt, 2, 16])
                    sn = sinA[0:nt, g, tau:tau + 1, :].broadcast_to([nt, 2, 16])
                    tv = [rt[0:nt, i, 0:32].rearrange("p (a d) -> p a d", d=16) for i in range(4)]
                    kq_ = "qkf%d" % ti
                    P.op("gpsimd", lambda e, tv=tv, x1=x1, cs=cs: e.tensor_tensor(out=tv[0], in0=x1, in1=cs, op=ALU.mult), reads=[kq_, "cosA"], writes=["rt0"])
                    P.op("gpsimd", lambda e, tv=tv, x2=x2, sn=sn: e.tensor_tensor(out=tv[1], in0=x2, in1=sn, op=ALU.mult), reads=[kq_, "sinA"], writes=["rt1"])
                    P.op("gpsimd", lambda e, tv=tv, x1=x1, sn=sn: e.tensor_tensor(out=tv[2], in0=x1, in1=sn, op=ALU.mult), reads=[kq_, "sinA"], writes=["rt2"])
                    P.op("gpsimd", lambda e, tv=tv, x2=x2, cs=cs: e.tensor_tensor(out=tv[3], in0=x2, in1=cs, op=ALU.mult), reads=[kq_, "cosA"], writes=["rt3"])
                    P.op("gpsimd", lambda e, tv=tv, x1=x1: e.tensor_tensor(out=x1, in0=tv[0], in1=tv[1], op=ALU.subtract), reads=["rt0", "rt1", kq_], writes=[kq_])
                    P.op("gpsimd", lambda e, tv=tv, x2=x2: e.tensor_tensor(out=x2, in0=tv[2], in1=tv[3], op=ALU.add), reads=["rt2", "rt3", kq_], writes=[kq_])
                    qb = qkb[ti]
                    P.op("vector", lambda e, qb=qb, qf=qf, nt=nt: e.tensor_copy(out=qb[0:nt, 0:256], in_=qf[0:nt, 0:256]),
                         reads=[kq_], writes=["qkb%d" % ti])
                    def emit_T(tau, nt, ti, qb):
                        sl = 0
                        for a in range(2):
                            P.op("tensor", lambda e, a=a, sl=sl, qb=qb, nt=nt: e.transpose(out=ptb[:, sl * 512 + a * 128: sl * 512 + a * 128 + max(nt, 32)],
                                                                                        in_=qb[0:max(nt, 32), a * 128:(a + 1) * 128], identity=ident[0:max(nt, 32), 0:max(nt, 32)]),
                                 reads=["qkb%d" % ti, "ident"], writes=["ptb%d" % sl])
                        srcT = ptb[:, sl * 512:sl * 512 + 256].rearrange("p (a t) -> p a t", t=128)[:, :, 0:nt]
                        P.op("scalar", lambda e, srcT=srcT, tau=tau, nt=nt: e.copy(out=qkT[:, :, tau, 0:nt], in_=srcT),
                             reads=["ptb%d" % sl], writes=["qkT%d" % tau])
                    if pendT is not None:
                        emit_T(*pendT)
                    pendT = (tau, nt, ti, qb)
                    if need:
                        if tau == 16:
                            dk = s_kv[slot, g, :, h * 128:(h + 1) * 128]
                            dv = s_kv[slot, g, :, 2048 + h * 128:2048 + (h + 1) * 128]
                        else:
                            r, n = divmod(tau, nb)
                            keep = min(LBUF[g], S)
                            base = (n * 128 * dil + r) - (S - keep)
                            rows = slice(base, base + 127 * dil + 1, dil)
                            dk = p_kv[g][slot, rows, h * 128:(h + 1) * 128]
                            dv = p_kv[g][slot, rows, 2048 + h * 128:2048 + (h + 1) * 128]
                        P.dma("sync", dk, qf[0:nt, 128:256], reads=[kq_], slot="okv%d" % ti)
                        P.dma("sync", dv, vf[ti][0:nt, :], reads=["vf%d" % ti], slot="okv%d" % ti)
                emit_T(*pendT)
                def emit_S(pair):
                    ai = cnt["att"] % 2
                    cnt["att"] += 1
                    pst, pot, ptt = psc[ai], pod[ai], PT[ai]
                    ks_, ko_, kp_ = "psc%d" % ai, "pod%d" % ai, "PT%d" % ai
                    for b in range(2):
                        tau = 2 * pair + b
                        r, n = divmod(tau, nb)
                        if n > 0:
                            P.op("tensor", lambda e, tau=tau, b=b, pst=pst: e.matmul(pst[:, (2 * b) * 128:(2 * b + 1) * 128], lhsT=qkT[:, 1, tau - 1, :],
                                                                                   rhs=qkT[:, 0, tau, :], start=True, stop=True),
                                 reads=["qkT%d" % (tau - 1), "qkT%d" % tau], writes=[ks_])
                        P.op("tensor", lambda e, tau=tau, b=b, pst=pst: e.matmul(pst[:, (2 * b + 1) * 128:(2 * b + 2) * 128], lhsT=qkT[:, 1, tau, :],
                                                                               rhs=qkT[:, 0, tau, :], start=True, stop=True),
                             reads=["qkT%d" % tau], writes=[ks_])
                    P.op("scalar", lambda e, pst=pst, ptt=ptt: e.activation(out=ptt[:, :], in_=pst[:, :], func=AF.Exp, scale=scale_a),
                         reads=[ks_], writes=[kp_])
                    P.op("vector", lambda e, ptt=ptt: e.tensor_tensor(out=ptt[:, :], in0=ptt[:, :], in1=maskAB[:, :], op=ALU.mult),
                         reads=[kp_, "maskAB"], writes=[kp_])
                    P.op("vector", lambda e, pot=pot: e.memset(pot[:, :], 0.0), writes=[ko_])
                    return (pot, ptt, ko_, kp_)

                def emit_PV(pair, pot, ptt, ko_, kp_):
                    for b in range(2):
                        tau = 2 * pair + b
                        r, n = divmod(tau, nb)
                        oreg = pot[:, b * 256:b * 256 + 128]
                        dreg = pot[:, b * 256 + 128:b * 256 + 256]
                        parts = ([(tau - 1, 2 * b)] if n > 0 else []) + [(tau, 2 * b + 1)]
                        for (kt, pslot) in parts:
                            P.op("tensor", lambda e, kt=kt, pslot=pslot, oreg=oreg, ptt=ptt: e.matmul(oreg, lhsT=vg[:, kt, :], rhs=ptt[:, pslot * 128:(pslot + 1) * 128],
                                                                                                    start=False, stop=False, skip_group_check=True),
                                 reads=["vg%d" % kt, kp_, ko_], writes=[ko_])
                            P.op("tensor", lambda e, pslot=pslot, dreg=dreg, ptt=ptt: e.matmul(dreg, lhsT=ones_bf[:, :], rhs=ptt[:, pslot * 128:(pslot + 1) * 128],
                                                                                             start=False, stop=False, skip_group_check=True),
                                 reads=["ones_bf", kp_, ko_], writes=[ko_])
                    for b in range(2):
                        tau = 2 * pair + b
                        r, n = divmod(tau, nb)
                        st0 = n * 128 * dil + r
                        dst = acc[:, :, st0:st0 + 127 * dil + 1:dil]
                        src = pot[:, b * 256:(b + 1) * 256].rearrange("p (a t) -> p a t", t=128)
                        if g == 0:
                            P.op("scalar", lambda e, dst=dst, src=src: e.copy(out=dst, in_=src), reads=[ko_], writes=["acc"])
                        else:
                            P.op("vector", lambda e, dst=dst, src=src: e.tensor_tensor(out=dst, in0=dst, in1=src, op=ALU.add), reads=[ko_, "acc"], writes=["acc"])

                pendA = None
                for pair in range(8):
                    ctx = emit_S(pair)
                    if pendA is not None:
                        emit_PV(*pendA)
                    pendA = (pair,) + ctx
                emit_PV(*pendA)
                nset = 1 if g == 0 else 4
                for kv in range(2):
                    if g == 0:
                        srcc = ckv[0][slot, :, kv * 2048 + h * 128: kv * 2048 + (h + 1) * 128]
                        P.dma("gpsimd", ck[:, 0, kv, :], srcc, writes=["ck%d" % kv], slot="ck%d" % kv)
                    else:
                        srcc = ckv[g][slot, :, :].rearrange("(i t) c -> i t c", t=dil)[:, 0:4, kv * 2048 + h * 128: kv * 2048 + (h + 1) * 128]
                        P.dma("gpsimd", ck[:, :, kv, :], srcc, writes=["ck%d" % kv], slot="ck%d" % kv)
                ai = cnt["att"] % 2
                cnt["att"] += 1
                pst, pot = psc[ai], pod[ai]
                ks_, ko_ = "psc%d" % ai, "pod%d" % ai
                P.op("vector", lambda e, pot=pot: e.memset(pot[:, 0:8], 0.0), writes=[ko_])
                for s_ in range(nset):
                    sl = 0
                    cnt["ptb"] += 1
                    P.op("tensor", lambda e, s_=s_, sl=sl: e.transpose(out=ptb[:, sl * 512:sl * 512 + 128], in_=ck[:, s_, 0, :], identity=ident[:, :]),
                         reads=["ck0", "ident"], writes=["ptb%d" % sl])
                    P.op("scalar", lambda e, sl=sl: e.copy(out=ckT[:, :], in_=ptb[:, sl * 512:sl * 512 + 128]), reads=["ptb%d" % sl], writes=["ckT"])
                    P.op("tensor", lambda e, s_=s_, pst=pst: e.matmul(pst[:, s_ * 4:s_ * 4 + 4], lhsT=ckT[:, :], rhs=qkT[:, 0, 16, 0:4], start=True, stop=True),
                         reads=["ckT", "qkT16"], writes=[ks_])
                    P.op("scalar", lambda e, s_=s_, pst=pst: e.activation(out=PTs[:, s_ * 4:s_ * 4 + 4], in_=pst[:, s_ * 4:s_ * 4 + 4], func=AF.Exp, scale=scale_a),
                         reads=[ks_], writes=["PTs"])
                    mi = 0 if g == 0 else 1 + s_
                    P.op("vector", lambda e, s_=s_, mi=mi: e.tensor_tensor(out=PTs[:, s_ * 4:s_ * 4 + 4], in0=PTs[:, s_ * 4:s_ * 4 + 4], in1=smask[:, mi, :], op=ALU.mult),
                         reads=["PTs", "smask"], writes=["PTs"])
                    P.op("tensor", lambda e, s_=s_, pot=pot: e.matmul(pot[:, 0:4], lhsT=ck[:, s_, 1, :], rhs=PTs[:, s_ * 4:s_ * 4 + 4], start=False, stop=False, skip_group_check=True),
                         reads=["ck1", "PTs", ko_], writes=[ko_])
                    P.op("tensor", lambda e, s_=s_, pot=pot: e.matmul(pot[:, 4:8], lhsT=ones_bf[:, :], rhs=PTs[:, s_ * 4:s_ * 4 + 4], start=False, stop=False, skip_group_check=True),
                         reads=["ones_bf", "PTs", ko_], writes=[ko_])
                P.op("tensor", lambda e, pst=pst: e.matmul(pst[0:4, 16:20], lhsT=qkT[:, 1, 16, 0:4], rhs=qkT[:, 0, 16, 0:4], start=True, stop=True),
                     reads=["qkT16"], writes=[ks_])
                P.op("scalar", lambda e, pst=pst: e.activation(out=PTs[0:4, 16:20], in_=pst[0:4, 16:20], func=AF.Exp, scale=scale_a), reads=[ks_], writes=["PTs2"])
                P.op("vector", lambda e, g=g: e.tensor_tensor(out=PTs[0:4, 16:20], in0=PTs[0:4, 16:20], in1=mnew[0:4, g, :], op=ALU.mult),
                     reads=["PTs2", "mnew"], writes=["PTs2"])
                P.op("tensor", lambda e, pot=pot: e.matmul(pot[:, 0:4], lhsT=vg[0:4, 16, :], rhs=PTs[0:4, 16:20], start=False, stop=False, skip_group_check=True),
                     reads=["vg16", "PTs2", ko_], writes=[ko_])
                P.op("tensor", lambda e, pot=pot: e.matmul(pot[:, 4:8], lhsT=ones_bf[0:4, :], rhs=PTs[0:4, 16:20], start=False, stop=False, skip_group_check=True),
                     reads=["ones_bf", "PTs2", ko_], writes=[ko_])
                dst = acc[:, :, S:S + 4]
                src = pot[:, 0:8].rearrange("p (a t) -> p a t", t=4)
                if g == 0:
                    P.op("scalar", lambda e, dst=dst, src=src: e.copy(out=dst, in_=src), reads=[ko_], writes=["acc"])
                else:
                    P.op("vector", lambda e, dst=dst, src=src: e.tensor_tensor(out=dst, in0=dst, in1=src, op=ALU.add), reads=[ko_, "acc"], writes=["acc"])
            if debug and h == 0:
                P.dma("sync", dbg_acc, acc[:, :, :], reads=["acc"], slot="dbg")
                P.dma("sync", dbg_vg, vg[:, :, :], reads=["vg%d" % i for i in range(17)], slot="dbg")
                P.dma("sync", dbg_qkT.rearrange("p a t d -> p (a t) d"), qkT[:, :, :, :].rearrange("p a t d -> p (a t) d"), reads=["qkT%d" % i for i in range(17)], slot="dbg")
            bi = wi % 2
            wi += 1
            wgt = wbs[bi]
            gk = "wb%d_0" % bi
            P.dma("gpsimd", wgt[:, :, 0:128], wsrc[:, :, 9, h * 128:(h + 1) * 128], writes=[gk], slot=gk)
            for c in range(5):
                c0 = c * 512
                n_ = 512 if c < 4 else 4
                for kc in range(16):
                    P.op("tensor", lambda e, kc=kc, c0=c0, n_=n_, wgt=wgt: e.matmul(pg[:, 0:n_], lhsT=wgt[:, kc, 0:128], rhs=hT[:, kc, c0:c0 + n_],
                                                                                  start=(kc == 0), stop=(kc == 15)),
                         reads=HT_ALL + [gk], writes=["pg"])
                P.op("scalar", lambda e, n_=n_: e.activation(out=sg[:, 0:n_], in_=pg[:, 0:n_], func=AF.Silu), reads=["pg"], writes=["sg"])
                P.op("vector", lambda e, c0=c0, n_=n_: e.reciprocal(out=rc[:, 0:n_], in_=acc[:, 1, c0:c0 + n_]), reads=["acc"], writes=["rc"])
                P.op("vector", lambda e, c0=c0, n_=n_: e.tensor_tensor(out=rc[:, 0:n_], in0=rc[:, 0:n_], in1=acc[:, 0, c0:c0 + n_], op=ALU.mult),
                     reads=["rc", "acc"], writes=["rc"])
                P.op("vector", lambda e, c0=c0, n_=n_, hs=hs: e.tensor_tensor(out=yT[:, hs, c0:c0 + n_], in0=rc[:, 0:n_], in1=sg[:, 0:n_], op=ALU.mult),
                     reads=["rc", "sg"], writes=["yT"])
            if hs == 3:
                P.dma("gpsimd", wo[:, :, :], w_out_a[slot, (h - 3) * 128:(h + 1) * 128, :].rearrange("(k p) n -> p k n", p=128), writes=["wo"], slot="wo")
                if h == NHA - 1:
                    load_gvec(norm_g[li + 1, :] if li < NL - 1 else final_g)
                if h == NHA - 1 and next_kind != 0 and li < NL - 1:
                    P.barrier()
                wout_pass(li, yT, wo, 4, first=(h == 3), last=(h == NHA - 1), next_kind=next_kind, hT_next=hT_next)

    def views_BC():
        ar_reset()
        V = {}
        V["hTt"] = [av([128, 16, 128], BF16) for _ in range(2)]
        V["hTst"] = av([128, 16, 128], BF16)
        V["wb"] = [av([128, 16, 512], BF16) for _ in range(3)]
        V["wo"] = av([128, 4, DM], BF16)
        V["yT"] = av([128, 4, SS], BF16)
        V["stf"] = [av([128, 2, 512], F32) for _ in range(2)]
        V["stb"] = [av([128, 2, 512], BF16) for _ in range(2)]
        V["cs"] = [av([128, 2, 128], F32) for _ in range(2)]
        V["qkT"] = av([128, 4, 128], BF16)
        V["vb"] = [av([128, 512], BF16) for _ in range(2)]
        V["scT"] = av([128, 128], BF16)
        V["of"] = av([128, 512], F32)
        V["gf"] = [av([128, 512], F32) for _ in range(2)]
        V["gnv"] = av([128, 512], F32)
        V["yb"] = [av([128, 512], BF16) for _ in range(2)]
        V["qb"] = [av([128, 768], BF16) for _ in range(2)]
        V["bst"] = av([128, 4, 6], F32)
        V["mv"] = av([128, 8], F32)
        V["wlr"] = av([128, 16, 16], BF16)
        V["lrT"] = av([32, 128], F32)
        V["wg2"] = av([17, 1024], F32)
        V["sp"] = av([128, 256], F32)
        V["eb"] = av([128, 2, 256], F32)
        V["ebl"] = av([128, 2], F32)
        return V

    def run_layer_BC(li, kind, V, next_kind, hT_next):
        NH = 8 if kind == 1 else 4
        w_in = w_in_b if kind == 1 else w_in_c
        w_out = w_out_b if kind == 1 else w_out_c
        gn = gn_b if kind == 1 else gn_c
        wsrc = w_in.rearrange("(kc p) n -> p kc n", p=128)
        hTt, wbs, wo, yT = V["hTt"], V["wb"], V["wo"], V["yT"]
        stf, stbf, cs, qkT, vbs, scT = V["stf"], V["stb"], V["cs"], V["qkT"], V["vb"], V["scT"]
        qbs = V["qb"]
        of, gfs, gnv, ybs, bst, mv = V["of"], V["gf"], V["gnv"], V["yb"], V["bst"], V["mv"]
        wlr, lrT, wg2, sp, eb, ebl = V["wlr"], V["lrT"], V["wg2"], V["sp"], V["eb"], V["ebl"]
        lgam = [math.log(1.0 - 2.0 ** (-5.0 - hh)) for hh in range(8)]
        if kind == 2:
            P.dma("gpsimd", wlr[:, :, :], wsrc[:, :, 6144:6160], writes=["wlr"], slot="wlr")
            P.dma("sync", wg2[0:16, :], w_gate2, writes=["wg2a"], slot="wg2")
            P.dma("sync", wg2[16:17, :], b_gate, writes=["wg2b"], slot="wg2")
        hcnt = 0
        for h in range(NH):
            if kind == 1:
                cols = [(h * 256, 256, 0), (2048 + h * 256, 256, 256)]
                vcol = 4096 + h * 512
                gcol = 8192 + h * 512
            else:
                cols = [(h * 256, 256, 0), (1024 + h * 256, 256, 256)]
                vcol = 2048 + h * 512
                gcol = 4096 + h * 512
            for (c0, n_, d0) in cols:
                for kq in range(2):
                    P.dma("gpsimd", wbs[0][:, kq * 8:(kq + 1) * 8, d0:d0 + n_], wsrc[:, kq * 8:(kq + 1) * 8, c0:c0 + n_], writes=["wq%d_%d" % (d0, kq)], slot="wq%d_%d" % (d0, kq))
            for kq in range(2):
                P.dma("gpsimd", wbs[1][:, kq * 8:(kq + 1) * 8, :], wsrc[:, kq * 8:(kq + 1) * 8, vcol:vcol + 512], writes=["wv_%d" % kq], slot="wv_%d" % kq)
                P.dma("gpsimd", wbs[2][:, kq * 8:(kq + 1) * 8, :], wsrc[:, kq * 8:(kq + 1) * 8, gcol:gcol + 512], writes=["wg_%d" % kq], slot="wg_%d" % kq)
            WQK = ["wq0_0", "wq0_1", "wq256_0", "wq256_1"]
            P.dma("sync", gnv[:, :], gn[h, :].partition_broadcast(128), writes=["gnv"], slot="gnv")
            P.op("gpsimd", lambda e: e.memset(stf[0][:, :, :], 0.0), writes=["stf0"])
            P.op("gpsimd", lambda e: e.memset(stbf[0][:, :, :], 0.0), writes=["stb0"])
            st_in = stb_in if kind == 1 else stc_in
            P.dma("sync", stf[1][:, :, :], st_in[h].rearrange("(c p) v -> p c v", p=128), writes=["stf1"], slot="stin")
            P.dma("gpsimd", stbf[1][:, :, :], st_in[h].rearrange("(c p) v -> p c v", p=128), writes=["stb1"], slot="stin2")
            def f_proj(t):
                    nt = 128 if t < 16 else 4
                    si = 0 if t < 16 else 1
                    hi = t % 2
                    par = t % 2
                    ht = hTt[hi]
                    hk = "hTt%d" % hi
                    vb = vbs[par]; gf = gfs[par]; yb = ybs[par]; qb = qbs[par]
                    kvb = "vb%d" % par; kgf = "gf%d" % par; kyb = "yb%d" % par; kqb = "qbB%d" % par
                    gC = math.exp(lgam[h] * (128.0 if t < 16 else 4.0)) if kind == 1 else 1.0

                    P.dma("sync", ht[:, :, 0:nt], hTd[t, :, :, 0:nt], reads=["hTd%d" % t], writes=[hk], slot=hk)
                    if kind == 1:
                        P.dma("sync", cs[hi][0:nt, 0, :], cd["cosB"][t, 0:nt, :], writes=["cs%da" % hi], slot="cs%da" % hi)
                        P.dma("sync", cs[hi][0:nt, 1, :], cd["sinB"][t, 0:nt, :], writes=["cs%db" % hi], slot="cs%db" % hi)
                    for kc in range(16):
                        P.op("tensor", lambda e, kc=kc, ht=ht, nt=nt: e.matmul(pj[0][0:nt, :], lhsT=ht[:, kc, 0:nt], rhs=wbs[0][:, kc, :], start=(kc == 0), stop=(kc == 15)),
                             reads=[hk] + WQK, writes=["pj0"])
                    for kc in range(16):
                        P.op("tensor", lambda e, kc=kc, ht=ht, nt=nt: e.matmul(pj[1][0:nt, :], lhsT=ht[:, kc, 0:nt], rhs=wbs[1][:, kc, :], start=(kc == 0), stop=(kc == 15)),
                             reads=[hk, "wv_0", "wv_1"], writes=["pj1"])
                    for kc in range(16):
                        P.op("tensor", lambda e, kc=kc, ht=ht, nt=nt: e.matmul(pg[0:nt, :], lhsT=ht[:, kc, 0:nt], rhs=wbs[2][:, kc, :], start=(kc == 0), stop=(kc == 15)),
                             reads=[hk, "wg_0", "wg_1"], writes=["pg"])

            def f_evac(t):
                    nt = 128 if t < 16 else 4
                    si = 0 if t < 16 else 1
                    hi = t % 2
                    par = t % 2
                    ht = hTt[hi]
                    hk = "hTt%d" % hi
                    vb = vbs[par]; gf = gfs[par]; yb = ybs[par]; qb = qbs[par]
                    kvb = "vb%d" % par; kgf = "gf%d" % par; kyb = "yb%d" % par; kqb = "qbB%d" % par
                    gC = math.exp(lgam[h] * (128.0 if t < 16 else 4.0)) if kind == 1 else 1.0

                    P.op("vector", lambda e, nt=nt: e.tensor_copy(out=vb[0:nt, :], in_=pj[1][0:nt, :]), reads=["pj1"], writes=[kvb])
                    P.op("scalar", lambda e, nt=nt: e.activation(out=gf[0:nt, :], in_=pg[0:nt, :], func=AF.Silu), reads=["pg"], writes=[kgf])
                    if kind == 1:
                        qf = qkf[0]
                        P.op("scalar", lambda e, nt=nt, qf=qf: e.copy(out=qf[0:nt, :], in_=pj[0][0:nt, :]), reads=["pj0"], writes=["qkf0"])
                        q4 = qf[0:nt, :].rearrange("p (a b d) -> p a b d", a=2, b=2)
                        x1 = q4[:, :, 0, :]
                        x2 = q4[:, :, 1, :]
                        cs_ = cs[hi][0:nt, 0:1, :].broadcast_to([nt, 2, 128])
                        sn_ = cs[hi][0:nt, 1:2, :].broadcast_to([nt, 2, 128])
                        tv = [rt[0:nt, i, :].rearrange("p (a d) -> p a d", d=128) for i in range(6)]
                        ck_ = "cs%d" % hi
                        P.op("gpsimd", lambda e, tv=tv, x1=x1, cs_=cs_: e.tensor_tensor(out=tv[0], in0=x1, in1=cs_, op=ALU.mult), reads=["qkf0", ck_ + "a", ck_ + "b"], writes=["rt0"])
                        P.op("gpsimd", lambda e, tv=tv, x2=x2, sn_=sn_: e.tensor_tensor(out=tv[1], in0=x2, in1=sn_, op=ALU.mult), reads=["qkf0", ck_ + "a", ck_ + "b"], writes=["rt1"])
                        P.op("gpsimd", lambda e, tv=tv, x1=x1, sn_=sn_: e.tensor_tensor(out=tv[2], in0=x1, in1=sn_, op=ALU.mult), reads=["qkf0", ck_ + "a", ck_ + "b"], writes=["rt2"])
                        P.op("gpsimd", lambda e, tv=tv, x2=x2, cs_=cs_: e.tensor_tensor(out=tv[3], in0=x2, in1=cs_, op=ALU.mult), reads=["qkf0", ck_ + "a", ck_ + "b"], writes=["rt3"])
                        P.op("gpsimd", lambda e, tv=tv: e.tensor_tensor(out=tv[4], in0=tv[0], in1=tv[1], op=ALU.subtract), reads=["rt0", "rt1"], writes=["rt4"])
                        P.op("gpsimd", lambda e, tv=tv: e.tensor_tensor(out=tv[5], in0=tv[2], in1=tv[3], op=ALU.add), reads=["rt2", "rt3"], writes=["rt5"])
                        for (a, dsti, di) in ((0, 0, 0), (1, 1, 1), (1, 2, 2)):
                            for hf in range(2):
                                P.op("vector", lambda e, a=a, dsti=dsti, di=di, hf=hf, tv=tv, nt=nt, si=si, h=h: e.tensor_scalar(
                                    out=qb[0:nt, dsti * 256 + hf * 128: dsti * 256 + (hf + 1) * 128], in0=tv[4 + hf][:, a, :],
                                    scalar1=decB[0:nt, si, h, di:di + 1], scalar2=None, op0=ALU.mult),
                                    reads=["rt4", "rt5", "decB"], writes=[kqb])
                    else:
                        for kc in range(16):
                            P.op("tensor", lambda e, kc=kc, ht=ht, nt=nt: e.matmul(psc[1][0:16, 0:nt], lhsT=wlr[:, kc, :], rhs=ht[:, kc, 0:nt], start=(kc == 0), stop=(kc == 15)),
                                 reads=[hk, "wlr"], writes=["psc1"])
                        P.op("vector", lambda e: e.memset(lrT[:, :], 1.0), writes=["lrT"])
                        P.op("vector", lambda e, nt=nt: e.tensor_copy(out=lrT[0:16, 0:nt], in_=psc[1][0:16, 0:nt]), reads=["psc1", "lrT"], writes=["lrT"])
                        P.op("tensor", lambda e, nt=nt, h=h: e.matmul(pod[1][0:nt, 0:256], lhsT=lrT[0:17, 0:nt], rhs=wg2[0:17, h * 256:(h + 1) * 256], start=True, stop=True),
                             reads=["lrT", "wg2a", "wg2b"], writes=["pod1"])
                        P.op("scalar", lambda e, nt=nt: e.activation(out=sp[0:nt, :], in_=pod[1][0:nt, 0:256], func=AF.Exp, scale=-1.0), reads=["pod1"], writes=["sp"])
                        P.op("scalar", lambda e, nt=nt: e.activation(out=sp[0:nt, :], in_=sp[0:nt, :], func=AF.Ln, bias=1.0), reads=["sp"], writes=["sp"])
                        P.op("tensor", lambda e, nt=nt: e.matmul(pod[1][0:nt, 256:512], lhsT=triU[0:nt, 0:nt], rhs=sp[0:nt, :], start=True, stop=True),
                             reads=["sp", "triU", "pod1"], writes=["pod1"])
                        P.op("scalar", lambda e, nt=nt: e.activation(out=eb[0:nt, 0, :], in_=pod[1][0:nt, 256:512], func=AF.Exp, scale=-1.0 / 16.0), reads=["pod1"], writes=["eb"])
                        P.op("scalar", lambda e, nt=nt: e.activation(out=eb[0:nt, 1, :], in_=pod[1][0:nt, 256:512], func=AF.Exp, scale=1.0 / 16.0), reads=["pod1"], writes=["eb"])
                        for ch in range(2):
                            P.op("tensor", lambda e, nt=nt, ch=ch: e.matmul(psc[1][:, 128 + ch:129 + ch], lhsT=sp[0:nt, ch * 128:(ch + 1) * 128], rhs=ones_f[0:nt, 0:1], start=True, stop=True),
                                 reads=["sp", "ones_f", "psc1"], writes=["psc1"])
                        P.op("scalar", lambda e: e.activation(out=ebl[:, 0:2], in_=psc[1][:, 128:130], func=AF.Exp, scale=-1.0 / 16.0), reads=["psc1"], writes=["ebl"])
                        P.op("vector", lambda e, nt=nt: e.scalar_tensor_tensor(out=qb[0:nt, 0:256], in0=pj[0][0:nt, 0:256], scalar=1.0 / 16.0, in1=eb[0:nt, 0, :], op0=ALU.mult, op1=ALU.mult),
                             reads=["pj0", "eb"], writes=[kqb])
                        P.op("vector", lambda e, nt=nt: e.tensor_tensor(out=qb[0:nt, 256:512], in0=pj[0][0:nt, 256:512], in1=eb[0:nt, 1, :], op=ALU.mult),
                             reads=["pj0", "eb"], writes=[kqb])

            def f_rest(t):
                    nt = 128 if t < 16 else 4
                    si = 0 if t < 16 else 1
                    hi = t % 2
                    par = t % 2
                    ht = hTt[hi]
                    hk = "hTt%d" % hi
                    vb = vbs[par]; gf = gfs[par]; yb = ybs[par]; qb = qbs[par]
                    kvb = "vb%d" % par; kgf = "gf%d" % par; kyb = "yb%d" % par; kqb = "qbB%d" % par
                    gC = math.exp(lgam[h] * (128.0 if t < 16 else 4.0)) if kind == 1 else 1.0

                    kd0 = 512 if kind == 1 else 256
                    sl = 0
                    cnt["ptb"] += 1
                    for j in range(4):
                        P.op("tensor", lambda e, j=j, sl=sl, nt=nt: e.transpose(out=ptb[:, sl * 512 + j * 128: sl * 512 + j * 128 + max(nt, 32)], in_=qb[0:max(nt, 32), j * 128:(j + 1) * 128], identity=ident[0:max(nt, 32), 0:max(nt, 32)]),
                             reads=[kqb, "ident"], writes=["ptb%d" % sl])
                    srcT = ptb[:, sl * 512:(sl + 1) * 512].rearrange("p (a t) -> p a t", t=128)[:, :, 0:nt]
                    P.op("scalar", lambda e, srcT=srcT, nt=nt: e.copy(out=qkT[:, :, 0:nt], in_=srcT), reads=["ptb%d" % sl], writes=["qkT"])
                    for ch in range(2):
                        P.op("tensor", lambda e, ch=ch, nt=nt: e.matmul(psc[0][0:nt, 0:nt], lhsT=qkT[:, 2 + ch, 0:nt], rhs=qkT[:, ch, 0:nt], start=(ch == 0), stop=(ch == 1)),
                             reads=["qkT"], writes=["psc0"])
                    P.op("vector", lambda e, nt=nt: e.tensor_tensor(out=scT[0:nt, 0:nt], in0=psc[0][0:nt, 0:nt], in1=maskBf[0:nt, 0:nt], op=ALU.mult),
                         reads=["psc0", "maskBf"], writes=["scT"])
                    P.op("tensor", lambda e, nt=nt: e.matmul(pod[0][0:nt, :], lhsT=scT[0:nt, 0:nt], rhs=vb[0:nt, :], start=True, stop=False),
                         reads=["scT", kvb], writes=["pod0"])
                    for ch in range(2):
                        P.op("tensor", lambda e, ch=ch, nt=nt, si=si: e.matmul(pod[0][0:nt, :], lhsT=qkT[:, ch, 0:nt], rhs=stbf[si][:, ch, :], start=False, stop=(ch == 1)),
                             reads=["qkT", "stb%d" % si], writes=["pod0"])
                    for ch in range(2):
                        pstt = psc[1] if ch == 0 else pod[1]
                        pk = "psc1" if ch == 0 else "pod1"
                        P.op("tensor", lambda e, ch=ch, nt=nt, pstt=pstt: e.matmul(pstt[:, :], lhsT=qb[0:nt, kd0 + ch * 128: kd0 + (ch + 1) * 128], rhs=vb[0:nt, :], start=True, stop=True),
                             reads=[kqb, kvb, pk], writes=[pk])
                        if kind == 1:
                            P.op("vector", lambda e, ch=ch, si=si, pstt=pstt, gC=gC: e.scalar_tensor_tensor(out=stf[si][:, ch, :], in0=stf[si][:, ch, :], scalar=gC, in1=pstt[:, :], op0=ALU.mult, op1=ALU.add),
                                 reads=["stf%d" % si, pk], writes=["stf%d" % si])
                        else:
                            P.op("vector", lambda e, ch=ch, si=si, pstt=pstt: e.tensor_tensor(out=stf[si][:, ch, :], in0=stf[si][:, ch, :], in1=pstt[:, :], op=ALU.add),
                                 reads=["stf%d" % si, pk], writes=["stf%d" % si])
                            P.op("vector", lambda e, ch=ch, si=si: e.tensor_scalar(out=stf[si][:, ch, :], in0=stf[si][:, ch, :], scalar1=ebl[:, ch:ch + 1], scalar2=None, op0=ALU.mult),
                                 reads=["stf%d" % si, "ebl"], writes=["stf%d" % si])
                        P.op("scalar", lambda e, ch=ch, si=si: e.copy(out=stbf[si][:, ch, :], in_=stf[si][:, ch, :]), reads=["stf%d" % si], writes=["stb%d" % si])
                    if kind == 1:
                        for c4 in range(4):
                            P.op("vector", lambda e, c4=c4, nt=nt: e.bn_stats(out=bst[0:nt, c4, :], in_=pod[0][0:nt, c4 * 128:(c4 + 1) * 128]), reads=["pod0"], writes=["bst"])
                        P.op("vector", lambda e, nt=nt: e.bn_aggr(out=mv[0:nt, 0:2], in_=bst[0:nt, :, :]), reads=["bst"], writes=["mv"])
                        P.op("scalar", lambda e, nt=nt: e.activation(out=mv[0:nt, 2:3], in_=mv[0:nt, 1:2], func=AF.Sqrt, bias=EPS), reads=["mv"], writes=["mv"])
                        P.op("vector", lambda e, nt=nt: e.reciprocal(out=mv[0:nt, 3:4], in_=mv[0:nt, 2:3]), reads=["mv"], writes=["mv"])
                        P.op("vector", lambda e, nt=nt: e.tensor_scalar(out=mv[0:nt, 4:5], in0=mv[0:nt, 0:1], scalar1=mv[0:nt, 3:4], scalar2=-1.0, op0=ALU.mult, op1=ALU.mult),
                             reads=["mv"], writes=["mv"])
                        P.op("scalar", lambda e, nt=nt: e.activation(out=of[0:nt, :], in_=pod[0][0:nt, :], func=AF.Identity, scale=mv[0:nt, 3:4], bias=mv[0:nt, 4:5]),
                             reads=["pod0", "mv"], writes=["of"])
                    else:
                        P.op("scalar", lambda e, nt=nt: e.activation(out=of[0:nt, :], in_=pod[0][0:nt, :], func=AF.Square, accum_out=mv[0:nt, 0:1]), reads=["pod0"], writes=["of", "mv"])
                        P.op("scalar", lambda e, nt=nt: e.activation(out=mv[0:nt, 2:3], in_=mv[0:nt, 0:1], func=AF.Sqrt, scale=1.0 / 512.0, bias=EPS), reads=["mv"], writes=["mv"])
                        P.op("vector", lambda e, nt=nt: e.reciprocal(out=mv[0:nt, 3:4], in_=mv[0:nt, 2:3]), reads=["mv"], writes=["mv"])
                        P.op("scalar", lambda e, nt=nt: e.activation(out=of[0:nt, :], in_=pod[0][0:nt, :], func=AF.Identity, scale=mv[0:nt, 3:4]), reads=["pod0", "mv", "of"], writes=["of"])
                    P.op("gpsimd", lambda e, nt=nt: e.tensor_tensor(out=of[0:nt, :], in0=of[0:nt, :], in1=gnv[0:nt, :], op=ALU.mult), reads=["of", "gnv"], writes=["of"])
                    P.op("vector", lambda e, nt=nt: e.tensor_tensor(out=yb[0:nt, :], in0=of[0:nt, :], in1=gf[0:nt, :], op=ALU.mult), reads=["of", kgf], writes=[kyb])

            def f_yt(t):
                    nt = 128 if t < 16 else 4
                    si = 0 if t < 16 else 1
                    hi = t % 2
                    par = t % 2
                    ht = hTt[hi]
                    hk = "hTt%d" % hi
                    vb = vbs[par]; gf = gfs[par]; yb = ybs[par]; qb = qbs[par]
                    kvb = "vb%d" % par; kgf = "gf%d" % par; kyb = "yb%d" % par; kqb = "qbB%d" % par
                    gC = math.exp(lgam[h] * (128.0 if t < 16 else 4.0)) if kind == 1 else 1.0

                    sl = 0
                    cnt["ptb"] += 1
                    for j in range(4):
                        P.op("tensor", lambda e, j=j, sl=sl, nt=nt: e.transpose(out=ptb[:, sl * 512 + j * 128: sl * 512 + j * 128 + max(nt, 32)], in_=yb[0:max(nt, 32), j * 128:(j + 1) * 128], identity=ident[0:max(nt, 32), 0:max(nt, 32)]),
                             reads=[kyb, "ident"], writes=["ptb%d" % sl])
                    srcT = ptb[:, sl * 512:(sl + 1) * 512].rearrange("p (a t) -> p a t", t=128)[:, :, 0:nt]
                    P.op("scalar", lambda e, srcT=srcT, nt=nt, t=t: e.copy(out=yT[:, :, t * 128:t * 128 + nt], in_=srcT), reads=["ptb%d" % sl], writes=["yT"])
            f_proj(0)
            f_evac(0)
            for t in range(17):
                if t + 1 < 17:
                    f_proj(t + 1)
                if t >= 1:
                    f_yt(t - 1)
                f_rest(t)
                if t + 1 < 17:
                    f_evac(t + 1)
            f_yt(16)

            po = p_stb if kind == 1 else p_stc
            so = s_stb if kind == 1 else s_stc
            P.dma("sync", po[h].rearrange("(c p) v -> p c v", p=128), stf[0][:, :, :], reads=["stf0"], slot="sto")
            P.dma("sync", so[h].rearrange("(c p) v -> p c v", p=128), stf[1][:, :, :], reads=["stf1"], slot="sto")
            P.dma("gpsimd", wo[:, :, :], w_out[h * 512:(h + 1) * 512, :].rearrange("(k p) n -> p k n", p=128), writes=["wo"], slot="wo")
            if h == NH - 1:
                load_gvec(norm_g[li + 1, :] if li < NL - 1 else final_g)
            wout_pass(li, yT, wo, 4, first=(h == 0), last=(h == NH - 1), next_kind=next_kind, hT_next=hT_next)

    kinds = [i % 3 for i in range(NL)]
    VA = layer_A(0, 0)
    load_gvec(norm_g[0, :])
    for t in range(17):
        nt = 128 if t < 16 else 4
        src = xp[t * 128:t * 128 + nt, :] if t < 16 else xs[:, :]
        P.dma("sync", xts[t % 2][0:nt, :], src, writes=["xt%d" % (t % 2)], slot="xt%d" % (t % 2))
        rms_to(nt, "gvec", hn[0:nt, :], "hn", xts[t % 2], "xt%d" % (t % 2), extra_reads=["hn"])
        produce_h(t, nt, VA["hT"], to_dram=False)
    for li in range(NL):
        kind = kinds[li]
        print('MARK layer', li, 'starts at op', len(P.ops), flush=True)
        nk_ = kinds[li + 1] if li + 1 < NL else -1
        if kind == 0:
            run_layer_A(li, li // 3, VA, nk_, VA["hst"] if nk_ in (1, 2) else VA["hT"])
            P.barrier()
        else:
            VB = views_BC()
            if nk_ == 0:
                hT_next = VB["hTst"]
                run_layer_BC(li, kind, VB, 1, hT_next)
                P.barrier()
                VA = layer_A(li + 1, (li + 1) // 3)
                for t in range(17):
                    nt = 128 if t < 16 else 4
                    P.dma("sync", VA["hT"][:, :, t * 128:t * 128 + nt], hTd[t, :, :, 0:nt], reads=["hTd%d" % t], writes=["hT%d" % t], slot="hTl")
            else:
                run_layer_BC(li, kind, VB, nk_, VB["hTst"])
                P.barrier()
    P.emit()
    for g in reversed(ctxs):
        g.__exit__(None, None, None)
    return nc, hc, P.stats


_CACHE = {}


def kernel(**inp):
    if "nc" not in _CACHE:
        _CACHE["nc"] = build()
    nc, hc, stats = _CACHE["nc"]
    f = lambda a: np.ascontiguousarray(np.asarray(a, dtype=np.float32))
    in_maps = []
    shared = dict(
        norm_g=f(inp["norm_g"]), final_g=f(inp["final_g"]), w_in_a=f(inp["w_in_a"]), w_out_a=f(inp["w_out_a"]),
        w_in_b=f(inp["w_in_b"][0]), gn_b=f(inp["gn_b"][0]), w_out_b=f(inp["w_out_b"][0]), w_in_c=f(inp["w_in_c"][0]),
        w_gate2=f(inp["w_gate2_c"][0]), b_gate=f(inp["b_gate_c"][0]).reshape(1, 1024), gn_c=f(inp["gn_c"][0]), w_out_c=f(inp["w_out_c"][0]))
    for kk, v in hc.items():
        shared["c_" + kk] = v
    for c in range(8):
        m = dict(shared)
        m["xp"] = f(inp["x_prompt"][c % 4])
        m["xs"] = f(inp["x_sample"][c])
        for g, nm in enumerate(("cache_a_kv1", "cache_a_kv2", "cache_a_kv3")):
            m["ckv%d" % g] = f(inp[nm][:, c]).reshape(2, LBUF[g], 4096)
        m["stb"] = f(inp["state_b"][0, c])
        m["stc"] = f(inp["state_c"][0, c])
        in_maps.append(m)
    res = run_bass_kernel_spmd(nc, in_maps, core_ids=list(range(8)))
    R = res.results
    y_prompt = np.stack([R[b]["y_p"] for b in range(4)])
    y_sample = np.stack([R[c]["y_s"] for c in range(8)])
    pkv = []
    for g in range(3):
        keep = min(LBUF[g], S)
        a = np.stack([R[b]["p_kv%d" % g] for b in range(4)], axis=1)
        pkv.append(a.reshape(2, 4, keep, 2, 16, 128))
    p_stb = np.stack([R[b]["p_stb"] for b in range(4)])[None]
    p_stc = np.stack([R[b]["p_stc"] for b in range(4)])[None]
    skv = np.stack([R[c]["s_kv"] for c in range(8)], axis=0)
    s_list = [np.ascontiguousarray(skv[:, :, g].transpose(1, 0, 2, 3)).reshape(2, 8, 4, 2, 16, 128) for g in range(3)]
    s_stb = np.stack([R[c]["s_stb"] for c in range(8)])[None]
    s_stc = np.stack([R[c]["s_stc"] for c in range(8)])[None]
    outs = (y_prompt, y_sample, pkv[0], pkv[1], pkv[2], p_stb, p_stc, s_list[0], s_list[1], s_list[2], s_stb, s_stc)
    return tuple(np.ascontiguousarray(o, dtype=np.float32) for o in outs)
```

```python
import math
import numpy as np
import ml_dtypes
import concourse.bass as bass
import concourse.mybir as mybir
from concourse.bass_utils import run_bass_kernel_spmd

F32 = mybir.dt.float32
BF16 = mybir.dt.bfloat16
ALU = mybir.AluOpType
AF = mybir.ActivationFunctionType
AX = mybir.AxisListType

ENGS = ("sync", "scalar", "vector", "gpsimd", "tensor")
EPOCH = 12000
S = 2048
DM = 2048
SS = 2052
PAST = 16384
EPS = 1e-6
DILS = (1, 4, 16)
LBUF = (128, 512, 2048)
NL_DEFAULT = 4


PSUM_KEYS = {"pj0", "pj1", "ptb0", "psc0", "psc1", "pod0", "pod1", "pg"}


class Prog:
    def __init__(self, nc):
        self.nc = nc
        self.ops = []

    def op(self, eng, fn, reads=(), writes=(), dma=None):
        self.ops.append(dict(eng=eng, fn=fn, reads=tuple(reads), writes=tuple(writes), dma=dma, bar=False))

    def dma(self, eng, out, in_, reads=(), writes=(), slot="d", **kw):
        self.op(eng, lambda e: e.dma_start(out=out, in_=in_, **kw), reads, writes, dma=slot)

    def barrier(self):
        self.ops.append(dict(bar=True))

    def emit(self):
        nc = self.nc
        import os
        allops = self.ops[:int(os.environ.get('MAXOPS', '100000000'))]
        ops = []
        last_w = {}
        readers = {}
        dma_count = {}
        last_eng = {}
        last_dma = {}
        pending_bar = {}
        for o in allops:
            if o["bar"]:
                deps = set(last_eng.values()) | set(last_dma.values())
                for e in ENGS:
                    pending_bar[e] = set(deps)
                last_w.clear()
                readers.clear()
                continue
            j = len(ops)
            ops.append(o)
            deps = set()
            for r in o["reads"]:
                if r in last_w:
                    deps.add(last_w[r])
                if r in PSUM_KEYS:
                    for rd in readers.get(r, ()):
                        if ops[rd]["eng"] != o["eng"]:
                            deps.add(rd)
            for w in o["writes"]:
                if w in last_w:
                    deps.add(last_w[w])
                for rd in readers.get(w, ()):
                    if ops[rd]["eng"] == o["eng"] and ops[rd]["dma"] is None and o["dma"] is None:
                        continue
                    deps.add(rd)
            if o["eng"] in pending_bar:
                deps |= pending_bar.pop(o["eng"])
            fin = set()
            for i in deps:
                if i == j:
                    continue
                if ops[i]["dma"] is None and o["dma"] is None and ops[i]["eng"] == "tensor" and o["eng"] == "tensor":
                    continue
                fin.add(i)
            o["deps"] = fin
            o["dma_snap"] = dict(dma_count)
            if o["dma"] is not None:
                dma_count[o["dma"]] = dma_count.get(o["dma"], 0) + 1
                o["dma_idx"] = dma_count[o["dma"]]
                last_dma[o["dma"]] = j
            else:
                last_eng[o["eng"]] = j
            for r in o["reads"]:
                readers.setdefault(r, []).append(j)
            for w in o["writes"]:
                last_w[w] = j
                readers[w] = []
        for o in ops:
            o["ms"] = False
        for o in ops:
            for i in o["deps"]:
                if ops[i]["dma"] is None:
                    ops[i]["ms"] = True
        cnt = {e: 0 for e in ENGS}
        for o in ops:
            if o["ms"]:
                cnt[o["eng"]] += 1
                o["ms_idx"] = cnt[o["eng"]]
        sems = {}
        stack = []

        def get_sem(key):
            if key not in sems:
                g = nc.semaphore("s%d" % len(sems))
                sems[key] = g.__enter__()
                stack.append(g)
            return sems[key]

        seen = {e: {} for e in ENGS}
        for o in ops:
            waits = {}
            for i in o["deps"]:
                d = ops[i]
                if d["dma"] is not None:
                    n = max(o["dma_snap"].get(d["dma"], 0), d["dma_idx"])
                    ep, v = divmod(n - 1, EPOCH // 16)
                    key = ("dma", d["dma"], ep)
                    waits[key] = max(waits.get(key, 0), (v + 1) * 16)
                else:
                    ep, v = divmod(d["ms_idx"] - 1, EPOCH)
                    key = ("eng", d["eng"], ep)
                    waits[key] = max(waits.get(key, 0), v + 1)
            wl = []
            sn = seen[o["eng"]]
            for key, v in waits.items():
                if sn.get(key, 0) >= v:
                    continue
                if key[0] == "eng" and any(k2[0] == "eng" and k2[1] == key[1] and k2[2] > key[2] for k2 in sn):
                    continue
                sn[key] = v
                wl.append((key, v))
            o["waits"] = wl
        for o in ops:
            for key, v in o["waits"]:
                get_sem(key)
            if o["ms"]:
                ep, v = divmod(o["ms_idx"] - 1, EPOCH)
                o["inc"] = (get_sem(("eng", o["eng"], ep)), 1)
            if o["dma"] is not None:
                ep, v = divmod(o["dma_idx"] - 1, EPOCH // 16)
                o["inc"] = (get_sem(("dma", o["dma"], ep)), 16)
        if os.environ.get('DUMPW'):
            a_, b_ = [int(v_) for v_ in os.environ['DUMPW'].split(':')]
            for i_ in range(a_, min(b_, len(ops))):
                o_ = ops[i_]
                print('OP', i_, o_['eng'], o_['dma'], 'deps', sorted(o_['deps'])[-6:], 'waits', o_['waits'], 'ms', o_.get('ms_idx') if o_['ms'] else None, 'dmaidx', o_.get('dma_idx'), flush=True)
        final = []
        for slot, n in dma_count.items():
            ep, v = divmod(n - 1, EPOCH // 16)
            final.append((get_sem(("dma", slot, ep)), (v + 1) * 16))

        with nc.Block() as block:
            def mk(engname):
                def body(e):
                    for o in ops:
                        if o["eng"] != engname:
                            continue
                        for key, v in o["waits"]:
                            e.wait_ge(sems[key], v)
                        ins = o["fn"](e)
                        if "inc" in o:
                            ins.then_inc(o["inc"][0], o["inc"][1])
                    if engname == "sync":
                        for s_, v in final:
                            e.wait_ge(s_, v)
                return body
            block.sync(mk("sync"))
            block.scalar(mk("scalar"))
            block.vector(mk("vector"))
            block.gpsimd(mk("gpsimd"))
            block.tensor(mk("tensor"))
        for g in reversed(stack):
            g.__exit__(None, None, None)
        self.stats = dict(n_ops=len(ops), ms=cnt, nsem=len(sems))


def host_consts():
    c = {}
    bf = ml_dtypes.bfloat16
    c["ident"] = np.eye(128, dtype=np.float32).astype(bf)
    k = np.arange(128)[:, None]
    q = np.arange(128)[None, :]
    A = (k >= q).astype(np.float32)
    B = (k <= q).astype(np.float32)
    c["maskAB"] = np.concatenate([A, B, A, B], axis=1).astype(bf)
    c["maskBf"] = B.astype(np.float32)
    c["ones_bf"] = np.ones((128, 128), np.float32).astype(bf)
    c["ones_f"] = np.ones((128, 8), np.float32)
    half = 16
    inv = (500000.0 ** (-np.arange(half, dtype=np.float32) / half)).astype(np.float32)
    cosA = np.zeros((128, 3, 17, 16), np.float32)
    sinA = np.zeros((128, 3, 17, 16), np.float32)
    for g, dil in enumerate(DILS):
        nb = 16 // dil
        for tau in range(16):
            r, n = divmod(tau, nb)
            pos = ((128 * n + np.arange(128)) * dil + r).astype(np.float32)
            ang = pos[:, None] * inv[None, :]
            cosA[:, g, tau] = np.cos(ang)
            sinA[:, g, tau] = np.sin(ang)
        pos = (PAST + np.arange(4)).astype(np.float32)
        ang = pos[:, None] * inv[None, :]
        cosA[0:4, g, 16] = np.cos(ang)
        sinA[0:4, g, 16] = np.sin(ang)
    c["cosA"] = cosA
    c["sinA"] = sinA
    invb = (10000.0 ** (-np.arange(128, dtype=np.float32) / 128)).astype(np.float32)
    cosB = np.zeros((17, 128, 128), np.float32)
    sinB = np.zeros((17, 128, 128), np.float32)
    for t in range(16):
        pos = (128 * t + np.arange(128)).astype(np.float32)
        ang = pos[:, None] * invb[None, :]
        cosB[t] = np.cos(ang)
        sinB[t] = np.sin(ang)
    pos = (PAST + np.arange(4)).astype(np.float32)
    ang = pos[:, None] * invb[None, :]
    cosB[16, 0:4] = np.cos(ang)
    sinB[16, 0:4] = np.sin(ang)
    c["cosB"] = cosB
    c["sinB"] = sinB
    lg = np.log(1.0 - 2.0 ** (-5.0 - np.arange(8, dtype=np.float64)))
    dec = np.zeros((128, 2, 8, 3), np.float64)
    i = np.arange(128, dtype=np.float64)[:, None]
    dec[:, 0, :, 0] = np.exp(lg[None, :] * (i + 1.0))
    dec[:, 0, :, 1] = np.exp(-lg[None, :] * (i + 1.0)) / 16.0
    dec[:, 0, :, 2] = np.exp(lg[None, :] * (127.0 - i)) / 16.0
    dec[:, 1, :, 0] = np.exp(lg[None, :] * (i + 1.0))
    dec[:, 1, :, 1] = np.exp(-lg[None, :] * (i + 1.0)) / 16.0
    dec[:, 1, :, 2] = np.exp(lg[None, :] * (3.0 - i)) / 16.0
    c["decB"] = dec.astype(np.float32)
    j = np.arange(128)[:, None]
    t = np.arange(4)[None, :]
    sm = np.zeros((128, 9, 4), np.float32)
    sm[:, 0, :] = (j >= t)
    for s_ in range(4):
        sm[:, 1 + s_, s_] = 1.0
    c["smask"] = sm.astype(bf)
    mn = np.zeros((4, 3, 4), np.float32)
    tp = np.arange(4)[:, None]
    mn[:, 0, :] = (tp <= t)
    mn[:, 1, :] = (tp == t)
    mn[:, 2, :] = (tp == t)
    c["mnew"] = mn.astype(bf)
    c["triU"] = (k <= q).astype(np.float32)
    return c


CONST_SHAPES = None


def build(NL=NL_DEFAULT, debug=False, NHA=16):
    nc = bass.Bass("TRN2", target_bir_lowering=False)
    hc = host_consts()

    def din(name, shape, dt=F32):
        return nc.dram_tensor(name, list(shape), dt, kind="ExternalInput").ap()

    def dout(name, shape, dt=F32):
        return nc.dram_tensor(name, list(shape), dt, kind="ExternalOutput").ap()

    xp = din("xp", [S, DM])
    xs = din("xs", [4, DM])
    ckv = [din("ckv%d" % g, [2, LBUF[g], 4096]) for g in range(3)]
    stb_in = din("stb", [8, 256, 512])
    stc_in = din("stc", [4, 256, 512])
    norm_g = din("norm_g", [4, DM])
    final_g = din("final_g", [DM])
    w_in_a = din("w_in_a", [2, DM, 20480])
    w_out_a = din("w_out_a", [2, 2048, DM])
    w_in_b = din("w_in_b", [DM, 12288])
    gn_b = din("gn_b", [8, 512])
    w_out_b = din("w_out_b", [4096, DM])
    w_in_c = din("w_in_c", [DM, 6160])
    w_gate2 = din("w_gate2", [16, 1024])
    b_gate = din("b_gate", [1, 1024])
    gn_c = din("gn_c", [4, 512])
    w_out_c = din("w_out_c", [2048, DM])
    cd = {}
    for kk, v in hc.items():
        cd[kk] = din("c_" + kk, v.shape, BF16 if v.dtype == ml_dtypes.bfloat16 else F32)

    y_p = dout("y_p", [S, DM])
    y_s = dout("y_s", [4, DM])
    p_kv = [dout("p_kv%d" % g, [2, LBUF[g] if LBUF[g] < S else S, 4096]) for g in range(3)]
    p_stb = dout("p_stb", [8, 256, 512])
    p_stc = dout("p_stc", [4, 256, 512])
    s_kv = dout("s_kv", [2, 3, 4, 4096])
    s_stb = dout("s_stb", [8, 256, 512])
    s_stc = dout("s_stc", [4, 256, 512])
    xres = nc.dram_tensor("xres", [SS, DM], F32, kind="Internal").ap()
    dbg_x = dout("dbg_x", [SS, DM]) if debug else None
    dbg_acc = dout("dbg_acc", [128, 2, SS]) if debug else None
    dbg_vg = dout("dbg_vg", [128, 17, 128], BF16) if debug else None
    dbg_qkT = dout("dbg_qkT", [128, 2, 17, 128], BF16) if debug else None
    dbg_yT = dout("dbg_yT", [128, 2, SS], BF16) if debug else None
    hTd = nc.dram_tensor("hTd", [17, 128, 16, 128], BF16, kind="Internal").ap()

    P = Prog(nc)
    ctxs = []

    def sb(name, shape, dt):
        g = nc.sbuf_tensor(name, list(shape), dt)
        t = g.__enter__()
        ctxs.append(g)
        return t

    def ps(name, shape, dt):
        g = nc.psum_tensor(name, list(shape), dt)
        t = g.__enter__()
        ctxs.append(g)
        return t

    pj = [ps("pj0", [128, 512], F32), ps("pj1", [128, 512], F32)]
    ptb = ps("ptb", [128, 512], BF16)
    psc = [ps("psc0", [128, 512], F32), ps("psc1", [128, 512], F32)]
    pod = [ps("pod0", [128, 512], F32), ps("pod1", [128, 512], F32)]
    pg = ps("pg", [128, 512], F32)

    ident = sb("ident", [128, 128], BF16)
    maskAB = sb("maskAB", [128, 512], BF16)
    maskBf = sb("maskBf", [128, 128], F32)
    ones_bf = sb("ones_bf", [128, 128], BF16)
    ones_f = sb("ones_f", [128, 8], F32)
    cosA = sb("cosA", [128, 3, 17, 16], F32)
    sinA = sb("sinA", [128, 3, 17, 16], F32)
    decB = sb("decB", [128, 2, 8, 3], F32)
    smask = sb("smask", [128, 9, 4], BF16)
    mnew = sb("mnew", [4, 3, 4], BF16)
    triU = sb("triU", [128, 128], F32)
    gvec = sb("gvec", [128, DM], F32)
    for nm, t in (("ident", ident), ("maskAB", maskAB), ("maskBf", maskBf), ("ones_bf", ones_bf), ("ones_f", ones_f),
                  ("cosA", cosA), ("sinA", sinA), ("decB", decB), ("smask", smask), ("mnew", mnew), ("triU", triU)):
        P.dma("sync", t[:], cd[nm], writes=[nm], slot="const")

    AR_BYTES = 154880
    arena = sb("arena", [128, AR_BYTES // 4], F32)
    ar_off = [0]

    def ar_reset():
        ar_off[0] = 0

    def av(shape, dt):
        n = int(np.prod(shape[1:])) * (2 if dt == BF16 else 4)
        n4 = (n + 3) // 4
        o = ar_off[0]
        assert o + n4 <= AR_BYTES // 4, ("arena overflow", o, n4)
        ar_off[0] = o + ((n4 + 7) // 8) * 8
        v = arena[0:shape[0], o:o + n4]
        if dt == BF16:
            v = v.bitcast(BF16)
        if len(shape) == 3:
            v = v.rearrange("p (a b) -> p a b", b=shape[2])
        elif len(shape) == 4:
            v = v.rearrange("p (a b c) -> p a b c", b=shape[2], c=shape[3])
        return v

    xts = [sb("xt0", [128, DM], F32), sb("xt1", [128, DM], F32)]
    hn = sb("hn", [128, DM], BF16)
    st4 = sb("st4", [128, 8], F32)
    qkf = [sb("qkf0", [128, 512], F32), sb("qkf1", [128, 256], F32)]
    vf = [sb("vf0", [128, 128], F32), sb("vf1", [128, 128], F32)]
    qkb = [sb("qkb0", [128, 1024], BF16), sb("qkb1", [128, 256], BF16)]
    rt = sb("rt", [128, 6, 256], F32)
    PT = [sb("PT0", [128, 512], BF16), sb("PT1", [128, 512], BF16)]
    sg = sb("sg", [128, 512], F32)
    rc = sb("rc", [128, 512], F32)

    HT_ALL = ["hT%d" % t for t in range(17)]
    scale_a = 1.0 / math.sqrt(128.0)
    cnt = {"pj": 0, "att": 0, "ptb": 0, "t": 0}

    def rms_to(nt, gsrc_key, out_ap, out_key, xt, xk, extra_reads=()):
        P.op("scalar", lambda e: e.activation(out=hn[0:nt, :], in_=xt[0:nt, :], func=AF.Square, accum_out=st4[0:nt, 0:1]),
             reads=[xk], writes=["hn", "st4"])
        P.op("scalar", lambda e: e.activation(out=st4[0:nt, 1:2], in_=st4[0:nt, 0:1], func=AF.Sqrt, scale=1.0 / DM, bias=EPS),
             reads=["st4"], writes=["st4"])
        P.op("vector", lambda e: e.reciprocal(out=st4[0:nt, 2:3], in_=st4[0:nt, 1:2]), reads=["st4"], writes=["st4"])
        P.op("vector", lambda e: e.scalar_tensor_tensor(out=out_ap, in0=xt[0:nt, :], scalar=st4[0:nt, 2:3], in1=gvec[0:nt, :],
                                                       op0=ALU.mult, op1=ALU.mult),
             reads=[xk, "st4", gsrc_key] + list(extra_reads), writes=[out_key])

    def load_gvec(src_row_ap):
        P.dma("sync", gvec[:], src_row_ap.partition_broadcast(128), writes=["gvec"], slot="gvec")

    def produce_h(t, nt, hT, to_dram):
        for q4 in range(4):
            sl = 0
            cnt["ptb"] += 1
            for j in range(4):
                kc = q4 * 4 + j
                P.op("tensor", lambda e, kc=kc, j=j, sl=sl: e.transpose(out=ptb[:, sl * 512 + j * 128: sl * 512 + j * 128 + max(nt, 32)],
                                                                         in_=hn[0:max(nt, 32), kc * 128:(kc + 1) * 128], identity=ident[0:max(nt, 32), 0:max(nt, 32)]),
                     reads=["hn", "ident"], writes=["ptb%d" % sl])
            src = ptb[:, sl * 512:(sl + 1) * 512].rearrange("p (a t) -> p a t", t=128)[:, :, 0:nt]
            if not to_dram:
                dst = hT[:, q4 * 4:(q4 + 1) * 4, t * 128:t * 128 + nt]
                P.op("scalar", lambda e, dst=dst, src=src: e.copy(out=dst, in_=src), reads=["ptb%d" % sl], writes=["hT%d" % t])
            else:
                dst = hT[:, q4 * 4:(q4 + 1) * 4, 0:nt]
                P.op("scalar", lambda e, dst=dst, src=src: e.copy(out=dst, in_=src), reads=["ptb%d" % sl], writes=["hTst"])
        if to_dram:
            P.dma("sync", hTd[t, :, :, 0:nt], hT[:, :, 0:nt], reads=["hTst"], writes=["hTd%d" % t], slot="hTd")

    def wout_pass(li, yT, wo, nk, first, last, next_kind, hT_next, xbufs=None):
        if xbufs is None:
            xbufs = xts
        nxb = len(xbufs)

        def emit_load(t):
            nt = 128 if t < 16 else 4
            r0 = t * 128
            xt = xbufs[t % nxb]
            xk = "xt%d" % (t % nxb)
            if li == 0 and first:
                src = xp[r0:r0 + nt, :] if t < 16 else xs[:, :]
                P.dma("sync", xt[0:nt, :], src, writes=[xk], slot=xk)
            else:
                P.dma("sync", xt[0:nt, :], xres[r0:r0 + nt, :], reads=["xres%d" % t], writes=[xk], slot=xk)

        for t0 in range(nxb - 1):
            emit_load(t0)
        for t in range(17):
            nt = 128 if t < 16 else 4
            r0 = t * 128
            xt = xbufs[t % nxb]
            xk = "xt%d" % (t % nxb)
            if t + nxb - 1 < 17:
                emit_load(t + nxb - 1)
            banks = [pj[0], pj[1], psc[0], psc[1]]
            bkeys = ["pj0", "pj1", "psc0", "psc1"]
            for cb in range(4):
                for k in range(nk):
                    P.op("tensor", lambda e, cb=cb, k=k, nt=nt, r0=r0, banks=banks: e.matmul(banks[cb][0:nt, :], lhsT=yT[:, k, r0:r0 + nt],
                                                                   rhs=wo[:, k, cb * 512:(cb + 1) * 512], start=(k == 0), stop=(k == nk - 1)),
                         reads=["yT", "wo"], writes=[bkeys[cb]])
            for cb in range(4):
                P.op("vector", lambda e, cb=cb, nt=nt, banks=banks, xt=xt: e.tensor_tensor(out=xt[0:nt, cb * 512:(cb + 1) * 512], in0=xt[0:nt, cb * 512:(cb + 1) * 512],
                                                                 in1=banks[cb][0:nt, :], op=ALU.add),
                     reads=[xk, bkeys[cb]], writes=[xk])
            is_final = last and (li == NL - 1)
            if not is_final:
                P.dma("sync", xres[r0:r0 + nt, :], xt[0:nt, :], reads=[xk], writes=["xres%d" % t], slot="xres")
            if last:
                if is_final:
                    if debug:
                        P.dma("sync", dbg_x[r0:r0 + nt, :], xt[0:nt, :], reads=[xk], slot="dbg")
                    rms_to(nt, "gvec", xt[0:nt, :], xk, xt, xk)
                    dst = y_p[r0:r0 + nt, :] if t < 16 else y_s[:, :]
                    P.dma("sync", dst, xt[0:nt, :], reads=[xk], slot="yout")
                else:
                    rms_to(nt, "gvec", hn[0:nt, :], "hn", xt, xk, extra_reads=["hn"])
                    produce_h(t, nt, hT_next, to_dram=(next_kind != 0))

    def layer_A(li, slot):
        ar_reset()
        hT = av([128, 16, SS], BF16)
        wb = [av([128, 16, 384], BF16) for _ in range(2)]
        wo = av([128, 4, DM], BF16)
        yT = av([128, 4, SS], BF16)
        qkT = av([128, 2, 17, 128], BF16)
        vg = av([128, 17, 128], BF16)
        o_acc = ar_off[0]
        acc = av([128, 2, SS], F32)
        ck = av([128, 4, 2, 128], BF16)
        ckT = av([128, 128], BF16)
        PTs = av([128, 32], BF16)
        hst = arena[0:128, o_acc:o_acc + 1024].bitcast(BF16).rearrange("p (a b) -> p a b", b=128)
        return dict(hT=hT, wb=wb, wo=wo, yT=yT, qkT=qkT, vg=vg, acc=acc, ck=ck, ckT=ckT, PTs=PTs, hst=hst)

    def run_layer_A(li, slot, V, next_kind, hT_next):
        hT, wbs, wo, yT, qkT, vg, acc, ck, ckT, PTs = (V[k] for k in ("hT", "wb", "wo", "yT", "qkT", "vg", "acc", "ck", "ckT", "PTs"))
        wsrc = w_in_a[slot].rearrange("(kc p) (s c) -> p kc s c", p=128, c=2048)
        wi = 0
        for h in range(NHA):
            hs = h % 4
            for g in range(3):
                dil = DILS[g]
                nb = 16 // dil
                bi = wi % 2
                wi += 1
                wb = wbs[bi]
                wkeys = ["wb%d_%d" % (bi, kq) for kq in range(3)]
                for kq in range(3):
                    P.dma("gpsimd", wb[:, :, kq * 128:(kq + 1) * 128],
                          wsrc[:, :, 3 * g + kq, h * 128:(h + 1) * 128], writes=[wkeys[kq]], slot=wkeys[kq])
                pendT = None
                for tau in range(17):
                    nt = 128 if tau < 16 else 4
                    if tau < 16:
                        r, n = divmod(tau, nb)
                        st0 = n * 128 * dil + r
                        tsl = slice(st0, st0 + 127 * dil + 1, dil)
                    else:
                        tsl = slice(S, S + 4)
                    pi = cnt["pj"] % 2
                    cnt["pj"] += 1
                    pjt = pj[pi]
                    for kc in range(16):
                        P.op("tensor", lambda e, kc=kc, tsl=tsl, pjt=pjt, wb=wb, nt=nt: e.matmul(pjt[0:nt, 0:384], lhsT=hT[:, kc, tsl], rhs=wb[:, kc, :],
                                                                                          start=(kc == 0), stop=(kc == 15)),
                             reads=HT_ALL + wkeys, writes=["pj%d" % pi])
                    ti = cnt["t"] % 2
                    cnt["t"] += 1
                    qf = qkf[ti]
                    P.op("scalar", lambda e, qf=qf, pjt=pjt, nt=nt: e.copy(out=qf[0:nt, 0:256], in_=pjt[0:nt, 0:256]),
                         reads=["pj%d" % pi], writes=["qkf%d" % ti])
                    P.op("vector", lambda e, pjt=pjt, nt=nt, tau=tau: e.tensor_copy(out=vg[0:nt, tau, :], in_=pjt[0:nt, 256:384]),
                         reads=["pj%d" % pi], writes=["vg%d" % tau])
                    if tau == 16:
                        need = True
                    elif g == 0:
                        need = (tau == 15)
                    elif g == 1:
                        need = (tau % nb == nb - 1)
                    else:
                        need = True
                    if need:
                        P.op("scalar", lambda e, pjt=pjt, nt=nt, ti=ti: e.copy(out=vf[ti][0:nt, :], in_=pjt[0:nt, 256:384]),
                             reads=["pj%d" % pi], writes=["vf%d" % ti])
                    q3 = qf[0:nt, 0:256].rearrange("p (a d) -> p a d", d=128)
                    x1 = q3[:, :, 0:16]
                    x2 = q3[:, :, 16:32]
                    cs = cosA[0:nt, g, tau:tau + 1, :].broadcast_to([nt, 2, 16])
                    sn = sinA[0:nt, g, tau:tau + 1, :].broadcast_to([nt, 2, 16])
                    tv = [rt[0:nt, i, 0:32].rearrange("p (a d) -> p a d", d=16) for i in range(4)]
                    kq_ = "qkf%d" % ti
                    P.op("gpsimd", lambda e, tv=tv, x1=x1, cs=cs: e.tensor_tensor(out=tv[0], in0=x1, in1=cs, op=ALU.mult), reads=[kq_, "cosA"], writes=["rt0"])
                    P.op("gpsimd", lambda e, tv=tv, x2=x2, sn=sn: e.tensor_tensor(out=tv[1], in0=x2, in1=sn, op=ALU.mult), reads=[kq_, "sinA"], writes=["rt1"])
                    P.op("gpsimd", lambda e, tv=tv, x1=x1, sn=sn: e.tensor_tensor(out=tv[2], in0=x1, in1=sn, op=ALU.mult), reads=[kq_, "sinA"], writes=["rt2"])
                    P.op("gpsimd", lambda e, tv=tv, x2=x2, cs=cs: e.tensor_tensor(out=tv[3], in0=x2, in1=cs, op=ALU.mult), reads=[kq_, "cosA"], writes=["rt3"])
                    P.op("gpsimd", lambda e, tv=tv, x1=x1: e.tensor_tensor(out=x1, in0=tv[0], in1=tv[1], op=ALU.subtract), reads=["rt0", "rt1", kq_], writes=[kq_])
                    P.op("gpsimd", lambda e, tv=tv, x2=x2: e.tensor_tensor(out=x2, in0=tv[2], in1=tv[3], op=ALU.add), reads=["rt2", "rt3", kq_], writes=[kq_])
                    qb = qkb[ti]
                    P.op("vector", lambda e, qb=qb, qf=qf, nt=nt: e.tensor_copy(out=qb[0:nt, 0:256], in_=qf[0:nt, 0:256]),
                         reads=[kq_], writes=["qkb%d" % ti])
                    def emit_T(tau, nt, ti, qb):
                        sl = 0
                        for a in range(2):
                            P.op("tensor", lambda e, a=a, sl=sl, qb=qb, nt=nt: e.transpose(out=ptb[:, sl * 512 + a * 128: sl * 512 + a * 128 + max(nt, 32)],
                                                                                        in_=qb[0:max(nt, 32), a * 128:(a + 1) * 128], identity=ident[0:max(nt, 32), 0:max(nt, 32)]),
                                 reads=["qkb%d" % ti, "ident"], writes=["ptb%d" % sl])
                        srcT = ptb[:, sl * 512:sl * 512 + 256].rearrange("p (a t) -> p a t", t=128)[:, :, 0:nt]
                        P.op("scalar", lambda e, srcT=srcT, tau=tau, nt=nt: e.copy(out=qkT[:, :, tau, 0:nt], in_=srcT),
                             reads=["ptb%d" % sl], writes=["qkT%d" % tau])
                    if pendT is not None:
                        emit_T(*pendT)
                    pendT = (tau, nt, ti, qb)
                    if need:
                        if tau == 16:
                            dk = s_kv[slot, g, :, h * 128:(h + 1) * 128]
                            dv = s_kv[slot, g, :, 2048 + h * 128:2048 + (h + 1) * 128]
                        else:
                            r, n = divmod(tau, nb)
                            keep = min(LBUF[g], S)
                            base = (n * 128 * dil + r) - (S - keep)
                            rows = slice(base, base + 127 * dil + 1, dil)
                            dk = p_kv[g][slot, rows, h * 128:(h + 1) * 128]
                            dv = p_kv[g][slot, rows, 2048 + h * 128:2048 + (h + 1) * 128]
                        P.dma("sync", dk, qf[0:nt, 128:256], reads=[kq_], slot="okv%d" % ti)
                        P.dma("sync", dv, vf[ti][0:nt, :], reads=["vf%d" % ti], slot="okv%d" % ti)
                emit_T(*pendT)
                def emit_S(pair):
                    ai = cnt["att"] % 2
                    cnt["att"] += 1
                    pst, pot, ptt = psc[ai], pod[ai], PT[ai]
                    ks_, ko_, kp_ = "psc%d" % ai, "pod%d" % ai, "PT%d" % ai
                    for b in range(2):
                        tau = 2 * pair + b
                        r, n = divmod(tau, nb)
                        if n > 0:
                            P.op("tensor", lambda e, tau=tau, b=b, pst=pst: e.matmul(pst[:, (2 * b) * 128:(2 * b + 1) * 128], lhsT=qkT[:, 1, tau - 1, :],
                                                                                   rhs=qkT[:, 0, tau, :], start=True, stop=True),
                                 reads=["qkT%d" % (tau - 1), "qkT%d" % tau], writes=[ks_])
                        P.op("tensor", lambda e, tau=tau, b=b, pst=pst: e.matmul(pst[:, (2 * b + 1) * 128:(2 * b + 2) * 128], lhsT=qkT[:, 1, tau, :],
                                                                               rhs=qkT[:, 0, tau, :], start=True, stop=True),
                             reads=["qkT%d" % tau], writes=[ks_])
                    P.op("scalar", lambda e, pst=pst, ptt=ptt: e.activation(out=ptt[:, :], in_=pst[:, :], func=AF.Exp, scale=scale_a),
                         reads=[ks_], writes=[kp_])
                    P.op("vector", lambda e, ptt=ptt: e.tensor_tensor(out=ptt[:, :], in0=ptt[:, :], in1=maskAB[:, :], op=ALU.mult),
                         reads=[kp_, "maskAB"], writes=[kp_])
                    P.op("vector", lambda e, pot=pot: e.memset(pot[:, :], 0.0), writes=[ko_])
                    return (pot, ptt, ko_, kp_)

                def emit_PV(pair, pot, ptt, ko_, kp_):
                    for b in range(2):
                        tau = 2 * pair + b
                        r, n = divmod(tau, nb)
                        oreg = pot[:, b * 256:b * 256 + 128]
                        dreg = pot[:, b * 256 + 128:b * 256 + 256]
                        parts = ([(tau - 1, 2 * b)] if n > 0 else []) + [(tau, 2 * b + 1)]
                        for (kt, pslot) in parts:
                            P.op("tensor", lambda e, kt=kt, pslot=pslot, oreg=oreg, ptt=ptt: e.matmul(oreg, lhsT=vg[:, kt, :], rhs=ptt[:, pslot * 128:(pslot + 1) * 128],
                                                                                                    start=False, stop=False, skip_group_check=True),
                                 reads=["vg%d" % kt, kp_, ko_], writes=[ko_])
                            P.op("tensor", lambda e, pslot=pslot, dreg=dreg, ptt=ptt: e.matmul(dreg, lhsT=ones_bf[:, :], rhs=ptt[:, pslot * 128:(pslot + 1) * 128],
                                                                                             start=False, stop=False, skip_group_check=True),
                                 reads=["ones_bf", kp_, ko_], writes=[ko_])
                    for b in range(2):
                        tau = 2 * pair + b
                        r, n = divmod(tau, nb)
                        st0 = n * 128 * dil + r
                        dst = acc[:, :, st0:st0 + 127 * dil + 1:dil]
                        src = pot[:, b * 256:(b + 1) * 256].rearrange("p (a t) -> p a t", t=128)
                        if g == 0:
                            P.op("scalar", lambda e, dst=dst, src=src: e.copy(out=dst, in_=src), reads=[ko_], writes=["acc"])
                        else:
                            P.op("vector", lambda e, dst=dst, src=src: e.tensor_tensor(out=dst, in0=dst, in1=src, op=ALU.add), reads=[ko_, "acc"], writes=["acc"])

                pendA = None
                for pair in range(8):
                    ctx = emit_S(pair)
                    if pendA is not None:
                        emit_PV(*pendA)
                    pendA = (pair,) + ctx
                emit_PV(*pendA)
                nset = 1 if g == 0 else 4
                for kv in range(2):
                    if g == 0:
                        srcc = ckv[0][slot, :, kv * 2048 + h * 128: kv * 2048 + (h + 1) * 128]
                        P.dma("gpsimd", ck[:, 0, kv, :], srcc, writes=["ck%d" % kv], slot="ck%d" % kv)
                    else:
                        srcc = ckv[g][slot, :, :].rearrange("(i t) c -> i t c", t=dil)[:, 0:4, kv * 2048 + h * 128: kv * 2048 + (h + 1) * 128]
                        P.dma("gpsimd", ck[:, :, kv, :], srcc, writes=["ck%d" % kv], slot="ck%d" % kv)
                ai = cnt["att"] % 2
                cnt["att"] += 1
                pst, pot = psc[ai], pod[ai]
                ks_, ko_ = "psc%d" % ai, "pod%d" % ai
                P.op("vector", lambda e, pot=pot: e.memset(pot[:, 0:8], 0.0), writes=[ko_])
                for s_ in range(nset):
                    sl = 0
                    cnt["ptb"] += 1
                    P.op("tensor", lambda e, s_=s_, sl=sl: e.transpose(out=ptb[:, sl * 512:sl * 512 + 128], in_=ck[:, s_, 0, :], identity=ident[:, :]),
                         reads=["ck0", "ident"], writes=["ptb%d" % sl])
                    P.op("scalar", lambda e, sl=sl: e.copy(out=ckT[:, :], in_=ptb[:, sl * 512:sl * 512 + 128]), reads=["ptb%d" % sl], writes=["ckT"])
                    P.op("tensor", lambda e, s_=s_, pst=pst: e.matmul(pst[:, s_ * 4:s_ * 4 + 4], lhsT=ckT[:, :], rhs=qkT[:, 0, 16, 0:4], start=True, stop=True),
                         reads=["ckT", "qkT16"], writes=[ks_])
                    P.op("scalar", lambda e, s_=s_, pst=pst: e.activation(out=PTs[:, s_ * 4:s_ * 4 + 4], in_=pst[:, s_ * 4:s_ * 4 + 4], func=AF.Exp, scale=scale_a),
                         reads=[ks_], writes=["PTs"])
                    mi = 0 if g == 0 else 1 + s_
                    P.op("vector", lambda e, s_=s_, mi=mi: e.tensor_tensor(out=PTs[:, s_ * 4:s_ * 4 + 4], in0=PTs[:, s_ * 4:s_ * 4 + 4], in1=smask[:, mi, :], op=ALU.mult),
                         reads=["PTs", "smask"], writes=["PTs"])
                    P.op("tensor", lambda e, s_=s_, pot=pot: e.matmul(pot[:, 0:4], lhsT=ck[:, s_, 1, :], rhs=PTs[:, s_ * 4:s_ * 4 + 4], start=False, stop=False, skip_group_check=True),
                         reads=["ck1", "PTs", ko_], writes=[ko_])
                    P.op("tensor", lambda e, s_=s_, pot=pot: e.matmul(pot[:, 4:8], lhsT=ones_bf[:, :], rhs=PTs[:, s_ * 4:s_ * 4 + 4], start=False, stop=False, skip_group_check=True),
                         reads=["ones_bf", "PTs", ko_], writes=[ko_])
                P.op("tensor", lambda e, pst=pst: e.matmul(pst[0:4, 16:20], lhsT=qkT[:, 1, 16, 0:4], rhs=qkT[:, 0, 16, 0:4], start=True, stop=True),
                     reads=["qkT16"], writes=[ks_])
                P.op("scalar", lambda e, pst=pst: e.activation(out=PTs[0:4, 16:20], in_=pst[0:4, 16:20], func=AF.Exp, scale=scale_a), reads=[ks_], writes=["PTs2"])
                P.op("vector", lambda e, g=g: e.tensor_tensor(out=PTs[0:4, 16:20], in0=PTs[0:4, 16:20], in1=mnew[0:4, g, :], op=ALU.mult),
                     reads=["PTs2", "mnew"], writes=["PTs2"])
                P.op("tensor", lambda e, pot=pot: e.matmul(pot[:, 0:4], lhsT=vg[0:4, 16, :], rhs=PTs[0:4, 16:20], start=False, stop=False, skip_group_check=True),
                     reads=["vg16", "PTs2", ko_], writes=[ko_])
                P.op("tensor", lambda e, pot=pot: e.matmul(pot[:, 4:8], lhsT=ones_bf[0:4, :], rhs=PTs[0:4, 16:20], start=False, stop=False, skip_group_check=True),
                     reads=["ones_bf", "PTs2", ko_], writes=[ko_])
                dst = acc[:, :, S:S + 4]
                src = pot[:, 0:8].rearrange("p (a t) -> p a t", t=4)
                if g == 0:
                    P.op("scalar", lambda e, dst=dst, src=src: e.copy(out=dst, in_=src), reads=[ko_], writes=["acc"])
                else:
                    P.op("vector", lambda e, dst=dst, src=src: e.tensor_tensor(out=dst, in0=dst, in1=src, op=ALU.add), reads=[ko_, "acc"], writes=["acc"])
            if debug and h == 0:
                P.dma("sync", dbg_acc, acc[:, :, :], reads=["acc"], slot="dbg")
                P.dma("sync", dbg_vg, vg[:, :, :], reads=["vg%d" % i for i in range(17)], slot="dbg")
                P.dma("sync", dbg_qkT.rearrange("p a t d -> p (a t) d"), qkT[:, :, :, :].rearrange("p a t d -> p (a t) d"), reads=["qkT%d" % i for i in range(17)], slot="dbg")
            bi = wi % 2
            wi += 1
            wgt = wbs[bi]
            gk = "wb%d_0" % bi
            P.dma("gpsimd", wgt[:, :, 0:128], wsrc[:, :, 9, h * 128:(h + 1) * 128], writes=[gk], slot=gk)
            for c in range(5):
                c0 = c * 512
                n_ = 512 if c < 4 else 4
                for kc in range(16):
                    P.op("tensor", lambda e, kc=kc, c0=c0, n_=n_, wgt=wgt: e.matmul(pg[:, 0:n_], lhsT=wgt[:, kc, 0:128], rhs=hT[:, kc, c0:c0 + n_],
                                                                                  start=(kc == 0), stop=(kc == 15)),
                         reads=HT_ALL + [gk], writes=["pg"])
                P.op("scalar", lambda e, n_=n_: e.activation(out=sg[:, 0:n_], in_=pg[:, 0:n_], func=AF.Silu), reads=["pg"], writes=["sg"])
                P.op("vector", lambda e, c0=c0, n_=n_: e.reciprocal(out=rc[:, 0:n_], in_=acc[:, 1, c0:c0 + n_]), reads=["acc"], writes=["rc"])
                P.op("vector", lambda e, c0=c0, n_=n_: e.tensor_tensor(out=rc[:, 0:n_], in0=rc[:, 0:n_], in1=acc[:, 0, c0:c0 + n_], op=ALU.mult),
                     reads=["rc", "acc"], writes=["rc"])
                P.op("vector", lambda e, c0=c0, n_=n_, hs=hs: e.tensor_tensor(out=yT[:, hs, c0:c0 + n_], in0=rc[:, 0:n_], in1=sg[:, 0:n_], op=ALU.mult),
                     reads=["rc", "sg"], writes=["yT"])
            if hs == 3:
                P.dma("gpsimd", wo[:, :, :], w_out_a[slot, (h - 3) * 128:(h + 1) * 128, :].rearrange("(k p) n -> p k n", p=128), writes=["wo"], slot="wo")
                if h == NHA - 1:
                    load_gvec(norm_g[li + 1, :] if li < NL - 1 else final_g)
                if h == NHA - 1 and next_kind != 0 and li < NL - 1:
                    P.barrier()
                wout_pass(li, yT, wo, 4, first=(h == 3), last=(h == NHA - 1), next_kind=next_kind, hT_next=hT_next)

    def views_BC():
        ar_reset()
        V = {}
        V["hTt"] = [av([128, 16, 128], BF16) for _ in range(2)]
        V["hTst"] = av([128, 16, 128], BF16)
        V["xt2"] = av([128, DM], F32)
        V["wb"] = [av([128, 16, 512], BF16) for _ in range(3)]
        V["wo"] = av([128, 4, DM], BF16)
        V["yT"] = av([128, 4, SS], BF16)
        V["stf"] = [av([128, 2, 512], F32) for _ in range(2)]
        V["stb"] = [av([128, 2, 512], BF16) for _ in range(2)]
        V["cs"] = [av([128, 2, 128], F32) for _ in range(2)]
        V["qkT"] = av([128, 4, 128], BF16)
        V["vb"] = [av([128, 512], BF16) for _ in range(2)]
        V["scT"] = av([128, 128], BF16)
        V["of"] = av([128, 512], F32)
        V["gf"] = [av([128, 512], F32) for _ in range(2)]
        V["gnv"] = av([128, 512], F32)
        V["yb"] = [av([128, 512], BF16) for _ in range(2)]
        V["qb"] = [av([128, 768], BF16) for _ in range(2)]
        V["bst"] = av([128, 4, 6], F32)
        V["mv"] = av([128, 8], F32)
        V["wlr"] = av([128, 16, 16], BF16)
        V["lrT"] = av([32, 128], F32)
        V["wg2"] = av([17, 1024], F32)
        V["sp"] = av([128, 256], F32)
        V["eb"] = av([128, 2, 256], F32)
        V["ebl"] = av([128, 2], F32)
        return V

    def run_layer_BC(li, kind, V, next_kind, hT_next):
        NH = 8 if kind == 1 else 4
        w_in = w_in_b if kind == 1 else w_in_c
        w_out = w_out_b if kind == 1 else w_out_c
        gn = gn_b if kind == 1 else gn_c
        wsrc = w_in.rearrange("(kc p) n -> p kc n", p=128)
        hTt, wbs, wo, yT = V["hTt"], V["wb"], V["wo"], V["yT"]
        stf, stbf, cs, qkT, vbs, scT = V["stf"], V["stb"], V["cs"], V["qkT"], V["vb"], V["scT"]
        qbs = V["qb"]
        of, gfs, gnv, ybs, bst, mv = V["of"], V["gf"], V["gnv"], V["yb"], V["bst"], V["mv"]
        wlr, lrT, wg2, sp, eb, ebl = V["wlr"], V["lrT"], V["wg2"], V["sp"], V["eb"], V["ebl"]
        lgam = [math.log(1.0 - 2.0 ** (-5.0 - hh)) for hh in range(8)]
        if kind == 2:
            P.dma("gpsimd", wlr[:, :, :], wsrc[:, :, 6144:6160], writes=["wlr"], slot="wlr")
            P.dma("sync", wg2[0:16, :], w_gate2, writes=["wg2a"], slot="wg2")
            P.dma("sync", wg2[16:17, :], b_gate, writes=["wg2b"], slot="wg2")
        hcnt = 0
        for h in range(NH):
            if kind == 1:
                cols = [(h * 256, 256, 0), (2048 + h * 256, 256, 256)]
                vcol = 4096 + h * 512
                gcol = 8192 + h * 512
            else:
                cols = [(h * 256, 256, 0), (1024 + h * 256, 256, 256)]
                vcol = 2048 + h * 512
                gcol = 4096 + h * 512
            for (c0, n_, d0) in cols:
                for kq in range(2):
                    P.dma("gpsimd", wbs[0][:, kq * 8:(kq + 1) * 8, d0:d0 + n_], wsrc[:, kq * 8:(kq + 1) * 8, c0:c0 + n_], writes=["wq%d_%d" % (d0, kq)], slot="wq%d_%d" % (d0, kq))
            for kq in range(2):
                P.dma("gpsimd", wbs[1][:, kq * 8:(kq + 1) * 8, :], wsrc[:, kq * 8:(kq + 1) * 8, vcol:vcol + 512], writes=["wv_%d" % kq], slot="wv_%d" % kq)
                P.dma("gpsimd", wbs[2][:, kq * 8:(kq + 1) * 8, :], wsrc[:, kq * 8:(kq + 1) * 8, gcol:gcol + 512], writes=["wg_%d" % kq], slot="wg_%d" % kq)
            WQK = ["wq0_0", "wq0_1", "wq256_0", "wq256_1"]
            P.dma("sync", gnv[:, :], gn[h, :].partition_broadcast(128), writes=["gnv"], slot="gnv")
            P.op("gpsimd", lambda e: e.memset(stf[0][:, :, :], 0.0), writes=["stf0"])
            P.op("gpsimd", lambda e: e.memset(stbf[0][:, :, :], 0.0), writes=["stb0"])
            st_in = stb_in if kind == 1 else stc_in
            P.dma("sync", stf[1][:, :, :], st_in[h].rearrange("(c p) v -> p c v", p=128), writes=["stf1"], slot="stin")
            P.dma("gpsimd", stbf[1][:, :, :], st_in[h].rearrange("(c p) v -> p c v", p=128), writes=["stb1"], slot="stin2")
            def f_proj(t):
                    nt = 128 if t < 16 else 4
                    si = 0 if t < 16 else 1
                    hi = t % 2
                    par = t % 2
                    ht = hTt[hi]
                    hk = "hTt%d" % hi
                    vb = vbs[par]; gf = gfs[par]; yb = ybs[par]; qb = qbs[par]
                    kvb = "vb%d" % par; kgf = "gf%d" % par; kyb = "yb%d" % par; kqb = "qbB%d" % par
                    gC = math.exp(lgam[h] * (128.0 if t < 16 else 4.0)) if kind == 1 else 1.0

                    P.dma("sync", ht[:, :, 0:nt], hTd[t, :, :, 0:nt], reads=["hTd%d" % t], writes=[hk], slot=hk)
                    if kind == 1:
                        P.dma("sync", cs[hi][0:nt, 0, :], cd["cosB"][t, 0:nt, :], writes=["cs%da" % hi], slot="cs%da" % hi)
                        P.dma("sync", cs[hi][0:nt, 1, :], cd["sinB"][t, 0:nt, :], writes=["cs%db" % hi], slot="cs%db" % hi)
                    for kc in range(16):
                        P.op("tensor", lambda e, kc=kc, ht=ht, nt=nt: e.matmul(pj[0][0:nt, :], lhsT=ht[:, kc, 0:nt], rhs=wbs[0][:, kc, :], start=(kc == 0), stop=(kc == 15)),
                             reads=[hk] + WQK, writes=["pj0"])
                    for kc in range(16):
                        P.op("tensor", lambda e, kc=kc, ht=ht, nt=nt: e.matmul(pj[1][0:nt, :], lhsT=ht[:, kc, 0:nt], rhs=wbs[1][:, kc, :], start=(kc == 0), stop=(kc == 15)),
                             reads=[hk, "wv_0", "wv_1"], writes=["pj1"])
                    for kc in range(16):
                        P.op("tensor", lambda e, kc=kc, ht=ht, nt=nt: e.matmul(pg[0:nt, :], lhsT=ht[:, kc, 0:nt], rhs=wbs[2][:, kc, :], start=(kc == 0), stop=(kc == 15)),
                             reads=[hk, "wg_0", "wg_1"], writes=["pg"])

            def f_evac(t):
                    nt = 128 if t < 16 else 4
                    si = 0 if t < 16 else 1
                    hi = t % 2
                    par = t % 2
                    ht = hTt[hi]
                    hk = "hTt%d" % hi
                    vb = vbs[par]; gf = gfs[par]; yb = ybs[par]; qb = qbs[par]
                    kvb = "vb%d" % par; kgf = "gf%d" % par; kyb = "yb%d" % par; kqb = "qbB%d" % par
                    gC = math.exp(lgam[h] * (128.0 if t < 16 else 4.0)) if kind == 1 else 1.0

                    P.op("vector", lambda e, nt=nt: e.tensor_copy(out=vb[0:nt, :], in_=pj[1][0:nt, :]), reads=["pj1"], writes=[kvb])
                    P.op("scalar", lambda e, nt=nt: e.activation(out=gf[0:nt, :], in_=pg[0:nt, :], func=AF.Silu), reads=["pg"], writes=[kgf])
                    if kind == 1:
                        qf = qkf[0]
                        P.op("scalar", lambda e, nt=nt, qf=qf: e.copy(out=qf[0:nt, :], in_=pj[0][0:nt, :]), reads=["pj0"], writes=["qkf0"])
                        q4 = qf[0:nt, :].rearrange("p (a b d) -> p a b d", a=2, b=2)
                        x1 = q4[:, :, 0, :]
                        x2 = q4[:, :, 1, :]
                        cs_ = cs[hi][0:nt, 0:1, :].broadcast_to([nt, 2, 128])
                        sn_ = cs[hi][0:nt, 1:2, :].broadcast_to([nt, 2, 128])
                        tv = [rt[0:nt, i, :].rearrange("p (a d) -> p a d", d=128) for i in range(6)]
                        ck_ = "cs%d" % hi
                        P.op("gpsimd", lambda e, tv=tv, x1=x1, cs_=cs_: e.tensor_tensor(out=tv[0], in0=x1, in1=cs_, op=ALU.mult), reads=["qkf0", ck_ + "a", ck_ + "b"], writes=["rt0"])
                        P.op("gpsimd", lambda e, tv=tv, x2=x2, sn_=sn_: e.tensor_tensor(out=tv[1], in0=x2, in1=sn_, op=ALU.mult), reads=["qkf0", ck_ + "a", ck_ + "b"], writes=["rt1"])
                        P.op("gpsimd", lambda e, tv=tv, x1=x1, sn_=sn_: e.tensor_tensor(out=tv[2], in0=x1, in1=sn_, op=ALU.mult), reads=["qkf0", ck_ + "a", ck_ + "b"], writes=["rt2"])
                        P.op("gpsimd", lambda e, tv=tv, x2=x2, cs_=cs_: e.tensor_tensor(out=tv[3], in0=x2, in1=cs_, op=ALU.mult), reads=["qkf0", ck_ + "a", ck_ + "b"], writes=["rt3"])
                        P.op("gpsimd", lambda e, tv=tv: e.tensor_tensor(out=tv[4], in0=tv[0], in1=tv[1], op=ALU.subtract), reads=["rt0", "rt1"], writes=["rt4"])
                        P.op("gpsimd", lambda e, tv=tv: e.tensor_tensor(out=tv[5], in0=tv[2], in1=tv[3], op=ALU.add), reads=["rt2", "rt3"], writes=["rt5"])
                        for (a, dsti, di) in ((0, 0, 0), (1, 1, 1), (1, 2, 2)):
                            for hf in range(2):
                                P.op("vector", lambda e, a=a, dsti=dsti, di=di, hf=hf, tv=tv, nt=nt, si=si, h=h: e.tensor_scalar(
                                    out=qb[0:nt, dsti * 256 + hf * 128: dsti * 256 + (hf + 1) * 128], in0=tv[4 + hf][:, a, :],
                                    scalar1=decB[0:nt, si, h, di:di + 1], scalar2=None, op0=ALU.mult),
                                    reads=["rt4", "rt5", "decB"], writes=[kqb])
                    else:
                        for kc in range(16):
                            P.op("tensor", lambda e, kc=kc, ht=ht, nt=nt: e.matmul(psc[1][0:16, 0:nt], lhsT=wlr[:, kc, :], rhs=ht[:, kc, 0:nt], start=(kc == 0), stop=(kc == 15)),
                                 reads=[hk, "wlr"], writes=["psc1"])
                        P.op("vector", lambda e: e.memset(lrT[:, :], 1.0), writes=["lrT"])
                        P.op("vector", lambda e, nt=nt: e.tensor_copy(out=lrT[0:16, 0:nt], in_=psc[1][0:16, 0:nt]), reads=["psc1", "lrT"], writes=["lrT"])
                        P.op("tensor", lambda e, nt=nt, h=h: e.matmul(pod[1][0:nt, 0:256], lhsT=lrT[0:17, 0:nt], rhs=wg2[0:17, h * 256:(h + 1) * 256], start=True, stop=True),
                             reads=["lrT", "wg2a", "wg2b"], writes=["pod1"])
                        P.op("scalar", lambda e, nt=nt: e.activation(out=sp[0:nt, :], in_=pod[1][0:nt, 0:256], func=AF.Exp, scale=-1.0), reads=["pod1"], writes=["sp"])
                        P.op("scalar", lambda e, nt=nt: e.activation(out=sp[0:nt, :], in_=sp[0:nt, :], func=AF.Ln, bias=1.0), reads=["sp"], writes=["sp"])
                        P.op("tensor", lambda e, nt=nt: e.matmul(pod[1][0:nt, 256:512], lhsT=triU[0:nt, 0:nt], rhs=sp[0:nt, :], start=True, stop=True),
                             reads=["sp", "triU", "pod1"], writes=["pod1"])
                        P.op("scalar", lambda e, nt=nt: e.activation(out=eb[0:nt, 0, :], in_=pod[1][0:nt, 256:512], func=AF.Exp, scale=-1.0 / 16.0), reads=["pod1"], writes=["eb"])
                        P.op("scalar", lambda e, nt=nt: e.activation(out=eb[0:nt, 1, :], in_=pod[1][0:nt, 256:512], func=AF.Exp, scale=1.0 / 16.0), reads=["pod1"], writes=["eb"])
                        for ch in range(2):
                            P.op("tensor", lambda e, nt=nt, ch=ch: e.matmul(psc[1][:, 128 + ch:129 + ch], lhsT=sp[0:nt, ch * 128:(ch + 1) * 128], rhs=ones_f[0:nt, 0:1], start=True, stop=True),
                                 reads=["sp", "ones_f", "psc1"], writes=["psc1"])
                        P.op("scalar", lambda e: e.activation(out=ebl[:, 0:2], in_=psc[1][:, 128:130], func=AF.Exp, scale=-1.0 / 16.0), reads=["psc1"], writes=["ebl"])
                        P.op("vector", lambda e, nt=nt: e.scalar_tensor_tensor(out=qb[0:nt, 0:256], in0=pj[0][0:nt, 0:256], scalar=1.0 / 16.0, in1=eb[0:nt, 0, :], op0=ALU.mult, op1=ALU.mult),
                             reads=["pj0", "eb"], writes=[kqb])
                        P.op("vector", lambda e, nt=nt: e.tensor_tensor(out=qb[0:nt, 256:512], in0=pj[0][0:nt, 256:512], in1=eb[0:nt, 1, :], op=ALU.mult),
                             reads=["pj0", "eb"], writes=[kqb])

            def f_rest(t):
                    nt = 128 if t < 16 else 4
                    si = 0 if t < 16 else 1
                    hi = t % 2
                    par = t % 2
                    ht = hTt[hi]
                    hk = "hTt%d" % hi
                    vb = vbs[par]; gf = gfs[par]; yb = ybs[par]; qb = qbs[par]
                    kvb = "vb%d" % par; kgf = "gf%d" % par; kyb = "yb%d" % par; kqb = "qbB%d" % par
                    gC = math.exp(lgam[h] * (128.0 if t < 16 else 4.0)) if kind == 1 else 1.0

                    kd0 = 512 if kind == 1 else 256
                    sl = 0
                    cnt["ptb"] += 1
                    for j in range(4):
                        P.op("tensor", lambda e, j=j, sl=sl, nt=nt: e.transpose(out=ptb[:, sl * 512 + j * 128: sl * 512 + j * 128 + max(nt, 32)], in_=qb[0:max(nt, 32), j * 128:(j + 1) * 128], identity=ident[0:max(nt, 32), 0:max(nt, 32)]),
                             reads=[kqb, "ident"], writes=["ptb%d" % sl])
                    srcT = ptb[:, sl * 512:(sl + 1) * 512].rearrange("p (a t) -> p a t", t=128)[:, :, 0:nt]
                    P.op("scalar", lambda e, srcT=srcT, nt=nt: e.copy(out=qkT[:, :, 0:nt], in_=srcT), reads=["ptb%d" % sl], writes=["qkT"])
                    for ch in range(2):
                        P.op("tensor", lambda e, ch=ch, nt=nt: e.matmul(psc[0][0:nt, 0:nt], lhsT=qkT[:, 2 + ch, 0:nt], rhs=qkT[:, ch, 0:nt], start=(ch == 0), stop=(ch == 1)),
                             reads=["qkT"], writes=["psc0"])
                    P.op("vector", lambda e, nt=nt: e.tensor_tensor(out=scT[0:nt, 0:nt], in0=psc[0][0:nt, 0:nt], in1=maskBf[0:nt, 0:nt], op=ALU.mult),
                         reads=["psc0", "maskBf"], writes=["scT"])
                    P.op("tensor", lambda e, nt=nt: e.matmul(pod[0][0:nt, :], lhsT=scT[0:nt, 0:nt], rhs=vb[0:nt, :], start=True, stop=False),
                         reads=["scT", kvb], writes=["pod0"])
                    for ch in range(2):
                        P.op("tensor", lambda e, ch=ch, nt=nt, si=si: e.matmul(pod[0][0:nt, :], lhsT=qkT[:, ch, 0:nt], rhs=stbf[si][:, ch, :], start=False, stop=(ch == 1)),
                             reads=["qkT", "stb%d" % si], writes=["pod0"])
                    for ch in range(2):
                        pstt = psc[1] if ch == 0 else pod[1]
                        pk = "psc1" if ch == 0 else "pod1"
                        P.op("tensor", lambda e, ch=ch, nt=nt, pstt=pstt: e.matmul(pstt[:, :], lhsT=qb[0:nt, kd0 + ch * 128: kd0 + (ch + 1) * 128], rhs=vb[0:nt, :], start=True, stop=True),
                             reads=[kqb, kvb, pk], writes=[pk])
                        if kind == 1:
                            P.op("vector", lambda e, ch=ch, si=si, pstt=pstt, gC=gC: e.scalar_tensor_tensor(out=stf[si][:, ch, :], in0=stf[si][:, ch, :], scalar=gC, in1=pstt[:, :], op0=ALU.mult, op1=ALU.add),
                                 reads=["stf%d" % si, pk], writes=["stf%d" % si])
                        else:
                            P.op("vector", lambda e, ch=ch, si=si, pstt=pstt: e.tensor_tensor(out=stf[si][:, ch, :], in0=stf[si][:, ch, :], in1=pstt[:, :], op=ALU.add),
                                 reads=["stf%d" % si, pk], writes=["stf%d" % si])
                            P.op("vector", lambda e, ch=ch, si=si: e.tensor_scalar(out=stf[si][:, ch, :], in0=stf[si][:, ch, :], scalar1=ebl[:, ch:ch + 1], scalar2=None, op0=ALU.mult),
                                 reads=["stf%d" % si, "ebl"], writes=["stf%d" % si])
                        P.op("scalar", lambda e, ch=ch, si=si: e.copy(out=stbf[si][:, ch, :], in_=stf[si][:, ch, :]), reads=["stf%d" % si], writes=["stb%d" % si])
                    if kind == 1:
                        for c4 in range(4):
                            P.op("vector", lambda e, c4=c4, nt=nt: e.bn_stats(out=bst[0:nt, c4, :], in_=pod[0][0:nt, c4 * 128:(c4 + 1) * 128]), reads=["pod0"], writes=["bst"])
                        P.op("vector", lambda e, nt=nt: e.bn_aggr(out=mv[0:nt, 0:2], in_=bst[0:nt, :, :]), reads=["bst"], writes=["mv"])
                        P.op("scalar", lambda e, nt=nt: e.activation(out=mv[0:nt, 2:3], in_=mv[0:nt, 1:2], func=AF.Sqrt, bias=EPS), reads=["mv"], writes=["mv"])
                        P.op("vector", lambda e, nt=nt: e.reciprocal(out=mv[0:nt, 3:4], in_=mv[0:nt, 2:3]), reads=["mv"], writes=["mv"])
                        P.op("vector", lambda e, nt=nt: e.tensor_scalar(out=mv[0:nt, 4:5], in0=mv[0:nt, 0:1], scalar1=mv[0:nt, 3:4], scalar2=-1.0, op0=ALU.mult, op1=ALU.mult),
                             reads=["mv"], writes=["mv"])
                        P.op("scalar", lambda e, nt=nt: e.activation(out=of[0:nt, :], in_=pod[0][0:nt, :], func=AF.Identity, scale=mv[0:nt, 3:4], bias=mv[0:nt, 4:5]),
                             reads=["pod0", "mv"], writes=["of"])
                    else:
                        P.op("scalar", lambda e, nt=nt: e.activation(out=of[0:nt, :], in_=pod[0][0:nt, :], func=AF.Square, accum_out=mv[0:nt, 0:1]), reads=["pod0"], writes=["of", "mv"])
                        P.op("scalar", lambda e, nt=nt: e.activation(out=mv[0:nt, 2:3], in_=mv[0:nt, 0:1], func=AF.Sqrt, scale=1.0 / 512.0, bias=EPS), reads=["mv"], writes=["mv"])
                        P.op("vector", lambda e, nt=nt: e.reciprocal(out=mv[0:nt, 3:4], in_=mv[0:nt, 2:3]), reads=["mv"], writes=["mv"])
                        P.op("scalar", lambda e, nt=nt: e.activation(out=of[0:nt, :], in_=pod[0][0:nt, :], func=AF.Identity, scale=mv[0:nt, 3:4]), reads=["pod0", "mv", "of"], writes=["of"])
                    P.op("gpsimd", lambda e, nt=nt: e.tensor_tensor(out=of[0:nt, :], in0=of[0:nt, :], in1=gnv[0:nt, :], op=ALU.mult), reads=["of", "gnv"], writes=["of"])
                    P.op("vector", lambda e, nt=nt: e.tensor_tensor(out=yb[0:nt, :], in0=of[0:nt, :], in1=gf[0:nt, :], op=ALU.mult), reads=["of", kgf], writes=[kyb])

            def f_yt(t):
                    nt = 128 if t < 16 else 4
                    si = 0 if t < 16 else 1
                    hi = t % 2
                    par = t % 2
                    ht = hTt[hi]
                    hk = "hTt%d" % hi
                    vb = vbs[par]; gf = gfs[par]; yb = ybs[par]; qb = qbs[par]
                    kvb = "vb%d" % par; kgf = "gf%d" % par; kyb = "yb%d" % par; kqb = "qbB%d" % par
                    gC = math.exp(lgam[h] * (128.0 if t < 16 else 4.0)) if kind == 1 else 1.0

                    sl = 0
                    cnt["ptb"] += 1
                    for j in range(4):
                        P.op("tensor", lambda e, j=j, sl=sl, nt=nt: e.transpose(out=ptb[:, sl * 512 + j * 128: sl * 512 + j * 128 + max(nt, 32)], in_=yb[0:max(nt, 32), j * 128:(j + 1) * 128], identity=ident[0:max(nt, 32), 0:max(nt, 32)]),
                             reads=[kyb, "ident"], writes=["ptb%d" % sl])
                    srcT = ptb[:, sl * 512:(sl + 1) * 512].rearrange("p (a t) -> p a t", t=128)[:, :, 0:nt]
                    P.op("scalar", lambda e, srcT=srcT, nt=nt, t=t: e.copy(out=yT[:, :, t * 128:t * 128 + nt], in_=srcT), reads=["ptb%d" % sl], writes=["yT"])
            f_proj(0)
            f_evac(0)
            for t in range(17):
                if t + 1 < 17:
                    f_proj(t + 1)
                if t >= 1:
                    f_yt(t - 1)
                f_rest(t)
                if t + 1 < 17:
                    f_evac(t + 1)
            f_yt(16)

            po = p_stb if kind == 1 else p_stc
            so = s_stb if kind == 1 else s_stc
            P.dma("sync", po[h].rearrange("(c p) v -> p c v", p=128), stf[0][:, :, :], reads=["stf0"], slot="sto")
            P.dma("sync", so[h].rearrange("(c p) v -> p c v", p=128), stf[1][:, :, :], reads=["stf1"], slot="sto")
            P.dma("gpsimd", wo[:, :, :], w_out[h * 512:(h + 1) * 512, :].rearrange("(k p) n -> p k n", p=128), writes=["wo"], slot="wo")
            if h == NH - 1:
                load_gvec(norm_g[li + 1, :] if li < NL - 1 else final_g)
            wout_pass(li, yT, wo, 4, first=(h == 0), last=(h == NH - 1), next_kind=next_kind, hT_next=hT_next, xbufs=[xts[0], xts[1], V["xt2"]])

    kinds = [i % 3 for i in range(NL)]
    VA = layer_A(0, 0)
    load_gvec(norm_g[0, :])
    for t in range(17):
        nt = 128 if t < 16 else 4
        src = xp[t * 128:t * 128 + nt, :] if t < 16 else xs[:, :]
        P.dma("sync", xts[t % 2][0:nt, :], src, writes=["xt%d" % (t % 2)], slot="xt%d" % (t % 2))
        rms_to(nt, "gvec", hn[0:nt, :], "hn", xts[t % 2], "xt%d" % (t % 2), extra_reads=["hn"])
        produce_h(t, nt, VA["hT"], to_dram=False)
    for li in range(NL):
        kind = kinds[li]
        print('MARK layer', li, 'starts at op', len(P.ops), flush=True)
        nk_ = kinds[li + 1] if li + 1 < NL else -1
        if kind == 0:
            run_layer_A(li, li // 3, VA, nk_, VA["hst"] if nk_ in (1, 2) else VA["hT"])
            P.barrier()
        else:
            VB = views_BC()
            if nk_ == 0:
                hT_next = VB["hTst"]
                run_layer_BC(li, kind, VB, 1, hT_next)
                P.barrier()
                VA = layer_A(li + 1, (li + 1) // 3)
                for t in range(17):
                    nt = 128 if t < 16 else 4
                    P.dma("sync", VA["hT"][:, :, t * 128:t * 128 + nt], hTd[t, :, :, 0:nt], reads=["hTd%d" % t], writes=["hT%d" % t], slot="hTl")
            else:
                run_layer_BC(li, kind, VB, nk_, VB["hTst"])
                P.barrier()
    P.emit()
    for g in reversed(ctxs):
        g.__exit__(None, None, None)
    return nc, hc, P.stats


_CACHE = {}


def kernel(**inp):
    if "nc" not in _CACHE:
        _CACHE["nc"] = build()
    nc, hc, stats = _CACHE["nc"]
    f = lambda a: np.ascontiguousarray(np.asarray(a, dtype=np.float32))
    in_maps = []
    shared = dict(
        norm_g=f(inp["norm_g"]), final_g=f(inp["final_g"]), w_in_a=f(inp["w_in_a"]), w_out_a=f(inp["w_out_a"]),
        w_in_b=f(inp["w_in_b"][0]), gn_b=f(inp["gn_b"][0]), w_out_b=f(inp["w_out_b"][0]), w_in_c=f(inp["w_in_c"][0]),
        w_gate2=f(inp["w_gate2_c"][0]), b_gate=f(inp["b_gate_c"][0]).reshape(1, 1024), gn_c=f(inp["gn_c"][0]), w_out_c=f(inp["w_out_c"][0]))
    for kk, v in hc.items():
        shared["c_" + kk] = v
    for c in range(8):
        m = dict(shared)
        m["xp"] = f(inp["x_prompt"][c % 4])
        m["xs"] = f(inp["x_sample"][c])
        for g, nm in enumerate(("cache_a_kv1", "cache_a_kv2", "cache_a_kv3")):
            m["ckv%d" % g] = f(inp[nm][:, c]).reshape(2, LBUF[g], 4096)
        m["stb"] = f(inp["state_b"][0, c])
        m["stc"] = f(inp["state_c"][0, c])
        in_maps.append(m)
    res = run_bass_kernel_spmd(nc, in_maps, core_ids=list(range(8)))
    R = res.results
    y_prompt = np.stack([R[b]["y_p"] for b in range(4)])
    y_sample = np.stack([R[c]["y_s"] for c in range(8)])
    pkv = []
    for g in range(3):
        keep = min(LBUF[g], S)
        a = np.stack([R[b]["p_kv%d" % g] for b in range(4)], axis=1)
        pkv.append(a.reshape(2, 4, keep, 2, 16, 128))
    p_stb = np.stack([R[b]["p_stb"] for b in range(4)])[None]
    p_stc = np.stack([R[b]["p_stc"] for b in range(4)])[None]
    skv = np.stack([R[c]["s_kv"] for c in range(8)], axis=0)
    s_list = [np.ascontiguousarray(skv[:, :, g].transpose(1, 0, 2, 3)).reshape(2, 8, 4, 2, 16, 128) for g in range(3)]
    s_stb = np.stack([R[c]["s_stb"] for c in range(8)])[None]
    s_stc = np.stack([R[c]["s_stc"] for c in range(8)])[None]
    outs = (y_prompt, y_sample, pkv[0], pkv[1], pkv[2], p_stb, p_stc, s_list[0], s_list[1], s_list[2], s_stb, s_stc)
    return tuple(np.ascontiguousarray(o, dtype=np.float32) for o in outs)
```
